# Optimizing a Trainium2 kernel written in Bass

```python
import math
import jax, jax.numpy as jnp
from jax import lax
import numpy as np


D_MODEL = 1024
BATCH = 16
SEQ = 2048
DEPTH = 1

CHUNK = 64
Q_BLOCK = 128

A_HEADS = 4
A_HEAD_DIM = 128
A_WIDTH = A_HEADS * A_HEAD_DIM
A_KV_RANK = 128
IDX_HEADS = 8
IDX_DIM = 64
TOPK_MAX = 256

B_HEADS = 4
B_KEY_DIM = 128
B_VAL_DIM = 128
B_WIDTH = B_HEADS * B_VAL_DIM
B_FORGET = B_HEADS * B_KEY_DIM

D_MIX = A_WIDTH + B_WIDTH

REL_BUCKETS = 32
REL_MAX_DIST = 256

DEEPNORM_ALPHA = (2.0 * DEPTH) ** 0.25
DEEPNORM_BETA = (8.0 * DEPTH) ** -0.25
LN_EPS = 1e-5
RMS_EPS = 1e-6

_SPLITS = (
    ('a_q', A_WIDTH),
    ('a_ckv', A_KV_RANK),
    ('a_iq', IDX_HEADS * IDX_DIM),
    ('a_ik', IDX_DIM),
    ('a_iw', IDX_HEADS),
    ('a_gate', A_WIDTH),
    ('b_q', B_FORGET),
    ('b_f', B_FORGET),
    ('b_i', B_WIDTH),
    ('b_gate', B_WIDTH),
)
D_IN_PROJ = int(sum(n for _, n in _SPLITS))
SPLIT_POINTS = tuple(int(v) for v in np.cumsum([n for _, n in _SPLITS])[:-1])

kernel_name = 'hybrid_dsa_hgrn2_deepnorm_layer'


def _rms_norm(x, g):
    xf = x.astype(jnp.float32)
    y = xf * lax.rsqrt(jnp.mean(xf * xf, axis=-1, keepdims=True) + RMS_EPS)
    return (y * g.astype(jnp.float32)).astype(x.dtype)


def _layer_norm(x, g, b):
    xf = x.astype(jnp.float32)
    mu = jnp.mean(xf, axis=-1, keepdims=True)
    var = jnp.mean(jnp.square(xf - mu), axis=-1, keepdims=True)
    y = (xf - mu) * lax.rsqrt(var + LN_EPS) * g.astype(jnp.float32) + b.astype(jnp.float32)
    return y.astype(x.dtype)


def _t5_bucket(rel):
    nb = REL_BUCKETS // 2
    max_exact = nb // 2
    ret = jnp.where(rel > 0, nb, 0).astype(jnp.int32)
    n = jnp.abs(rel)
    nf = jnp.maximum(n, 1).astype(jnp.float32)
    large = max_exact + (jnp.log(nf / max_exact) / math.log(REL_MAX_DIST / max_exact)
                         * (nb - max_exact)).astype(jnp.int32)
    large = jnp.minimum(large, nb - 1)
    return ret + jnp.where(n < max_exact, n, large)


def _dsa_mixer(q, c, iq, ik, iw, w_uk, rel_bias):
    bsz, seq = q.shape[0], q.shape[1]
    k_sel = min(TOPK_MAX, seq // 4)
    n_blk = seq // Q_BLOCK
    q_lat = jnp.einsum('bshd,hdr->bshr', q, w_uk)
    scale = A_HEAD_DIM ** -0.5
    idx_scale = IDX_DIM ** -0.5
    w_scale = IDX_HEADS ** -0.5
    pos = jnp.arange(seq, dtype=jnp.int32)
    key_chunk = pos // CHUNK

    def blockify(t):
        return jnp.moveaxis(t.reshape(bsz, n_blk, Q_BLOCK, *t.shape[2:]), 1, 0)

    xs = (blockify(q_lat), blockify(iq), blockify(iw), pos.reshape(n_blk, Q_BLOCK))

    def one_block(args):
        ql, iqb, iwb, qpos = args
        q_chunk = qpos // CHUNK
        admissible = key_chunk[None, :] <= q_chunk[:, None]
        dots = jnp.einsum('bqhd,bsd->bqhs', iqb, ik) * idx_scale
        score = jnp.einsum('bqh,bqhs->bqs', iwb * w_scale, jax.nn.relu(dots))
        score = jnp.where(admissible[None], score.astype(jnp.float32), -jnp.inf)
        _, sel = lax.top_k(score, k_sel)
        valid = (sel // CHUNK) <= q_chunk[None, :, None]
        c_sel = jax.vmap(lambda cb, ib: cb[ib])(c, sel)
        bias = rel_bias[_t5_bucket(sel - qpos[None, :, None])]
        logits = (jnp.einsum('bqhr,bqkr->bqhk', ql, c_sel).astype(jnp.float32) * scale
                  + jnp.swapaxes(bias, -1, -2).astype(jnp.float32))
        logits = jnp.where(valid[:, :, None, :], logits, -jnp.inf)
        p = jax.nn.softmax(logits, axis=-1).astype(c.dtype)
        return jnp.einsum('bqhk,bqkr->bqhr', p, c_sel)

    o = lax.map(one_block, xs)
    return jnp.moveaxis(o, 0, 1).reshape(bsz, seq, A_HEADS, A_KV_RANK)


def _hgrn2_mixer(q, f_raw, v, lb):
    bsz, seq = q.shape[0], q.shape[1]
    n_chunk = seq // CHUNK
    q = jax.nn.silu(q.astype(jnp.float32))
    lbh = lb.reshape(B_HEADS, B_KEY_DIM).astype(jnp.float32)
    f = lbh + (1.0 - lbh) * jax.nn.sigmoid(f_raw.astype(jnp.float32))
    k = 1.0 - f
    log_f = jnp.log(f)

    def chunkify(t):
        return jnp.transpose(t.reshape(bsz, n_chunk, CHUNK, B_HEADS, t.shape[-1]), (1, 0, 3, 2, 4))

    qc, kc, vc = chunkify(q), chunkify(k), chunkify(v.astype(jnp.float32))
    bc = jnp.cumsum(chunkify(log_f), axis=3)
    causal = jnp.tril(jnp.ones((CHUNK, CHUNK), dtype=bool))

    def step(state, inp):
        qn, kn, vn, bn = inp
        diff = bn[:, :, :, None, :] - bn[:, :, None, :, :]
        decay = jnp.exp(jnp.where(causal[:, :, None], diff, -jnp.inf))
        scores = jnp.einsum('bhtk,bhsk,bhtsk->bhts', qn, kn, decay)
        o = (jnp.einsum('bhts,bhsv->bhtv', scores, vn)
             + jnp.einsum('bhtk,bhkv->bhtv', qn * jnp.exp(bn), state))
        b_last = bn[:, :, -1:, :]
        state = (jnp.exp(b_last[:, :, 0, :])[..., None] * state
                 + jnp.einsum('bhsk,bhsv->bhkv', kn * jnp.exp(b_last - bn), vn))
        return state, o

    s0 = jnp.zeros((bsz, B_HEADS, B_KEY_DIM, B_VAL_DIM), jnp.float32)
    _, o = lax.scan(step, s0, (qc, kc, vc, bc))
    return jnp.transpose(o, (1, 0, 3, 2, 4)).reshape(bsz, seq, B_HEADS, B_VAL_DIM)


def setup_inputs(seed: int = 0) -> dict:
    key = jax.random.key(seed)
    ks = jax.random.split(key, 12)
    x = jax.random.normal(ks[0], (BATCH, SEQ, D_MODEL), jnp.float32)
    col_scale = jnp.concatenate([
        jnp.full((n,), DEEPNORM_BETA if name == 'b_i' else 1.0, jnp.float32) for name, n in _SPLITS])
    w_in = jax.random.normal(ks[1], (DEPTH, D_MODEL, D_IN_PROJ), jnp.float32) * (D_MODEL ** -0.5) * col_scale
    w_uk = jax.random.normal(ks[2], (DEPTH, A_HEADS, A_HEAD_DIM, A_KV_RANK), jnp.float32) * (A_HEAD_DIM ** -0.5)
    w_uv = (jax.random.normal(ks[3], (DEPTH, A_HEADS, A_KV_RANK, A_HEAD_DIM), jnp.float32)
            * (A_KV_RANK ** -0.5) * DEEPNORM_BETA)
    kv_norm_g = 1.0 + 0.05 * jax.random.normal(ks[4], (DEPTH, A_KV_RANK), jnp.float32)
    rel_bias = 0.2 * jax.random.normal(ks[5], (REL_BUCKETS, A_HEADS), jnp.float32)
    lb_logits = 0.5 * jax.random.normal(ks[6], (DEPTH + 1, B_FORGET), jnp.float32)
    hgrn_norm_g = 1.0 + 0.05 * jax.random.normal(ks[7], (DEPTH, B_WIDTH), jnp.float32)
    w_o = jax.random.normal(ks[8], (DEPTH, D_MIX, D_MODEL), jnp.float32) * (D_MIX ** -0.5) * DEEPNORM_BETA
    ln_g = 1.0 + 0.05 * jax.random.normal(ks[9], (DEPTH, D_MODEL), jnp.float32)
    ln_b = 0.02 * jax.random.normal(ks[10], (DEPTH, D_MODEL), jnp.float32)
    return {'x': x, 'w_in': w_in, 'w_uk': w_uk, 'w_uv': w_uv, 'kv_norm_g': kv_norm_g,
            'rel_bias': rel_bias, 'lb_logits': lb_logits, 'hgrn_norm_g': hgrn_norm_g,
            'w_o': w_o, 'ln_g': ln_g, 'ln_b': ln_b}


def reference(x, w_in, w_uk, w_uv, kv_norm_g, rel_bias, lb_logits, hgrn_norm_g, w_o, ln_g, ln_b):
    bsz, seq, _ = x.shape
    lb_all = jnp.cumsum(jax.nn.softmax(lb_logits.astype(jnp.float32), axis=0), axis=0)
    for layer in range(DEPTH):
        h = jnp.einsum('bsd,dc->bsc', x, w_in[layer])
        a_q, a_ckv, a_iq, a_ik, a_iw, a_gate, b_q, b_f, b_i, b_gate = jnp.split(h, SPLIT_POINTS, axis=-1)

        c = _rms_norm(a_ckv, kv_norm_g[layer])
        o_lat = _dsa_mixer(a_q.reshape(bsz, seq, A_HEADS, A_HEAD_DIM), c,
                           a_iq.reshape(bsz, seq, IDX_HEADS, IDX_DIM), a_ik, a_iw,
                           w_uk[layer], rel_bias)
        o_a = jnp.einsum('bshr,hrv->bshv', o_lat, w_uv[layer]).reshape(bsz, seq, A_WIDTH)
        o_a = o_a * jax.nn.silu(a_gate)

        o_b = _hgrn2_mixer(b_q.reshape(bsz, seq, B_HEADS, B_KEY_DIM),
                           b_f.reshape(bsz, seq, B_HEADS, B_KEY_DIM),
                           b_i.reshape(bsz, seq, B_HEADS, B_VAL_DIM), lb_all[layer])
        o_b = _rms_norm(o_b, hgrn_norm_g[layer].reshape(B_HEADS, B_VAL_DIM)).reshape(bsz, seq, B_WIDTH)
        o_b = o_b.astype(x.dtype) * jax.nn.silu(b_gate)

        y = jnp.einsum('bsc,cd->bsd', jnp.concatenate([o_a.astype(x.dtype), o_b], axis=-1), w_o[layer])
        x = _layer_norm(DEEPNORM_ALPHA * x + y, ln_g[layer], ln_b[layer])
    return x
```

```python
import math
from contextlib import ExitStack

import numpy as np
import concourse.bass as bass
import concourse.mybir as mybir
from concourse.bass_utils import run_bass_kernel_spmd

F32 = mybir.dt.float32
BF16 = mybir.dt.bfloat16
I32 = mybir.dt.int32
ALU = mybir.AluOpType
ACTF = mybir.ActivationFunctionType
AX = mybir.AxisListType

EPOCH = 8192

S = 2048
D = 1024
NB = S // 128
KC = D // 128
NCORES = 8
SEQ_PER_CORE = 2
NI = 16
TOPK = 256
NEG = -1.0e30
ALPHA = 2.0 ** 0.25
LN_EPS = 1e-5
RMS_EPS = 1e-6

C_Q, C_CKV, C_IQ, C_IK, C_IW, C_AG, C_BQ, C_BF, C_BI, C_BG = 0, 512, 640, 1152, 1216, 1224, 1736, 2248, 2760, 3272


class Tk:
    __slots__ = ("w", "r", "psum")

    def __init__(self, psum=False):
        self.w = None
        self.r = []
        self.psum = psum


class _Rec:
    def __init__(self):
        self.call = None

    def __getattr__(self, name):
        def f(*a, **k):
            self.call = (name, a, k)
            return None
        return f


class Op:
    __slots__ = ("eng", "call", "preds", "idx", "dur", "lat", "dma", "succ", "npred", "ready", "start", "finish", "pos",
                 "waits", "sig", "dtok", "bl")

    def __init__(self):
        self.preds = []
        self.succ = []
        self.waits = []
        self.sig = 0
        self.dtok = None


_DVE_F = {"tensor_tensor_scan": 2.1, "reciprocal": 6.5, "bn_stats": 1.3, "tensor_reduce": 1.1}


class FW:
    HOP = 0.55

    def __init__(self, nc, stack, n_epoch=8, n_dma_sem=10):
        self.nc = nc
        self.engs = {"pe": nc.tensor, "act": nc.scalar, "dve": nc.vector, "pool": nc.gpsimd, "sp": nc.sync}
        self.sems = {}
        for e in ("pe", "act", "dve", "pool"):
            self.sems[e] = [stack.enter_context(nc.semaphore(f"s_{e}_{i}")) for i in range(n_epoch)]
        self.sigcnt = {e: 0 for e in self.engs}
        self.seen = {e: {} for e in self.engs}
        self.dsems = {}
        for q in ("sp", "pool", "act"):
            self.dsems[q] = [[stack.enter_context(nc.semaphore(f"d_{q}_{i}")), 0] for i in range(n_dma_sem)]
        self.dptr = {q: 0 for q in self.dsems}
        self.ops = []
        self.nops = 0
        self.sched = True

    def _edges(self, op, e, reads, writes):
        ps = op.preds
        for t in reads:
            if t.w is not None:
                ps.append(t.w)
            if t.psum:
                for d in t.r:
                    if d.eng != e:
                        ps.append(d)
        for t in writes:
            if t.w is not None:
                ps.append(t.w)
            ps.extend(t.r)
        for t in reads:
            t.r.append(op)
        for t in writes:
            t.w = op
            t.r = []

    def op(self, e, fn, reads=(), writes=(), dur=None):
        r = _Rec()
        fn(r)
        o = Op()
        o.eng = e
        o.call = r.call
        o.dma = False
        o.idx = self.nops
        self.nops += 1
        if dur is None:
            name, a, k = r.call
            out = k.get("out", a[0] if a else None)
            try:
                n = out.free_size()
            except Exception:
                n = 128
            if e == "pe":
                dur = max(n, 64) / 1800.0 + 0.03
            elif e == "act":
                dur = (n + 260) / 1200.0
            elif e == "dve":
                f = _DVE_F.get(name, 1.0)
                if k.get("accum_out") is not None:
                    f = 1.25
                dur = (n * f + 110) / 960.0
            else:
                dur = (n * 5.0 + 200) / 1200.0
        o.dur = dur
        o.lat = dur
        self._edges(o, e, reads, writes)
        self.ops.append(o)
        return o

    def dma(self, q, out, in_, reads=(), writes=(), **kw):
        o = Op()
        o.eng = q
        o.call = ("dma_start", (), dict(out=out, in_=in_, **kw))
        o.dma = True
        o.idx = self.nops
        self.nops += 1
        try:
            nbytes = out.free_size() * out.partition_size() * 4
        except Exception:
            nbytes = 1 << 18
        o.dur = 0.15 if q == "sp" else 1.0
        o.lat = 1.5 + nbytes / 90000.0
        self._edges(o, q, reads, writes)
        self.ops.append(o)
        return o

    def _schedule(self, ops):
        inreg = set(id(o) for o in ops)
        for o in ops:
            o.preds = [p for p in dict.fromkeys(o.preds) if id(p) in inreg and p is not o]
            o.succ = []
        for o in ops:
            o.npred = len(o.preds)
            o.ready = 0.0
            for p in o.preds:
                p.succ.append(o)
        if not self.sched:
            return list(ops)
        for o in reversed(ops):
            b = 0.0
            for s_ in o.succ:
                if s_.bl > b:
                    b = s_.bl
            o.bl = b + o.lat
        free = {e: 0.0 for e in self.engs}
        cand = {e: [] for e in self.engs}
        for o in ops:
            if o.npred == 0:
                cand[o.eng].append(o)
        order = []
        n = len(ops)
        SLACK = 0.25
        while len(order) < n:
            best = None
            bst = None
            for e, lst in cand.items():
                if not lst:
                    continue
                fe = free[e]
                stmin = None
                for o in lst:
                    st = o.ready if o.ready > fe else fe
                    if stmin is None or st < stmin:
                        stmin = st
                pick = None
                for o in lst:
                    st = o.ready if o.ready > fe else fe
                    if st <= stmin + SLACK and (pick is None or o.bl > pick.bl):
                        pick = o
                if bst is None or stmin < bst:
                    bst = stmin
                    best = pick
            o = best
            cand[o.eng].remove(o)
            fe = free[o.eng]
            o.start = o.ready if o.ready > fe else fe
            free[o.eng] = o.start + o.dur
            o.finish = o.start + o.lat
            order.append(o)
            for s_ in o.succ:
                t = o.finish + ((0.0 if o.eng == 'pe' else 0.3) if (s_.eng == o.eng and not o.dma) else self.HOP)
                if t > s_.ready:
                    s_.ready = t
                s_.npred -= 1
                if s_.npred == 0:
                    cand[s_.eng].append(s_)
        return order

    def flush(self):
        ops = self.ops
        self.ops = []
        if not ops:
            return
        order = self._schedule(ops)
        last = {}
        for pos, o in enumerate(order):
            o.pos = pos
            if not o.dma:
                last[o.eng] = o
        seenpos = {e: {} for e in self.engs}
        for o in order:
            e = o.eng
            for p in o.preds:
                if p.dma:
                    o.waits.append(p)
                    continue
                if p.eng == e and e == "pe":
                    continue
                if seenpos[e].get(p.eng, -1) >= p.pos:
                    continue
                seenpos[e][p.eng] = p.pos
                o.waits.append(p)
                p.sig = -1
        for e, o in last.items():
            if not o.dma:
                o.sig = -1
        for o in order:
            e = o.eng
            eng = self.engs[e]
            for p in o.waits:
                if p.dma:
                    q, i, v = p.dtok
                    if self.seen[e].get(("d", q, i), 0) >= v:
                        continue
                    self.seen[e][("d", q, i)] = v
                    eng.wait_ge(self.dsems[q][i][0], v)
                else:
                    sv = p.sig
                    if self.seen[e].get(("c", p.eng), 0) >= sv:
                        continue
                    self.seen[e][("c", p.eng)] = sv
                    ep, c = divmod(sv - 1, EPOCH)
                    eng.wait_ge(self.sems[p.eng][ep], c + 1)
            name, a, k = o.call
            if o.dma:
                q = e
                i = self.dptr[q]
                self.dptr[q] = (i + 1) % len(self.dsems[q])
                slot = self.dsems[q][i]
                if slot[1] > 0 and self.seen[q].get(("d", q, i), 0) < slot[1]:
                    self.seen[q][("d", q, i)] = slot[1]
                    eng.wait_ge(slot[0], slot[1])
                slot[1] += 16
                eng.dma_start(**k).then_inc(slot[0], 16)
                o.dtok = (q, i, slot[1])
            else:
                ins = getattr(eng, name)(*a, **k)
                if o.sig == -1:
                    self.sigcnt[e] += 1
                    o.sig = self.sigcnt[e]
                    ep, c = divmod(o.sig - 1, EPOCH)
                    ins.then_inc(self.sems[e][ep], 1)
        for e in ("pe", "act", "dve", "pool", "sp"):
            eng = self.engs[e]
            for x, o in last.items():
                if o.dma or x == e:
                    continue
                sv = o.sig
                if self.seen[e].get(("c", x), 0) >= sv:
                    continue
                self.seen[e][("c", x)] = sv
                ep, c = divmod(sv - 1, EPOCH)
                eng.wait_ge(self.sems[x][ep], c + 1)
            for q in self.dsems:
                for i, slot in enumerate(self.dsems[q]):
                    if slot[1] > 0 and self.seen[e].get(("d", q, i), 0) < slot[1]:
                        self.seen[e][("d", q, i)] = slot[1]
                        eng.wait_ge(slot[0], slot[1])

    def barrier(self):
        self.flush()


def build_program(n_seq=SEQ_PER_CORE, dbg=None, sched=True):
    _build.sched = sched
    return _build(n_seq, dbg, None)


def _build(n_seq, dbg, targets):
    dbg = dbg or {}
    nc = bass.Bass("TRN2", target_bir_lowering=False)
    dt = nc.dram_tensor
    x_d = dt("x", [n_seq, S, D], F32, kind="ExternalInput").ap()
    win_d = dt("w_in", [D, 3784], F32, kind="ExternalInput").ap()
    wuk_d = dt("w_uk", [4, 128, 128], F32, kind="ExternalInput").ap()
    wuv_d = dt("w_uv", [4, 128, 128], F32, kind="ExternalInput").ap()
    kvg_d = dt("kv_g", [1, 128], F32, kind="ExternalInput").ap()
    bias_d = dt("bias_t", [4, 4, 128, 128], F32, kind="ExternalInput").ap()
    lb_d = dt("lb_t", [128, 2, 4], F32, kind="ExternalInput").ap()
    hg_d = dt("hg_t", [128, 4], F32, kind="ExternalInput").ap()
    wo_d = dt("w_o", [D, D], F32, kind="ExternalInput").ap()
    lng_d = dt("ln_g", [1, D], F32, kind="ExternalInput").ap()
    lnb_d = dt("ln_b", [1, D], F32, kind="ExternalInput").ap()
    out_d = dt("out", [n_seq, S, D], F32, kind="ExternalOutput").ap()
    dbg_d = {}
    for name, shape in dbg.items():
        dbg_d[name] = dt("dbg_" + name, list(shape), F32, kind="ExternalOutput").ap()

    with ExitStack() as st:
        fw = FW(nc, st)
        fw.sched = getattr(_build, 'sched', True)
        uid = [0]

        def sb(name, shape, dtype, stack=st):
            uid[0] += 1
            return stack.enter_context(nc.sbuf_tensor(f"{name}_{uid[0]}", list(shape), dtype))
        out_toks = []

        def dbg_out(name, ap_sb, tk, dst=None):
            if name in dbg_d:
                out_toks.append(fw.dma("sp", dst if dst is not None else dbg_d[name], ap_sb, reads=[tk]))

        PB = [st.enter_context(nc.psum_tensor(f"pb{i}", [128, 512], F32)) for i in range(8)]
        PK = [Tk(psum=True) for _ in range(8)]

        ident_f = sb("ident_f", [128, 128], F32)
        ident_b = sb("ident_b", [128, 128], BF16)
        ones_f = sb("ones_f", [128, 128], F32)
        ones_b = sb("ones_b", [128, 128], BF16)
        adm = sb("adm", [128, 128], F32)
        negb = sb("negb", [128, 128], F32)
        caus = sb("caus", [128, 128], F32)
        pow2 = sb("pow2", [128, NI], F32)
        kvg = sb("kvg", [128, 128], F32)
        ident4 = sb("ident4", [128, 4, 128], BF16)
        lbt = sb("lbt", [128, 2, 4], F32)
        lbA = sb("lbA", [128, 4], F32)
        lbB = sb("lbB", [128, 4], F32)
        lbNB = sb("lbNB", [128, 4], F32)
        hgt = sb("hgt", [128, 4], F32)
        wuk = sb("wuk", [128, 4, 128], BF16)
        wuv = sb("wuv", [128, 4, 128], BF16)
        epsc = sb("epsc", [128, 2], F32)
        k_const = Tk()
        k_wo = Tk()

        P = fw.op
        P("pool", lambda e: e.memset(ones_f[:], 1.0), writes=[k_const])
        P("pool", lambda e: e.memset(epsc[:, 0:1], RMS_EPS), writes=[k_const])
        P("pool", lambda e: e.memset(epsc[:, 1:2], LN_EPS), writes=[k_const])
        P("pool", lambda e: e.memset(ones_b[:], 1.0), writes=[k_const])
        P("pool", lambda e: e.affine_select(ident_f[:], ones_f[:], [[-1, 128]], ALU.is_equal, 0.0, base=0, channel_multiplier=1),
          reads=[k_const], writes=[k_const])
        P("pool", lambda e: e.tensor_copy(ident_b[:], ident_f[:]), reads=[k_const], writes=[k_const])
        for i4 in range(4):
            P("pool", lambda e: e.tensor_copy(ident4[:, i4, :], ident_f[:]), reads=[k_const], writes=[k_const])
        P("pool", lambda e: e.memset(adm[:], 1.0), writes=[k_const])
        P("pool", lambda e: e.memset(adm[0:64, 64:128], 0.0), writes=[k_const])
        P("pool", lambda e: e.memset(negb[:], 0.0), writes=[k_const])
        P("pool", lambda e: e.memset(negb[0:64, 64:128], NEG), writes=[k_const])
        P("pool", lambda e: e.memset(caus[:], 1.0), writes=[k_const])
        P("pool", lambda e: e.affine_select(caus[:], caus[:], [[1, 128]], ALU.is_ge, 0.0, base=0, channel_multiplier=-1),
          reads=[k_const], writes=[k_const])
        P("pool", lambda e: e.memset(caus[0:64, 64:128], 0.0), writes=[k_const])
        for i in range(NI):
            P("pool", lambda e: e.memset(pow2[:, i:i + 1], 2.0 ** (-(i + 1))), writes=[k_const])

        fw.dma("sp", kvg[:], kvg_d.partition_broadcast(128), writes=[k_const])
        fw.dma("sp", lbt[:], lb_d, writes=[k_const])
        fw.dma("sp", hgt[:], hg_d, writes=[k_const])
        P("dve", lambda e: e.tensor_scalar(hgt[:], hgt[:], 0.5, None, ALU.mult), reads=[k_const], writes=[k_const])
        P("dve", lambda e: e.tensor_tensor(lbA[:], lbt[:, 0, :], lbt[:, 1, :], ALU.subtract), reads=[k_const], writes=[k_const])
        P("act", lambda e: e.activation(lbB[:], lbA[:], ACTF.Tanh, scale=0.5), reads=[k_const], writes=[k_const])
        P("dve", lambda e: e.tensor_scalar(lbA[:], lbB[:], 0.25, 0.75, ALU.mult, ALU.add), reads=[k_const], writes=[k_const])
        P("dve", lambda e: e.tensor_scalar(lbNB[:], lbB[:], 0.25, -0.25, ALU.mult, ALU.add), reads=[k_const], writes=[k_const])
        P("dve", lambda e: e.tensor_scalar(lbB[:], lbB[:], -0.25, 0.25, ALU.mult, ALU.add), reads=[k_const], writes=[k_const])


        xT = sb("xT", [128, KC, S], BF16)
        k_xT = [[Tk() for _ in range(2)] for _ in range(NB)]
        OT = sb("OT", [128, 8, S], BF16)
        k_OT = [[Tk() for _ in range(4)] for _ in range(8)]
        wstg = [sb(f"wstg{i}", [128, KC, 136], F32) for i in range(2)]
        wbf = [sb(f"wbf{i}", [128, KC, 136], BF16) for i in range(2)]
        k_wstg = [Tk(), Tk()]
        k_wbf = [Tk(), Tk()]
        wctr = [0]

        def load_w(col_slices):
            i = wctr[0] % 2
            wctr[0] += 1
            off = 0
            for (c0, n) in col_slices:
                fw.dma("sp" if wctr[0] % 2 == 0 else "pool", wstg[i][:, :, off:off + n],
                       win_d[:, c0:c0 + n].rearrange("(k p) c -> p k c", p=128), writes=[k_wstg[i]])
                off += n
            P("dve", lambda e: e.tensor_copy(wbf[i][:, :, 0:off], wstg[i][:, :, 0:off]), reads=[k_wstg[i]], writes=[k_wbf[i]])
            return wbf[i], k_wbf[i], off

        def xT_keys(tt):
            return [k_xT[tb][hh] for tb in range(tt * 4, tt * 4 + 4) for hh in range(2)]

        pctr = [0]

        def inproj_fm(wt, k_w, tt, banks):
            bi = banks[pctr[0] % len(banks)]
            pctr[0] += 1
            for k in range(KC):
                P("pe", lambda e: e.matmul(PB[bi][:], wt[:, k, 0:128], xT[:, k, tt * 512:(tt + 1) * 512], start=(k == 0), stop=(k == KC - 1)),
                  reads=[k_w] + xT_keys(tt), writes=[PK[bi]])
            return bi

        evc = [0]

        def ev_eng():
            evc[0] += 1
            return "act" if evc[0] % 2 == 0 else "dve"

        def copy_on(e, out, in_, reads, writes):
            if e == "act":
                return P("act", lambda g: g.activation(out, in_, ACTF.Copy), reads=reads, writes=writes)
            return P(e, lambda g: g.tensor_copy(out, in_), reads=reads, writes=writes)

        bhi = sb("bhi", [128, 16, 128], BF16)
        blo = sb("blo", [128, 16, 128], BF16)
        k_bias = Tk()
        SQ = math.sqrt(128.0)

        def phase_x(b, ph):
            xs = [sb(f"xs{i}", [128, D], F32, ph) for i in range(3)]
            k_xs = [Tk() for _ in range(3)]
            for tb in range(NB):
                i = tb % 3
                fw.dma("sp" if tb % 2 == 0 else "pool", xs[i][:], x_d[b, tb * 128:(tb + 1) * 128, :], writes=[k_xs[i]])
                for hh in range(2):
                    bi = 6 + (tb * 2 + hh) % 2
                    for kk in range(4):
                        kc = hh * 4 + kk
                        P("pe", lambda e: e.transpose(PB[bi][:, kk * 128:(kk + 1) * 128], xs[i][:, kc * 128:(kc + 1) * 128], ident_f[:]),
                          reads=[k_xs[i], k_const], writes=[PK[bi]])
                    copy_on(ev_eng(), xT[:, hh * 4:hh * 4 + 4, tb * 128:(tb + 1) * 128],
                            PB[bi][:].rearrange("p (k t) -> p k t", k=4), [PK[bi]], [k_xT[tb][hh]])

        for b in range(n_seq):
            if b == 0:
                with ExitStack() as ph:
                    stg = sb("stg0", [128, 4, 128], F32, ph)
                    k_stg = Tk()
                    for src, dstt in ((wuk_d, wuk), (wuv_d, wuv)):
                        fw.dma("sp", stg[:], src.rearrange("h a b -> a h b"), writes=[k_stg])
                        P("dve", lambda e: e.tensor_copy(dstt[:], stg[:]), reads=[k_stg], writes=[k_const])
                    btmp = sb("btmp", [128, 16, 128], F32, ph)
                    bt2 = sb("bt2", [128, 16, 128], F32, ph)
                    k_bt = Tk()
                    fw.dma("pool", btmp[:], bias_d.rearrange("d h s q -> s (d h) q"), writes=[k_bt])
                    P("dve", lambda e: e.tensor_scalar(btmp[:], btmp[:], SQ, None, ALU.mult), reads=[k_bt], writes=[k_bt])
                    P("dve", lambda e: e.tensor_copy(bhi[:], btmp[:]), reads=[k_bt], writes=[k_bias])
                    P("dve", lambda e: e.tensor_copy(bt2[:], bhi[:]), reads=[k_bias], writes=[k_bt])
                    P("dve", lambda e: e.tensor_tensor(bt2[:], btmp[:], bt2[:], ALU.subtract), reads=[k_bt], writes=[k_bt])
                    P("dve", lambda e: e.tensor_copy(blo[:], bt2[:]), reads=[k_bt], writes=[k_bias])
                    phase_x(0, ph)
                    fw.barrier()

            with ExitStack() as ph:
                qlT = sb("qlT", [128, 4, S], BF16, ph)
                iqT = sb("iqT", [128, 4, S], BF16, ph)
                ikT = sb("ikT", [128, 2, S], BF16, ph)
                cT = sb("cT", [128, S], BF16, ph)
                ctok = sb("ctok", [128, NB, 128], BF16, ph)
                wabs = sb("wabs", [128, NB, 8], F32, ph)
                wsgn = sb("wsgn", [128, NB, 8], F32, ph)
                k_ql = [[Tk() for _ in range(4)] for _ in range(4)]
                k_iq = [[Tk() for _ in range(4)] for _ in range(4)]
                k_ik = [Tk() for _ in range(4)]
                k_cT = [Tk() for _ in range(NB)]
                k_ctok = [Tk() for _ in range(NB)]
                k_w8 = [Tk() for _ in range(NB)]
                qtmp = [sb(f"qtmp{i}", [128, 512], BF16, ph) for i in range(2)]
                k_qtmp = [Tk(), Tk()]

                for h in range(4):
                    wt, k_w, _ = load_w([(C_Q + h * 128, 128)])
                    for tt in range(4):
                        bi = inproj_fm(wt, k_w, tt, (2, 3, 4))
                        i = (h * 4 + tt) % 2
                        copy_on(ev_eng(), qtmp[i][:], PB[bi][:], [PK[bi]], [k_qtmp[i]])
                        bo = 5 + (h * 4 + tt) % 2
                        P("pe", lambda e: e.matmul(PB[bo][:], wuk[:, h, :], qtmp[i][:], start=True, stop=True),
                          reads=[k_qtmp[i], k_const], writes=[PK[bo]])
                        copy_on(ev_eng(), qlT[:, h, tt * 512:(tt + 1) * 512], PB[bo][:], [PK[bo]], [k_ql[h][tt]])
                for j in range(4):
                    wt, k_w, _ = load_w([(C_IQ + j * 128, 128)])
                    for tt in range(4):
                        bi = inproj_fm(wt, k_w, tt, (2, 3, 4))
                        copy_on(ev_eng(), iqT[:, j, tt * 512:(tt + 1) * 512], PB[bi][:], [PK[bi]], [k_iq[j][tt]])
                wt, k_w, _ = load_w([(C_IK, 64), (C_IK, 64)])
                for tt in range(4):
                    bi = inproj_fm(wt, k_w, tt, (2, 3, 4))
                    P("pool", lambda e: e.memset(ikT[64:128, 0, tt * 512:(tt + 1) * 512], 0.0), writes=[k_ik[tt]])
                    P("pool", lambda e: e.memset(ikT[0:64, 1, tt * 512:(tt + 1) * 512], 0.0), writes=[k_ik[tt]])
                    copy_on("act", ikT[0:64, 0, tt * 512:(tt + 1) * 512], PB[bi][0:64, :], [PK[bi]], [k_ik[tt]])
                    copy_on("dve", ikT[64:128, 1, tt * 512:(tt + 1) * 512], PB[bi][64:128, :], [PK[bi]], [k_ik[tt]])

                wt, k_w, _ = load_w([(C_CKV, 128), (C_IW, 8)])
                sm = sb("a2sm", [128, NB, 4], F32, ph)
                junk = sb("a2junk", [128, 128], F32, ph)
                k_sm = [Tk() for _ in range(NB)]
                k_junk = Tk()
                for tb in range(NB):
                    bi = tb % 2
                    for k in range(KC):
                        P("pe", lambda e: e.matmul(PB[bi][:, 0:136], xT[:, k, tb * 128:(tb + 1) * 128], wt[:, k, 0:136], start=(k == 0), stop=(k == KC - 1)),
                          reads=[k_w, k_xT[tb][0], k_xT[tb][1]], writes=[PK[bi]])
                    P("act", lambda e: e.activation(junk[:], PB[bi][:, 0:128], ACTF.Square, accum_out=sm[:, tb, 0:1]),
                      reads=[PK[bi]], writes=[k_junk, k_sm[tb]])
                    P("act", lambda e: e.activation(sm[:, tb, 1:2], sm[:, tb, 0:1], ACTF.Sqrt, bias=epsc[:, 0:1], scale=1.0 / 128),
                      reads=[k_sm[tb], k_const], writes=[k_sm[tb]])
                    P("dve", lambda e: e.reciprocal(sm[:, tb, 2:3], sm[:, tb, 1:2]), reads=[k_sm[tb]], writes=[k_sm[tb]])
                    P("dve", lambda e: e.scalar_tensor_tensor(ctok[:, tb, :], PB[bi][:, 0:128], sm[:, tb, 2:3], kvg[:], ALU.mult, ALU.mult),
                      reads=[PK[bi], k_sm[tb], k_const], writes=[k_ctok[tb]])
                    P("act", lambda e: e.activation(wabs[:, tb, :], PB[bi][:, 128:136], ACTF.Abs, scale=(64 ** -0.5) * (8 ** -0.5)),
                      reads=[PK[bi]], writes=[k_w8[tb]])
                    P("act", lambda e: e.activation(wsgn[:, tb, :], PB[bi][:, 128:136], ACTF.Sign),
                      reads=[PK[bi]], writes=[k_w8[tb]])
                    bo = 5 + tb % 2
                    pbv = PB[bo][:].bitcast(BF16)
                    P("pe", lambda e: e.transpose(pbv[:, 0:128], ctok[:, tb, :], ident_b[:]), reads=[k_ctok[tb], k_const], writes=[PK[bo]])
                    copy_on("act", cT[:, tb * 128:(tb + 1) * 128], pbv[:, 0:128], [PK[bo]], [k_cT[tb]])
                if b == 0 and "ctok" in dbg_d:
                    tmpf = sb("dbgtmp", [128, NB, 128], F32, ph)
                    k_t = Tk()
                    P("dve", lambda e: e.tensor_copy(tmpf[:], ctok[:]), reads=k_ctok, writes=[k_t])
                    dbg_out("ctok", tmpf[:], k_t)
                if b == 0 and "wabs" in dbg_d:
                    for tb in range(NB):
                        out_toks.append(fw.dma("sp", dbg_d["wabs"][:, tb, 0:8], wabs[:, tb, :], reads=[k_w8[tb]]))
                        out_toks.append(fw.dma("sp", dbg_d["wabs"][:, tb, 8:16], wsgn[:, tb, :], reads=[k_w8[tb]]))

                score = [[sb(f"score{i}{j}", [128, S], F32, ph) for j in range(2)] for i in range(2)]
                k_score = [[Tk(), Tk()] for _ in range(2)]
                rt = [sb(f"rt{i}", [128, 512], BF16, ph) for i in range(4)]
                k_rt = [Tk() for _ in range(4)]
                dsg = [sb(f"dsg{i}", [128, 8, 128], BF16, ph) for i in range(2)]
                k_dsg = [Tk(), Tk()]
                bsL = sb("bsL", [128, 2], F32, ph)
                bsM = sb("bsM", [128, 2], F32, ph)
                bsC = sb("bsC", [128, 2], F32, ph)
                bsG = sb("bsG", [128, 2], F32, ph)
                bsW = sb("bsW", [128, 2], F32, ph)
                bsH = sb("bsH", [128, NI, 2], F32, ph)
                bsT = sb("bsT", [128, NB], F32, ph)
                thrc = sb("thrc", [128, NB // 2, 2], F32, ph)
                sgnr = sb("sgnr", [128, 2], F32, ph)
                k_bs = Tk()
                k_bL, k_bM, k_bG, k_bW, k_bH = Tk(), Tk(), Tk(), Tk(), Tk()
                k_bC = [Tk(), Tk()]
                k_bsT = [Tk() for _ in range(NB)]
                k_thrc = Tk()
                k_bsj = [Tk(), Tk()]
                bsH2 = sb("bsH2", [128, NI], F32, ph)
                bsS = sb("bsS", [128, 2], F32, ph)
                bsK = sb("bsK", [128, NB // 2], F32, ph)
                negm = [[sb(f"negm{i}{j}", [128, S], BF16, ph) for j in range(2)] for i in range(2)]
                k_negm = [[Tk(), Tk()] for _ in range(2)]
                pt = [sb(f"pt{i}", [128, 512], BF16, ph) for i in range(3)]
                k_pt = [Tk() for _ in range(3)]
                rden = sb("rden", [128, 512], F32, ph)
                k_rden = Tk()
                P("dve", lambda e: e.memset(sgnr[:, 0:1], 1.0), writes=[k_thrc])
                P("dve", lambda e: e.memset(sgnr[:, 1:2], -1.0), writes=[k_thrc])
                for g in range(NB // 2):
                    P("dve", lambda e: e.memset(bsK[:, g:g + 1], 0.5 - float(2 * TOPK - 128 * (2 * g + 2))), writes=[k_thrc])
                    P("dve", lambda e: e.memset(thrc[:, g, 0:1], float(TOPK)), writes=[k_thrc])
                    P("dve", lambda e: e.memset(thrc[:, g, 1:2], float(2 * TOPK - 128 * (2 * g + 2))), writes=[k_thrc])
                rtc = [0]
                ptc = [0]

                def stage1(g):
                    for jj in range(2):
                        qb = 2 * g + jj
                        sc = score[g % 2][jj]
                        k_sc = k_score[g % 2][jj]
                        nk = 128 * (qb + 1)
                        ngrp = (nk + 511) // 512
                        di = qb % 2
                        P("pool", lambda e: e.tensor_tensor(dsg[di][:], ident_b[:].unsqueeze(1).to_broadcast([128, 8, 128]),
                                                            wsgn[:, qb, :].unsqueeze(2).to_broadcast([128, 8, 128]), ALU.mult),
                          reads=[k_const, k_w8[qb]], writes=[k_dsg[di]])
                        for gg in range(ngrp):
                            k0 = gg * 512
                            kn = min(512, nk - k0)
                            ba = 2 + gg % 2
                            ris = []

                            def dots(h):
                                j, half = h // 2, h % 2
                                bi = h % 2
                                p0 = half * 64
                                P("pe", lambda e: e.matmul(PB[bi][:, 0:kn], iqT[:, j, qb * 128:(qb + 1) * 128], ikT[:, half, k0:k0 + kn], start=True, stop=True),
                                  reads=[k_iq[j][qb // 4], k_ik[gg]], writes=[PK[bi]])
                                ri = rtc[0] % 4
                                rtc[0] += 1
                                ris.append(ri)
                                P("act", lambda e: e.activation(rt[ri][:, 0:kn], PB[bi][:, 0:kn], ACTF.Relu, scale=wabs[:, qb, h:h + 1]),
                                  reads=[PK[bi], k_w8[qb]], writes=[k_rt[ri]])

                            def accum(h):
                                ri = ris[h]
                                P("pe", lambda e: e.matmul(PB[ba][:, 0:kn], dsg[di][:, h, :], rt[ri][:, 0:kn], start=(h == 0), stop=(h == 7)),
                                  reads=[k_dsg[di], k_rt[ri]], writes=[PK[ba]])
                            dots(0)
                            for h in range(8):
                                if h + 1 < 8:
                                    dots(h + 1)
                                accum(h)
                                yield
                            d0 = qb * 128
                            last = (k0 + kn == nk)
                            nmain = kn - 128 if last else kn
                            if nmain > 0:
                                P("dve", lambda e: e.tensor_copy(sc[:, k0:k0 + nmain], PB[ba][:, 0:nmain]), reads=[PK[ba]], writes=[k_sc])
                            if last:
                                P("dve", lambda e: e.tensor_tensor(sc[:, d0:d0 + 128], PB[ba][:, kn - 128:kn], adm[:], ALU.mult), reads=[PK[ba], k_const], writes=[k_sc])
                                P("dve", lambda e: e.tensor_tensor(sc[:, d0:d0 + 128], sc[:, d0:d0 + 128], negb[:], ALU.add), reads=[k_sc, k_const], writes=[k_sc])

                def stage2(g):
                    qbs = (2 * g, 2 * g + 1)
                    scs_ = score[g % 2]
                    ks = k_score[g % 2]
                    nm = negm[g % 2]
                    knm = k_negm[g % 2]
                    if g == 0:
                        for jj in range(2):
                            P("dve", lambda e: e.memset(bsT[:, qbs[jj]:qbs[jj] + 1], -1.0e29), writes=[k_bsT[qbs[jj]]])
                    else:
                        nks = [128 * (q + 1) for q in qbs]
                        for jj in range(2):
                            P("dve", lambda e: e.tensor_reduce(bsL[:, jj:jj + 1], scs_[jj][:, 0:qbs[jj] * 128], AX.X, ALU.min), reads=[ks[jj]], writes=[k_bL])
                            P("dve", lambda e: e.tensor_reduce(bsW[:, jj:jj + 1], scs_[jj][:, 0:nks[jj]], AX.X, ALU.max), reads=[ks[jj]], writes=[k_bW])
                        P("dve", lambda e: e.tensor_tensor(bsW[:], bsW[:], bsL[:], ALU.subtract), reads=[k_bW, k_bL], writes=[k_bW])
                        P("dve", lambda e: e.tensor_tensor(bsW[:], bsW[:], sgnr[:], ALU.mult), reads=[k_bW, k_thrc], writes=[k_bW])
                        P("dve", lambda e: e.tensor_tensor(bsL[:], bsL[:], sgnr[:], ALU.mult), reads=[k_bL, k_thrc], writes=[k_bL])
                        P("dve", lambda e: e.tensor_tensor(bsH[:], pow2[:].unsqueeze(2).to_broadcast([128, NI, 2]),
                                                           bsW[:].unsqueeze(1).to_broadcast([128, NI, 2]), ALU.mult), reads=[k_bW, k_const], writes=[k_bH])
                        for it in range(NI):
                            P("dve", lambda e: e.tensor_tensor(bsM[:], bsL[:], bsH[:, it, :], ALU.add), reads=[k_bL, k_bH], writes=[k_bM])
                            P("dve", lambda e: e.tensor_scalar(nm[0][:, 0:nks[0]], scs_[0][:, 0:nks[0]], bsM[:, 0:1], float(TOPK - nks[1]), ALU.is_ge, ALU.add, accum_out=bsC[:, 0:1]),
                              reads=[ks[0], k_bM], writes=[knm[0], k_bC[0]])
                            P("act", lambda e: e.activation(nm[1][:, 0:nks[1]], scs_[1][:, 0:nks[1]], ACTF.Sign, bias=bsM[:, 1:2], scale=1.0, accum_out=bsC[:, 1:2]),
                              reads=[ks[1], k_bM], writes=[knm[1], k_bC[1]])
                            P("dve", lambda e: e.scalar_tensor_tensor(bsG[:], bsC[:], float(2 * TOPK - nks[1]), bsH[:, it, :], ALU.is_ge, ALU.mult), reads=k_bC + [k_bH], writes=[k_bG])
                            P("dve", lambda e: e.tensor_tensor(bsL[:], bsL[:], bsG[:], ALU.add), reads=[k_bL, k_bG], writes=[k_bL])
                            yield
                        P("dve", lambda e: e.tensor_tensor(bsT[:, qbs[0]:qbs[0] + 2], bsL[:], sgnr[:], ALU.mult), reads=[k_bL, k_thrc], writes=[k_bsT[qbs[0]], k_bsT[qbs[1]]])
                    for jj in range(2):
                        qb = qbs[jj]
                        nk = 128 * (qb + 1)
                        P("dve", lambda e: e.tensor_scalar(nm[jj][:, 0:nk], scs_[jj][:, 0:nk], bsT[:, qb:qb + 1], -30000.0, ALU.is_lt, ALU.mult),
                          reads=[ks[jj], k_bsT[qb]], writes=[knm[jj]])
                        if b == 0 and "score" in dbg_d:
                            out_toks.append(fw.dma("sp", dbg_d["score"][qb, :, 0:nk], scs_[jj][:, 0:nk], reads=[ks[jj]]))
                            out_toks.append(fw.dma("sp", dbg_d["thr"][qb, :, :], bsT[:, qb:qb + 1], reads=[k_bsT[qb]]))
                    yield

                def stage3(g):
                    for jj in range(2):
                        qb = 2 * g + jj
                        nm = negm[g % 2][jj]
                        knm = k_negm[g % 2][jj]
                        pis = {}

                        def logits(kb):
                            bl = 4 + kb % 2
                            dd = min(qb - kb, 3)
                            P("pe", lambda e: e.matmul(PB[bl][:], cT[:, kb * 128:(kb + 1) * 128], qlT[:, :, qb * 128:(qb + 1) * 128], start=True, stop=False),
                              reads=[k_cT[kb]] + [k_ql[h][qb // 4] for h in range(4)], writes=[PK[bl]])
                            P("pe", lambda e: e.matmul(PB[bl][:], ident_b[:], bhi[:, dd * 4:dd * 4 + 4, :], start=False, stop=False),
                              reads=[k_const, k_bias], writes=[PK[bl]])
                            P("pe", lambda e: e.matmul(PB[bl][:], ident_b[:], blo[:, dd * 4:dd * 4 + 4, :], start=False, stop=False),
                              reads=[k_const, k_bias], writes=[PK[bl]])
                            P("pe", lambda e: e.matmul(PB[bl][:], nm[:, kb * 128:(kb + 1) * 128], ident4[:].rearrange("p a b -> p (a b)"), start=False, stop=True),
                              reads=[knm, k_const], writes=[PK[bl]])
                            pi = ptc[0] % 3
                            ptc[0] += 1
                            pis[kb] = pi
                            P("act", lambda e: e.activation(pt[pi][:], PB[bl][:], ACTF.Exp, scale=128 ** -0.5), reads=[PK[bl]], writes=[k_pt[pi]])

                        def pv(kb):
                            pi = pis[kb]
                            P("pe", lambda e: e.matmul(PB[6][:], ctok[:, kb, :], pt[pi][:], start=(kb == 0), stop=(kb == qb)),
                              reads=[k_ctok[kb], k_pt[pi]], writes=[PK[6]])
                            P("pe", lambda e: e.matmul(PB[7][:], ones_b[:], pt[pi][:], start=(kb == 0), stop=(kb == qb)),
                              reads=[k_const, k_pt[pi]], writes=[PK[7]])
                        logits(0)
                        for kb in range(qb + 1):
                            if kb + 1 <= qb:
                                logits(kb + 1)
                            pv(kb)
                            yield
                        P("dve", lambda e: e.reciprocal(rden[:], PB[7][:]), reads=[PK[7]], writes=[k_rden])
                        tt = qb // 4
                        P("dve", lambda e: e.tensor_tensor(OT[:, 0:4, qb * 128:(qb + 1) * 128], PB[6][:].rearrange("r (h q) -> r h q", h=4),
                                                           rden[:].rearrange("r (h q) -> r h q", h=4), ALU.mult),
                          reads=[PK[6], k_rden], writes=[k_OT[h][tt] for h in range(4)])

                NG = NB // 2

                def n_units1(g):
                    return sum(((128 * (q + 1) + 511) // 512) * 8 for q in (2 * g, 2 * g + 1))

                def n_units3(g):
                    return sum(q + 1 for q in (2 * g, 2 * g + 1))

                def advance(gen, k):
                    for _ in range(k):
                        try:
                            next(gen)
                        except StopIteration:
                            return False
                    return True

                for step in range(NG + 2):
                    g1 = stage1(step) if step < NG else None
                    g2 = stage2(step - 1) if 1 <= step <= NG else None
                    g3 = stage3(step - 2) if 2 <= step else None
                    n2 = (NI + 1) if (g2 is not None and step - 1 >= 1) else 1
                    r1 = -(-n_units1(step) // n2) if g1 is not None else 0
                    r3 = -(-n_units3(step - 2) // n2) if g3 is not None else 0
                    alive = True
                    while alive:
                        alive = False
                        if g2 is not None and advance(g2, 1):
                            alive = True
                        if g1 is not None and advance(g1, r1):
                            alive = True
                        if g3 is not None and advance(g3, r3):
                            alive = True
                fw.barrier()

            with ExitStack() as ph:
                rmask = sb("rmask", [128, S], BF16, ph)
                k_rm = Tk()
                P("pool", lambda e: e.memset(rmask[:], 1.0), writes=[k_rm])
                P("pool", lambda e: e.memset(rmask[:].rearrange("p (n c) -> p n c", c=64)[:, :, 0:1], 0.0), writes=[k_rm])
                big1 = sb("big1", [128, 2 * S], F32, ph)
                big2 = sb("big2", [128, 2 * S], F32, ph)
                fb = big1[:, 0:S]
                kkb = big1[:, S:2 * S]
                ex = big2[:, 0:S]
                dS = big1[:].rearrange("k (v n) -> k v n", n=32)
                a0 = big2[:].rearrange("k (v n) -> k v n", n=32)
                k_fb = [Tk() for _ in range(4)]
                k_kkb = [Tk() for _ in range(4)]
                k_ex = [Tk() for _ in range(4)]
                k_b2b = Tk()
                k_big1 = k_fb + k_kkb
                k_big2 = k_ex + [k_b2b]
                qsf = sb("qsf", [128, S], F32, ph)
                k_qsf = [Tk() for _ in range(4)]
                th2_ = sb("th2", [128, 512], F32, ph)
                th2 = [th2_, th2_]
                k_th2_ = Tk()
                k_th2 = [k_th2_, k_th2_]
                the = sb("the", [128, 512], F32, ph)
                qse = sb("qse", [128, 512], F32, ph)
                ob = sb("ob", [128, 512], F32, ph)
                sq = sb("sq", [128, 512], F32, ph)
                rs = sq
                k_the, k_qse, k_ob, k_sq = Tk(), Tk(), Tk(), Tk()
                k_rs = k_sq
                NHB = 2
                qtl = [sb(f"qtl{i}", [128, S], BF16, ph) for i in range(NHB)]
                qhl = [sb(f"qhl{i}", [128, S], BF16, ph) for i in range(NHB)]
                ktl = [sb(f"ktl{i}", [128, S], BF16, ph) for i in range(NHB)]
                vtok = [sb(f"vtok{i}", [128, NB, 128], BF16, ph) for i in range(NHB)]
                Vb = [sb(f"Vb{i}", [128, 128, 32], BF16, ph) for i in range(NHB)]
                wgt = [sb(f"wgt{i}", [128, KC, 128], BF16, ph) for i in range(NHB)]
                ktA = sb("ktA", [128, NB, 128], BF16, ph)
                ktB = sb("ktB", [128, NB, 128], BF16, ph)
                cols = sb("hcols", [128, 4, 32], F32, ph)
                scs = [sb(f"scs{i}", [128, 128], BF16, ph) for i in range(2)]
                k_qtl = [[Tk() for _ in range(4)] for _ in range(NHB)]
                k_qhl = [Tk() for _ in range(NHB)]
                k_ktl = [Tk() for _ in range(NHB)]
                k_vtok = [[Tk() for _ in range(NB)] for _ in range(NHB)]
                k_Vb = [Tk() for _ in range(NHB)]
                k_wgt = [Tk() for _ in range(NHB)]
                k_kt = [Tk() for _ in range(NB)]
                k_cols = Tk()
                k_scs = [Tk(), Tk()]
                P("pool", lambda e: e.memset(ktA[64:128, :, :], 0.0), writes=k_kt)
                P("pool", lambda e: e.memset(ktB[0:64, :, :], 0.0), writes=k_kt)
                P("pool", lambda e: e.memset(cols[:, 3, 0:1], 0.0), writes=[k_cols])

                tht, sgt = the, qse
                k_tht, k_sgt = k_the, k_qse
                for h in range(4):
                    wt, k_w, _ = load_w([(C_AG + h * 128, 128)])
                    for tt in range(4):
                        bi = inproj_fm(wt, k_w, tt, (0, 1))
                        P("act", lambda e: e.activation(tht[:], PB[bi][:], ACTF.Tanh, scale=0.5), reads=[PK[bi]], writes=[k_tht])
                        P("dve", lambda e: e.scalar_tensor_tensor(sgt[:], tht[:], 1.0, PB[bi][:], ALU.add, ALU.mult),
                          reads=[k_tht, PK[bi]], writes=[k_sgt])
                        bo = 7
                        P("pe", lambda e: e.matmul(PB[bo][:], wuv[:, h, :], OT[:, h, tt * 512:(tt + 1) * 512], start=True, stop=True),
                          reads=[k_const, k_OT[h][tt]], writes=[PK[bo]])
                        P("dve", lambda e: e.scalar_tensor_tensor(OT[:, h, tt * 512:(tt + 1) * 512], PB[bo][:], 0.5, sgt[:], ALU.mult, ALU.mult),
                          reads=[PK[bo], k_sgt], writes=[k_OT[h][tt]])
                def prep(h):
                    hh = h % NHB
                    wt, k_w, _ = load_w([(C_BF + h * 128, 128)])
                    for tt in range(4):
                        bi = inproj_fm(wt, k_w, tt, (0, 1))
                        sl = slice(tt * 512, (tt + 1) * 512)
                        P("act", lambda e: e.activation(fb[:, sl], PB[bi][:], ACTF.Tanh, scale=0.5), reads=[PK[bi]], writes=[k_fb[tt]])
                        P("act", lambda e: e.activation(kkb[:, sl], fb[:, sl], ACTF.Identity, bias=lbB[:, h:h + 1], scale=lbNB[:, h:h + 1]),
                          reads=[k_fb[tt], k_const], writes=[k_kkb[tt]])
                        P("dve", lambda e: e.tensor_scalar(fb[:, sl], fb[:, sl], lbB[:, h:h + 1], lbA[:, h:h + 1], ALU.mult, ALU.add),
                          reads=[k_fb[tt], k_const], writes=[k_fb[tt]])
                        yield
                    P("act", lambda e: e.activation(fb, fb, ACTF.Ln), reads=k_fb, writes=k_fb)
                    P("dve", lambda e: e.tensor_tensor_scan(fb, rmask[:], fb, 0.0, ALU.mult, ALU.add), reads=k_fb + [k_rm], writes=k_fb)
                    yield
                    wt, k_w, _ = load_w([(C_BQ + h * 128, 128)])
                    for tt in range(4):
                        i = tt % 2
                        bi = inproj_fm(wt, k_w, tt, (0, 1))
                        sl = slice(tt * 512, (tt + 1) * 512)
                        P("act", lambda e: e.activation(th2[i][:], PB[bi][:], ACTF.Tanh, scale=0.5), reads=[PK[bi]], writes=[k_th2[i]])
                        P("dve", lambda e: e.scalar_tensor_tensor(qsf[:, sl], th2[i][:], 1.0, PB[bi][:], ALU.add, ALU.mult),
                          reads=[k_th2[i], PK[bi]], writes=[k_qsf[tt]])
                        yield
                    fb3 = fb.rearrange("p (n c) -> p n c", c=64)
                    cl = cols
                    kc_ = k_cols
                    P("dve", lambda e: e.tensor_copy(cl[:, 0, :], fb3[:, :, 31]), reads=k_fb, writes=[kc_])
                    P("dve", lambda e: e.tensor_copy(cl[:, 1, :], fb3[:, :, 63]), reads=k_fb, writes=[kc_])
                    P("dve", lambda e: e.tensor_tensor(cl[:, 2, 1:32], cl[:, 1, 0:31], cl[:, 0, 0:31], ALU.subtract), reads=[kc_], writes=[kc_])
                    P("dve", lambda e: e.tensor_tensor(cl[:, 2, 1:32], cl[:, 2, 1:32], cl[:, 0, 1:32], ALU.add), reads=[kc_], writes=[kc_])
                    P("act", lambda e: e.activation(cl[:, 3, 1:32], cl[:, 2, 1:32], ACTF.Exp), reads=[kc_], writes=[kc_])
                    P("dve", lambda e: e.tensor_tensor(fb3, fb3, cl[:, 0, :].unsqueeze(2).to_broadcast([128, 32, 64]), ALU.subtract),
                      reads=k_fb + [kc_], writes=k_fb)
                    yield
                    P("act", lambda e: e.activation(ex, fb, ACTF.Exp, scale=-1.0), reads=k_fb, writes=k_ex)
                    P("dve", lambda e: e.tensor_tensor(ktl[hh][:], kkb, ex, ALU.mult), reads=k_kkb + k_ex, writes=[k_ktl[hh]])
                    P("act", lambda e: e.activation(ex, fb, ACTF.Exp), reads=k_fb, writes=k_ex)
                    yield
                    wt, k_w, _ = load_w([(C_BI + h * 128, 128)])
                    for tb in range(NB):
                        bi = tb % 2
                        for k in range(KC):
                            P("pe", lambda e: e.matmul(PB[bi][:, 0:128], xT[:, k, tb * 128:(tb + 1) * 128], wt[:, k, 0:128], start=(k == 0), stop=(k == KC - 1)),
                              reads=[k_w, k_xT[tb][0], k_xT[tb][1]], writes=[PK[bi]])
                        copy_on(ev_eng(), vtok[hh][:, tb, :], PB[bi][:, 0:128], [PK[bi]], [k_vtok[hh][tb]])
                        if tb % 4 == 3:
                            yield
                    for tb in range(NB):
                        bo = 2 + tb % 2
                        pbv = PB[bo][:].bitcast(BF16)
                        P("pe", lambda e: e.transpose(pbv[:, 0:128], ktl[hh][:, tb * 128:(tb + 1) * 128], ident_b[:]), reads=[k_ktl[hh], k_const], writes=[PK[bo]])
                        copy_on("act", ktA[0:64, tb, :], pbv[0:64, 0:128], [PK[bo]], [k_kt[tb]])
                        copy_on("act", ktB[64:128, tb, :], pbv[64:128, 0:128], [PK[bo]], [k_kt[tb]])
                        if tb % 4 == 3:
                            yield
                    for tt in range(4):
                        sl = slice(tt * 512, (tt + 1) * 512)
                        P("dve", lambda e: e.scalar_tensor_tensor(qtl[hh][:, sl], qsf[:, sl], 0.5, ex[:, sl], ALU.mult, ALU.mult),
                          reads=[k_qsf[tt]] + k_ex, writes=[k_qtl[hh][tt]])
                    qt3 = qtl[hh][:].rearrange("p (n c) -> p n c", c=64)
                    qh3 = qhl[hh][:].rearrange("p (n c) -> p n c", c=64)
                    P("pool", lambda e: e.tensor_tensor(qh3[:, 1:32, :], qt3[:, 1:32, :], cl[:, 3, 1:32].unsqueeze(2).to_broadcast([128, 31, 64]), ALU.mult),
                      reads=k_qtl[hh] + [kc_], writes=[k_qhl[hh]])
                    yield
                    P("dve", lambda e: e.tensor_copy(a0, cl[:, 3, :].unsqueeze(1).to_broadcast([128, 128, 32])), reads=[kc_] + k_big2, writes=k_big2)
                    for grp in range(8):
                        bd = grp % 4
                        for c4 in range(4):
                            n = grp * 4 + c4
                            tb, half = n // 2, n % 2
                            kt_ = ktA if half == 0 else ktB
                            P("pe", lambda e: e.matmul(PB[bd][:, c4 * 128:(c4 + 1) * 128], kt_[:, tb, :], vtok[hh][:, tb, :], start=True, stop=True),
                              reads=[k_kt[tb], k_vtok[hh][tb]], writes=[PK[bd]])
                        copy_on("act", dS[:, :, grp * 4:grp * 4 + 4], PB[bd][:].rearrange("k (n v) -> k v n", n=4), [PK[bd]] + k_big1, k_big1)
                        if grp % 2 == 1:
                            yield
                    P("dve", lambda e: e.tensor_tensor_scan(big1[:], big2[:], big1[:], 0.0, ALU.mult, ALU.add), reads=k_big1 + k_big2, writes=k_big1)
                    P("act", lambda e: e.activation(Vb[hh][:], dS, ACTF.Copy), reads=k_big1, writes=[k_Vb[hh]])
                    wt, k_w, _ = load_w([(C_BG + h * 128, 128)])
                    P("dve", lambda e: e.tensor_copy(wgt[hh][:], wt[:, :, 0:128]), reads=[k_w], writes=[k_wgt[hh]])
                    yield

                def outp(h):
                    hh = h % NHB
                    for tt in range(4):
                        bacc = 5
                        for tbl in range(4):
                            tb = tt * 4 + tbl
                            tsl = slice(tb * 128, (tb + 1) * 128)
                            si = tb % 2
                            bs_ = 4
                            P("pe", lambda e: e.matmul(PB[bs_][:, 0:128], ktl[hh][:, tsl], qtl[hh][:, tsl], start=True, stop=True),
                              reads=[k_ktl[hh], k_qtl[hh][tt]], writes=[PK[bs_]])
                            P("dve", lambda e: e.tensor_tensor(scs[si][:], PB[bs_][:, 0:128], caus[:], ALU.mult), reads=[PK[bs_], k_const], writes=[k_scs[si]])
                            for half in range(2):
                                n = tb * 2 + half
                                csl = slice(n * 64, (n + 1) * 64)
                                oc = slice((tbl * 2 + half) * 64, (tbl * 2 + half + 1) * 64)
                                P("pe", lambda e: e.matmul(PB[bacc][:, oc], vtok[hh][:, tb, :], scs[si][:, half * 64:(half + 1) * 64], start=True, stop=(n == 0)),
                                  reads=[k_vtok[hh][tb], k_scs[si]], writes=[PK[bacc]])
                                if n > 0:
                                    P("pe", lambda e: e.matmul(PB[bacc][:, oc], Vb[hh][:, :, n - 1], qhl[hh][:, csl], start=False, stop=True),
                                      reads=[k_Vb[hh], k_qhl[hh]], writes=[PK[bacc]])
                            yield
                        sl = slice(tt * 512, (tt + 1) * 512)
                        P("act", lambda e: e.activation(ob[:], PB[bacc][:], ACTF.Copy), reads=[PK[bacc]], writes=[k_ob])
                        P("act", lambda e: e.activation(sq[:], PB[bacc][:], ACTF.Square), reads=[PK[bacc]], writes=[k_sq])
                        P("pe", lambda e: e.matmul(PB[6][:], ones_f[:], sq[:], start=True, stop=True), reads=[k_const, k_sq], writes=[PK[6]])
                        P("act", lambda e: e.activation(rs[:], PB[6][:], ACTF.Ln, bias=epsc[:, 0:1], scale=1.0 / 128), reads=[PK[6], k_const], writes=[k_rs])
                        P("act", lambda e: e.activation(rs[:], rs[:], ACTF.Exp, scale=-0.5), reads=[k_rs], writes=[k_rs])
                        yield
                        P("dve", lambda e: e.scalar_tensor_tensor(ob[:], ob[:], hgt[:, h:h + 1], rs[:], ALU.mult, ALU.mult),
                          reads=[k_ob, k_const, k_rs], writes=[k_ob])
                        bi = inproj_fm(wgt[hh], k_wgt[hh], tt, (7,))
                        P("act", lambda e: e.activation(the[:], PB[bi][:], ACTF.Tanh, scale=0.5), reads=[PK[bi]], writes=[k_the])
                        P("dve", lambda e: e.scalar_tensor_tensor(qse[:], the[:], 1.0, PB[bi][:], ALU.add, ALU.mult),
                          reads=[k_the, PK[bi]], writes=[k_qse])
                        P("pool", lambda e: e.tensor_tensor(OT[:, 4 + h, sl], ob[:], qse[:], ALU.mult),
                          reads=[k_ob, k_qse], writes=[k_OT[4 + h][tt]])
                        yield

                def run_pair(ga, gb):
                    alive = True
                    while alive:
                        alive = False
                        for g_ in (ga, gb):
                            if g_ is None:
                                continue
                            try:
                                next(g_)
                                alive = True
                            except StopIteration:
                                pass

                for h in range(5):
                    run_pair(prep(h) if h < 4 else None, outp(h - 1) if h >= 1 else None)
                fw.barrier()

            with ExitStack() as ph:
                if b == 0 and "OT" in dbg_d:
                    tmpf3 = sb("dbgtmp3", [128, 8, S], F32, ph)
                    k_t3 = Tk()
                    P("dve", lambda e: e.tensor_copy(tmpf3[:], OT[:]), reads=[k_OT[c][t] for c in range(8) for t in range(4)], writes=[k_t3])
                    dbg_out("OT", tmpf3[:], k_t3)
                wo = sb("wo", [128, KC, D], BF16, ph)
                k_wo = [Tk() for _ in range(4)]
                stg2 = [sb(f"stgw{i}", [128, KC, 256], F32, ph) for i in range(2)]
                k_s2 = [Tk(), Tk()]
                for j in range(4):
                    fw.dma("sp" if j % 2 == 0 else "pool", stg2[j % 2][:],
                           wo_d[:, j * 256:(j + 1) * 256].rearrange("(k p) c -> p k c", p=128), writes=[k_s2[j % 2]])
                    P("dve" if j % 2 == 0 else "act", (lambda e: e.tensor_copy(wo[:, :, j * 256:(j + 1) * 256], stg2[j % 2][:])) if j % 2 == 0 else
                      (lambda e: e.activation(wo[:, :, j * 256:(j + 1) * 256], stg2[j % 2][:], ACTF.Copy)),
                      reads=[k_s2[j % 2]], writes=[k_wo[j]])
                NR = 3
                xr = [sb(f"xr{i}", [128, D], F32, ph) for i in range(NR)]
                lng = sb("lng", [128, D], F32, ph)
                lnb = sb("lnb", [128, D], F32, ph)
                k_ln = Tk()
                fw.dma("sp", lng[:], lng_d.partition_broadcast(128), writes=[k_ln])
                fw.dma("sp", lnb[:], lnb_d.partition_broadcast(128), writes=[k_ln])
                zt = [sb(f"zt{i}", [128, D], F32, ph) for i in range(NR)]
                zn = [sb(f"zn{i}", [128, D], F32, ph) for i in range(NR)]
                st6 = [sb(f"st6{i}", [128, 2, 6], F32, ph) for i in range(NR)]
                mv = [sb(f"mv{i}", [128, 4], F32, ph) for i in range(NR)]
                k_xr = [Tk() for _ in range(NR)]
                k_zt = [Tk() for _ in range(NR)]
                k_zn = [Tk() for _ in range(NR)]
                k_st = [Tk() for _ in range(NR)]
                for tb0 in range(2):
                    fw.dma("sp", xr[tb0 % NR][:], x_d[b, tb0 * 128:(tb0 + 1) * 128, :], writes=[k_xr[tb0 % NR]])
                for tb in range(NB):
                    i = tb % NR
                    tt = tb // 4
                    if tb + 2 < NB:
                        fw.dma("sp", xr[(tb + 2) % NR][:], x_d[b, (tb + 2) * 128:(tb + 3) * 128, :], writes=[k_xr[(tb + 2) % NR]])
                    for hh in range(2):
                        bi = (tb % 2) * 2 + hh
                        for kc in range(8):
                            P("pe", lambda e: e.matmul(PB[bi][:], OT[:, kc, tb * 128:(tb + 1) * 128], wo[:, kc, hh * 512:(hh + 1) * 512], start=(kc == 0), stop=(kc == 7)),
                              reads=[k_OT[kc][tt], k_wo[2 * hh], k_wo[2 * hh + 1]], writes=[PK[bi]])
                        P("dve", lambda e: e.scalar_tensor_tensor(zt[i][:, hh * 512:(hh + 1) * 512], xr[i][:, hh * 512:(hh + 1) * 512], ALPHA, PB[bi][:], ALU.mult, ALU.add),
                          reads=[k_xr[i], PK[bi]], writes=[k_zt[i]])
                        P("dve", lambda e: e.bn_stats(st6[i][:, hh, :], zt[i][:, hh * 512:(hh + 1) * 512]), reads=[k_zt[i]], writes=[k_st[i]])
                    P("dve", lambda e: e.bn_aggr(mv[i][:, 0:2], st6[i][:]), reads=[k_st[i]], writes=[k_st[i]])
                    P("act", lambda e: e.activation(mv[i][:, 2:3], mv[i][:, 1:2], ACTF.Sqrt, bias=epsc[:, 1:2], scale=1.0), reads=[k_st[i], k_const], writes=[k_st[i]])
                    P("dve", lambda e: e.reciprocal(mv[i][:, 2:3], mv[i][:, 2:3]), reads=[k_st[i]], writes=[k_st[i]])
                    P("dve", lambda e: e.scalar_tensor_tensor(mv[i][:, 3:4], mv[i][:, 0:1], -1.0, mv[i][:, 2:3], ALU.mult, ALU.mult), reads=[k_st[i]], writes=[k_st[i]])
                    P("act", lambda e: e.activation(zn[i][:], zt[i][:], ACTF.Identity, bias=mv[i][:, 3:4], scale=mv[i][:, 2:3]), reads=[k_zt[i], k_st[i]], writes=[k_zn[i]])
                    P("dve", lambda e: e.tensor_tensor(zn[i][:], zn[i][:], lng[:], ALU.mult), reads=[k_zn[i], k_ln], writes=[k_zn[i]])
                    P("pool", lambda e: e.tensor_tensor(zn[i][:], zn[i][:], lnb[:], ALU.add), reads=[k_zn[i], k_ln], writes=[k_zn[i]])
                    out_toks.append(fw.dma("pool", out_d[b, tb * 128:(tb + 1) * 128, :], zn[i][:], reads=[k_zn[i]]))
                if b + 1 < n_seq:
                    phase_x(b + 1, ph)
                fw.barrier()
        fw.flush()
        _build.last_counts = (fw.nops, dict(fw.sigcnt))
    return nc


def _t5_bucket_np(rel):
    nb = 16
    max_exact = 8
    ret = np.where(rel > 0, nb, 0).astype(np.int32)
    n = np.abs(rel)
    nf = np.maximum(n, 1).astype(np.float32)
    large = max_exact + (np.log(nf / max_exact) / math.log(256 / max_exact) * (nb - max_exact)).astype(np.int32)
    large = np.minimum(large, nb - 1)
    return ret + np.where(n < max_exact, n, large)


def host_layout(inputs):
    x = np.asarray(inputs["x"], np.float32)
    rel_bias = np.asarray(inputs["rel_bias"], np.float32)
    s_idx = np.arange(128)[:, None]
    q_idx = np.arange(128)[None, :]
    tiles = []
    for d in range(4):
        rel = (s_idx - q_idx) - 128 * d
        bk = _t5_bucket_np(rel.astype(np.int32))
        tiles.append(np.transpose(rel_bias[bk], (2, 0, 1)))
    bias_t = np.ascontiguousarray(np.stack(tiles, 0))
    lb = np.asarray(inputs["lb_logits"], np.float32)
    lb_t = np.ascontiguousarray(lb.reshape(2, 4, 128).transpose(2, 0, 1))
    hg_t = np.ascontiguousarray(np.asarray(inputs["hgrn_norm_g"], np.float32).reshape(4, 128).T)
    common = {
        "w_in": np.ascontiguousarray(np.asarray(inputs["w_in"], np.float32)[0]),
        "w_uk": np.ascontiguousarray(np.asarray(inputs["w_uk"], np.float32)[0]),
        "w_uv": np.ascontiguousarray(np.asarray(inputs["w_uv"], np.float32)[0]),
        "kv_g": np.ascontiguousarray(np.asarray(inputs["kv_norm_g"], np.float32).reshape(1, 128)),
        "bias_t": bias_t,
        "lb_t": lb_t,
        "hg_t": hg_t,
        "w_o": np.ascontiguousarray(np.asarray(inputs["w_o"], np.float32)[0]),
        "ln_g": np.ascontiguousarray(np.asarray(inputs["ln_g"], np.float32).reshape(1, D)),
        "ln_b": np.ascontiguousarray(np.asarray(inputs["ln_b"], np.float32).reshape(1, D)),
    }
    return x, common


def kernel(**inputs):
    x, common = host_layout(inputs)
    nc = build_program(SEQ_PER_CORE)
    in_maps = []
    for c in range(NCORES):
        m = dict(common)
        m["x"] = np.ascontiguousarray(x[c * SEQ_PER_CORE:(c + 1) * SEQ_PER_CORE])
        in_maps.append(m)
    res = run_bass_kernel_spmd(nc, in_maps, core_ids=list(range(NCORES)))
    out = np.concatenate([np.asarray(r["out"], np.float32) for r in res.results], axis=0)
    return out
```

```python
import math
from contextlib import ExitStack

import numpy as np
import concourse.bass as bass
import concourse.mybir as mybir
from concourse.bass_utils import run_bass_kernel_spmd

F32 = mybir.dt.float32
BF16 = mybir.dt.bfloat16
I32 = mybir.dt.int32
ALU = mybir.AluOpType
ACTF = mybir.ActivationFunctionType
AX = mybir.AxisListType

EPOCH = 8192

S = 2048
D = 1024
NB = S // 128
KC = D // 128
NCORES = 8
SEQ_PER_CORE = 2
NI = 16
TOPK = 256
NEG = -1.0e30
ALPHA = 2.0 ** 0.25
LN_EPS = 1e-5
RMS_EPS = 1e-6

C_Q, C_CKV, C_IQ, C_IK, C_IW, C_AG, C_BQ, C_BF, C_BI, C_BG = 0, 512, 640, 1152, 1216, 1224, 1736, 2248, 2760, 3272


class Tk:
    __slots__ = ("w", "r", "psum")

    def __init__(self, psum=False):
        self.w = None
        self.r = []
        self.psum = psum


class _Rec:
    def __init__(self):
        self.call = None

    def __getattr__(self, name):
        def f(*a, **k):
            self.call = (name, a, k)
            return None
        return f


class Op:
    __slots__ = ("eng", "call", "preds", "idx", "dur", "lat", "dma", "succ", "npred", "ready", "start", "finish", "pos",
                 "waits", "sig", "dtok", "bl")

    def __init__(self):
        self.preds = []
        self.succ = []
        self.waits = []
        self.sig = 0
        self.dtok = None


_DVE_F = {"tensor_tensor_scan": 2.1, "reciprocal": 6.5, "bn_stats": 1.3, "tensor_reduce": 1.1}


class FW:
    HOP = 0.6

    def __init__(self, nc, stack, n_epoch=8, n_dma_sem=10):
        self.nc = nc
        self.engs = {"pe": nc.tensor, "act": nc.scalar, "dve": nc.vector, "pool": nc.gpsimd, "sp": nc.sync}
        self.sems = {}
        for e in ("pe", "act", "dve", "pool"):
            self.sems[e] = [stack.enter_context(nc.semaphore(f"s_{e}_{i}")) for i in range(n_epoch)]
        self.sigcnt = {e: 0 for e in self.engs}
        self.seen = {e: {} for e in self.engs}
        self.dsems = {}
        for q in ("sp", "pool", "act"):
            self.dsems[q] = [[stack.enter_context(nc.semaphore(f"d_{q}_{i}")), 0] for i in range(n_dma_sem)]
        self.dptr = {q: 0 for q in self.dsems}
        self.ops = []
        self.nops = 0
        self.sched = True

    def _edges(self, op, e, reads, writes):
        ps = op.preds
        for t in reads:
            if t.w is not None:
                ps.append(t.w)
            if t.psum:
                for d in t.r:
                    if d.eng != e:
                        ps.append(d)
        for t in writes:
            if t.w is not None:
                ps.append(t.w)
            ps.extend(t.r)
        for t in reads:
            t.r.append(op)
        for t in writes:
            t.w = op
            t.r = []

    def op(self, e, fn, reads=(), writes=(), dur=None):
        r = _Rec()
        fn(r)
        o = Op()
        o.eng = e
        o.call = r.call
        o.dma = False
        o.idx = self.nops
        self.nops += 1
        if dur is None:
            name, a, k = r.call
            out = k.get("out", a[0] if a else None)
            try:
                n = out.free_size()
            except Exception:
                n = 128
            if e == "pe":
                dur = max(n, 64) / 1800.0 + 0.03
            elif e == "act":
                dur = (n + 260) / 1200.0
            elif e == "dve":
                f = _DVE_F.get(name, 1.0)
                if k.get("accum_out") is not None:
                    f = 1.25
                dur = (n * f + 110) / 960.0
            else:
                dur = (n * 5.0 + 200) / 1200.0
        o.dur = dur
        o.lat = dur
        self._edges(o, e, reads, writes)
        self.ops.append(o)
        return o

    def dma(self, q, out, in_, reads=(), writes=(), **kw):
        o = Op()
        o.eng = q
        o.call = ("dma_start", (), dict(out=out, in_=in_, **kw))
        o.dma = True
        o.idx = self.nops
        self.nops += 1
        try:
            nbytes = out.free_size() * out.partition_size() * 4
        except Exception:
            nbytes = 1 << 18
        o.dur = 0.15 if q == "sp" else 1.0
        o.lat = 1.5 + nbytes / 90000.0
        self._edges(o, q, reads, writes)
        self.ops.append(o)
        return o

    def _schedule(self, ops):
        inreg = set(id(o) for o in ops)
        for o in ops:
            o.preds = [p for p in dict.fromkeys(o.preds) if id(p) in inreg and p is not o]
            o.succ = []
        for o in ops:
            o.npred = len(o.preds)
            o.ready = 0.0
            for p in o.preds:
                p.succ.append(o)
        if not self.sched:
            return list(ops)
        for o in reversed(ops):
            b = 0.0
            for s_ in o.succ:
                if s_.bl > b:
                    b = s_.bl
            o.bl = b + o.lat
        free = {e: 0.0 for e in self.engs}
        cand = {e: [] for e in self.engs}
        for o in ops:
            if o.npred == 0:
                cand[o.eng].append(o)
        order = []
        n = len(ops)
        SLACK = 0.25
        while len(order) < n:
            best = None
            bst = None
            for e, lst in cand.items():
                if not lst:
                    continue
                fe = free[e]
                stmin = None
                for o in lst:
                    st = o.ready if o.ready > fe else fe
                    if stmin is None or st < stmin:
                        stmin = st
                pick = None
                for o in lst:
                    st = o.ready if o.ready > fe else fe
                    if st <= stmin + SLACK and (pick is None or o.bl > pick.bl):
                        pick = o
                if bst is None or stmin < bst:
                    bst = stmin
                    best = pick
            o = best
            cand[o.eng].remove(o)
            fe = free[o.eng]
            o.start = o.ready if o.ready > fe else fe
            free[o.eng] = o.start + o.dur
            o.finish = o.start + o.lat
            order.append(o)
            for s_ in o.succ:
                t = o.finish + ((0.0 if o.eng == 'pe' else 0.3) if (s_.eng == o.eng and not o.dma) else self.HOP)
                if t > s_.ready:
                    s_.ready = t
                s_.npred -= 1
                if s_.npred == 0:
                    cand[s_.eng].append(s_)
        return order

    def flush(self):
        ops = self.ops
        self.ops = []
        if not ops:
            return
        order = self._schedule(ops)
        last = {}
        for pos, o in enumerate(order):
            o.pos = pos
            if not o.dma:
                last[o.eng] = o
        seenpos = {e: {} for e in self.engs}
        for o in order:
            e = o.eng
            for p in o.preds:
                if p.dma:
                    o.waits.append(p)
                    continue
                if p.eng == e and e == "pe":
                    continue
                if seenpos[e].get(p.eng, -1) >= p.pos:
                    continue
                seenpos[e][p.eng] = p.pos
                o.waits.append(p)
                p.sig = -1
        for e, o in last.items():
            if not o.dma:
                o.sig = -1
        for o in order:
            e = o.eng
            eng = self.engs[e]
            for p in o.waits:
                if p.dma:
                    q, i, v = p.dtok
                    if self.seen[e].get(("d", q, i), 0) >= v:
                        continue
                    self.seen[e][("d", q, i)] = v
                    eng.wait_ge(self.dsems[q][i][0], v)
                else:
                    sv = p.sig
                    if self.seen[e].get(("c", p.eng), 0) >= sv:
                        continue
                    self.seen[e][("c", p.eng)] = sv
                    ep, c = divmod(sv - 1, EPOCH)
                    eng.wait_ge(self.sems[p.eng][ep], c + 1)
            name, a, k = o.call
            if o.dma:
                q = e
                i = self.dptr[q]
                self.dptr[q] = (i + 1) % len(self.dsems[q])
                slot = self.dsems[q][i]
                if slot[1] > 0 and self.seen[q].get(("d", q, i), 0) < slot[1]:
                    self.seen[q][("d", q, i)] = slot[1]
                    eng.wait_ge(slot[0], slot[1])
                slot[1] += 16
                eng.dma_start(**k).then_inc(slot[0], 16)
                o.dtok = (q, i, slot[1])
            else:
                ins = getattr(eng, name)(*a, **k)
                if o.sig == -1:
                    self.sigcnt[e] += 1
                    o.sig = self.sigcnt[e]
                    ep, c = divmod(o.sig - 1, EPOCH)
                    ins.then_inc(self.sems[e][ep], 1)
        for e in ("pe", "act", "dve", "pool", "sp"):
            eng = self.engs[e]
            for x, o in last.items():
                if o.dma or x == e:
                    continue
                sv = o.sig
                if self.seen[e].get(("c", x), 0) >= sv:
                    continue
                self.seen[e][("c", x)] = sv
                ep, c = divmod(sv - 1, EPOCH)
                eng.wait_ge(self.sems[x][ep], c + 1)
            for q in self.dsems:
                for i, slot in enumerate(self.dsems[q]):
                    if slot[1] > 0 and self.seen[e].get(("d", q, i), 0) < slot[1]:
                        self.seen[e][("d", q, i)] = slot[1]
                        eng.wait_ge(slot[0], slot[1])

    def barrier(self):
        self.flush()


def build_program(n_seq=SEQ_PER_CORE, dbg=None, sched=True):
    _build.sched = sched
    return _build(n_seq, dbg, None)


def _build(n_seq, dbg, targets):
    dbg = dbg or {}
    nc = bass.Bass("TRN2", target_bir_lowering=False)
    dt = nc.dram_tensor
    x_d = dt("x", [n_seq, S, D], F32, kind="ExternalInput").ap()
    win_d = dt("w_in", [D, 3784], F32, kind="ExternalInput").ap()
    wuk_d = dt("w_uk", [4, 128, 128], F32, kind="ExternalInput").ap()
    wuv_d = dt("w_uv", [4, 128, 128], F32, kind="ExternalInput").ap()
    kvg_d = dt("kv_g", [1, 128], F32, kind="ExternalInput").ap()
    bias_d = dt("bias_t", [4, 4, 128, 128], F32, kind="ExternalInput").ap()
    lb_d = dt("lb_t", [128, 2, 4], F32, kind="ExternalInput").ap()
    hg_d = dt("hg_t", [128, 4], F32, kind="ExternalInput").ap()
    wo_d = dt("w_o", [D, D], F32, kind="ExternalInput").ap()
    lng_d = dt("ln_g", [1, D], F32, kind="ExternalInput").ap()
    lnb_d = dt("ln_b", [1, D], F32, kind="ExternalInput").ap()
    out_d = dt("out", [n_seq, S, D], F32, kind="ExternalOutput").ap()
    dbg_d = {}
    for name, shape in dbg.items():
        dbg_d[name] = dt("dbg_" + name, list(shape), F32, kind="ExternalOutput").ap()

    with ExitStack() as st:
        fw = FW(nc, st)
        fw.sched = getattr(_build, 'sched', True)
        uid = [0]

        def sb(name, shape, dtype, stack=st):
            uid[0] += 1
            return stack.enter_context(nc.sbuf_tensor(f"{name}_{uid[0]}", list(shape), dtype))
        out_toks = []

        def dbg_out(name, ap_sb, tk, dst=None):
            if name in dbg_d:
                out_toks.append(fw.dma("sp", dst if dst is not None else dbg_d[name], ap_sb, reads=[tk]))

        PB = [st.enter_context(nc.psum_tensor(f"pb{i}", [128, 512], F32)) for i in range(8)]
        PK = [Tk(psum=True) for _ in range(8)]

        ident_f = sb("ident_f", [128, 128], F32)
        ident_b = sb("ident_b", [128, 128], BF16)
        ones_f = sb("ones_f", [128, 128], F32)
        ones_b = sb("ones_b", [128, 128], BF16)
        adm = sb("adm", [128, 128], F32)
        negb = sb("negb", [128, 128], F32)
        caus = sb("caus", [128, 128], F32)
        pow2 = sb("pow2", [128, NI], F32)
        kvg = sb("kvg", [128, 128], F32)
        ident4 = sb("ident4", [128, 4, 128], BF16)
        lbt = sb("lbt", [128, 2, 4], F32)
        lbA = sb("lbA", [128, 4], F32)
        lbB = sb("lbB", [128, 4], F32)
        lbNB = sb("lbNB", [128, 4], F32)
        hgt = sb("hgt", [128, 4], F32)
        wuk = sb("wuk", [128, 4, 128], BF16)
        wuv = sb("wuv", [128, 4, 128], BF16)
        epsc = sb("epsc", [128, 2], F32)
        k_const = Tk()
        k_wo = Tk()

        P = fw.op
        P("pool", lambda e: e.memset(ones_f[:], 1.0), writes=[k_const])
        P("pool", lambda e: e.memset(epsc[:, 0:1], RMS_EPS), writes=[k_const])
        P("pool", lambda e: e.memset(epsc[:, 1:2], LN_EPS), writes=[k_const])
        P("pool", lambda e: e.memset(ones_b[:], 1.0), writes=[k_const])
        P("pool", lambda e: e.affine_select(ident_f[:], ones_f[:], [[-1, 128]], ALU.is_equal, 0.0, base=0, channel_multiplier=1),
          reads=[k_const], writes=[k_const])
        P("pool", lambda e: e.tensor_copy(ident_b[:], ident_f[:]), reads=[k_const], writes=[k_const])
        for i4 in range(4):
            P("pool", lambda e: e.tensor_copy(ident4[:, i4, :], ident_f[:]), reads=[k_const], writes=[k_const])
        P("pool", lambda e: e.memset(adm[:], 1.0), writes=[k_const])
        P("pool", lambda e: e.memset(adm[0:64, 64:128], 0.0), writes=[k_const])
        P("pool", lambda e: e.memset(negb[:], 0.0), writes=[k_const])
        P("pool", lambda e: e.memset(negb[0:64, 64:128], NEG), writes=[k_const])
        P("pool", lambda e: e.memset(caus[:], 1.0), writes=[k_const])
        P("pool", lambda e: e.affine_select(caus[:], caus[:], [[1, 128]], ALU.is_ge, 0.0, base=0, channel_multiplier=-1),
          reads=[k_const], writes=[k_const])
        P("pool", lambda e: e.memset(caus[0:64, 64:128], 0.0), writes=[k_const])
        for i in range(NI):
            P("pool", lambda e: e.memset(pow2[:, i:i + 1], 2.0 ** (-(i + 1))), writes=[k_const])

        fw.dma("sp", kvg[:], kvg_d.partition_broadcast(128), writes=[k_const])
        fw.dma("sp", lbt[:], lb_d, writes=[k_const])
        fw.dma("sp", hgt[:], hg_d, writes=[k_const])
        P("dve", lambda e: e.tensor_scalar(hgt[:], hgt[:], 0.5, None, ALU.mult), reads=[k_const], writes=[k_const])
        P("dve", lambda e: e.tensor_tensor(lbA[:], lbt[:, 0, :], lbt[:, 1, :], ALU.subtract), reads=[k_const], writes=[k_const])
        P("act", lambda e: e.activation(lbB[:], lbA[:], ACTF.Tanh, scale=0.5), reads=[k_const], writes=[k_const])
        P("dve", lambda e: e.tensor_scalar(lbA[:], lbB[:], 0.25, 0.75, ALU.mult, ALU.add), reads=[k_const], writes=[k_const])
        P("dve", lambda e: e.tensor_scalar(lbNB[:], lbB[:], 0.25, -0.25, ALU.mult, ALU.add), reads=[k_const], writes=[k_const])
        P("dve", lambda e: e.tensor_scalar(lbB[:], lbB[:], -0.25, 0.25, ALU.mult, ALU.add), reads=[k_const], writes=[k_const])


        xT = sb("xT", [128, KC, S], BF16)
        k_xT = [[Tk() for _ in range(2)] for _ in range(NB)]
        OT = sb("OT", [128, 8, S], BF16)
        k_OT = [[Tk() for _ in range(4)] for _ in range(8)]
        wstg = [sb(f"wstg{i}", [128, KC, 136], F32) for i in range(2)]
        wbf = [sb(f"wbf{i}", [128, KC, 136], BF16) for i in range(2)]
        k_wstg = [Tk(), Tk()]
        k_wbf = [Tk(), Tk()]
        wctr = [0]

        def load_w(col_slices):
            i = wctr[0] % 2
            wctr[0] += 1
            off = 0
            for (c0, n) in col_slices:
                fw.dma("sp" if wctr[0] % 2 == 0 else "pool", wstg[i][:, :, off:off + n],
                       win_d[:, c0:c0 + n].rearrange("(k p) c -> p k c", p=128), writes=[k_wstg[i]])
                off += n
            P("dve", lambda e: e.tensor_copy(wbf[i][:, :, 0:off], wstg[i][:, :, 0:off]), reads=[k_wstg[i]], writes=[k_wbf[i]])
            return wbf[i], k_wbf[i], off

        def xT_keys(tt):
            return [k_xT[tb][hh] for tb in range(tt * 4, tt * 4 + 4) for hh in range(2)]

        pctr = [0]

        def inproj_fm(wt, k_w, tt, banks):
            bi = banks[pctr[0] % len(banks)]
            pctr[0] += 1
            for k in range(KC):
                P("pe", lambda e: e.matmul(PB[bi][:], wt[:, k, 0:128], xT[:, k, tt * 512:(tt + 1) * 512], start=(k == 0), stop=(k == KC - 1)),
                  reads=[k_w] + xT_keys(tt), writes=[PK[bi]])
            return bi

        evc = [0]

        def ev_eng():
            evc[0] += 1
            return "act" if evc[0] % 2 == 0 else "dve"

        def copy_on(e, out, in_, reads, writes):
            if e == "act":
                return P("act", lambda g: g.activation(out, in_, ACTF.Copy), reads=reads, writes=writes)
            return P(e, lambda g: g.tensor_copy(out, in_), reads=reads, writes=writes)

        bhi = sb("bhi", [128, 16, 128], BF16)
        blo = sb("blo", [128, 16, 128], BF16)
        k_bias = Tk()
        SQ = math.sqrt(128.0)

        def phase_x(b, ph):
            xs = [sb(f"xs{i}", [128, D], F32, ph) for i in range(3)]
            k_xs = [Tk() for _ in range(3)]
            for tb in range(NB):
                i = tb % 3
                fw.dma("sp" if tb % 2 == 0 else "pool", xs[i][:], x_d[b, tb * 128:(tb + 1) * 128, :], writes=[k_xs[i]])
                for hh in range(2):
                    bi = 6 + (tb * 2 + hh) % 2
                    for kk in range(4):
                        kc = hh * 4 + kk
                        P("pe", lambda e: e.transpose(PB[bi][:, kk * 128:(kk + 1) * 128], xs[i][:, kc * 128:(kc + 1) * 128], ident_f[:]),
                          reads=[k_xs[i], k_const], writes=[PK[bi]])
                    copy_on(ev_eng(), xT[:, hh * 4:hh * 4 + 4, tb * 128:(tb + 1) * 128],
                            PB[bi][:].rearrange("p (k t) -> p k t", k=4), [PK[bi]], [k_xT[tb][hh]])

        for b in range(n_seq):
            if b == 0:
                with ExitStack() as ph:
                    stg = sb("stg0", [128, 4, 128], F32, ph)
                    k_stg = Tk()
                    for src, dstt in ((wuk_d, wuk), (wuv_d, wuv)):
                        fw.dma("sp", stg[:], src.rearrange("h a b -> a h b"), writes=[k_stg])
                        P("dve", lambda e: e.tensor_copy(dstt[:], stg[:]), reads=[k_stg], writes=[k_const])
                    btmp = sb("btmp", [128, 16, 128], F32, ph)
                    bt2 = sb("bt2", [128, 16, 128], F32, ph)
                    k_bt = Tk()
                    fw.dma("pool", btmp[:], bias_d.rearrange("d h s q -> s (d h) q"), writes=[k_bt])
                    P("dve", lambda e: e.tensor_scalar(btmp[:], btmp[:], SQ, None, ALU.mult), reads=[k_bt], writes=[k_bt])
                    P("dve", lambda e: e.tensor_copy(bhi[:], btmp[:]), reads=[k_bt], writes=[k_bias])
                    P("dve", lambda e: e.tensor_copy(bt2[:], bhi[:]), reads=[k_bias], writes=[k_bt])
                    P("dve", lambda e: e.tensor_tensor(bt2[:], btmp[:], bt2[:], ALU.subtract), reads=[k_bt], writes=[k_bt])
                    P("dve", lambda e: e.tensor_copy(blo[:], bt2[:]), reads=[k_bt], writes=[k_bias])
                    phase_x(0, ph)
                    fw.barrier()

            with ExitStack() as ph:
                qlT = sb("qlT", [128, 4, S], BF16, ph)
                iqT = sb("iqT", [128, 4, S], BF16, ph)
                ikT = sb("ikT", [128, 2, S], BF16, ph)
                cT = sb("cT", [128, S], BF16, ph)
                ctok = sb("ctok", [128, NB, 128], BF16, ph)
                wabs = sb("wabs", [128, NB, 8], F32, ph)
                wsgn = sb("wsgn", [128, NB, 8], F32, ph)
                k_ql = [[Tk() for _ in range(4)] for _ in range(4)]
                k_iq = [[Tk() for _ in range(4)] for _ in range(4)]
                k_ik = [Tk() for _ in range(4)]
                k_cT = [Tk() for _ in range(NB)]
                k_ctok = [Tk() for _ in range(NB)]
                k_w8 = [Tk() for _ in range(NB)]
                qtmp = [sb(f"qtmp{i}", [128, 512], BF16, ph) for i in range(2)]
                k_qtmp = [Tk(), Tk()]

                for h in range(4):
                    wt, k_w, _ = load_w([(C_Q + h * 128, 128)])
                    for tt in range(4):
                        bi = inproj_fm(wt, k_w, tt, (2, 3, 4, 7))
                        i = (h * 4 + tt) % 2
                        copy_on(ev_eng(), qtmp[i][:], PB[bi][:], [PK[bi]], [k_qtmp[i]])
                        bo = 5 + (h * 4 + tt) % 2
                        P("pe", lambda e: e.matmul(PB[bo][:], wuk[:, h, :], qtmp[i][:], start=True, stop=True),
                          reads=[k_qtmp[i], k_const], writes=[PK[bo]])
                        copy_on(ev_eng(), qlT[:, h, tt * 512:(tt + 1) * 512], PB[bo][:], [PK[bo]], [k_ql[h][tt]])
                for j in range(4):
                    wt, k_w, _ = load_w([(C_IQ + j * 128, 128)])
                    for tt in range(4):
                        bi = inproj_fm(wt, k_w, tt, (2, 3, 4, 7))
                        copy_on(ev_eng(), iqT[:, j, tt * 512:(tt + 1) * 512], PB[bi][:], [PK[bi]], [k_iq[j][tt]])
                wt, k_w, _ = load_w([(C_IK, 64), (C_IK, 64)])
                for tt in range(4):
                    bi = inproj_fm(wt, k_w, tt, (2, 3, 4, 7))
                    P("pool", lambda e: e.memset(ikT[64:128, 0, tt * 512:(tt + 1) * 512], 0.0), writes=[k_ik[tt]])
                    P("pool", lambda e: e.memset(ikT[0:64, 1, tt * 512:(tt + 1) * 512], 0.0), writes=[k_ik[tt]])
                    copy_on("act", ikT[0:64, 0, tt * 512:(tt + 1) * 512], PB[bi][0:64, :], [PK[bi]], [k_ik[tt]])
                    copy_on("dve", ikT[64:128, 1, tt * 512:(tt + 1) * 512], PB[bi][64:128, :], [PK[bi]], [k_ik[tt]])

                wt, k_w, _ = load_w([(C_CKV, 128), (C_IW, 8)])
                sm = sb("a2sm", [128, NB, 4], F32, ph)
                junk = sb("a2junk", [128, 128], F32, ph)
                k_sm = [Tk() for _ in range(NB)]
                k_junk = Tk()
                for tb in range(NB):
                    bi = tb % 2
                    for k in range(KC):
                        P("pe", lambda e: e.matmul(PB[bi][:, 0:136], xT[:, k, tb * 128:(tb + 1) * 128], wt[:, k, 0:136], start=(k == 0), stop=(k == KC - 1)),
                          reads=[k_w, k_xT[tb][0], k_xT[tb][1]], writes=[PK[bi]])
                    P("act", lambda e: e.activation(junk[:], PB[bi][:, 0:128], ACTF.Square, accum_out=sm[:, tb, 0:1]),
                      reads=[PK[bi]], writes=[k_junk, k_sm[tb]])
                    P("act", lambda e: e.activation(sm[:, tb, 1:2], sm[:, tb, 0:1], ACTF.Sqrt, bias=epsc[:, 0:1], scale=1.0 / 128),
                      reads=[k_sm[tb], k_const], writes=[k_sm[tb]])
                    P("dve", lambda e: e.reciprocal(sm[:, tb, 2:3], sm[:, tb, 1:2]), reads=[k_sm[tb]], writes=[k_sm[tb]])
                    P("dve", lambda e: e.scalar_tensor_tensor(ctok[:, tb, :], PB[bi][:, 0:128], sm[:, tb, 2:3], kvg[:], ALU.mult, ALU.mult),
                      reads=[PK[bi], k_sm[tb], k_const], writes=[k_ctok[tb]])
                    P("act", lambda e: e.activation(wabs[:, tb, :], PB[bi][:, 128:136], ACTF.Abs, scale=(64 ** -0.5) * (8 ** -0.5)),
                      reads=[PK[bi]], writes=[k_w8[tb]])
                    P("act", lambda e: e.activation(wsgn[:, tb, :], PB[bi][:, 128:136], ACTF.Sign),
                      reads=[PK[bi]], writes=[k_w8[tb]])
                    bo = 5 + tb % 2
                    pbv = PB[bo][:].bitcast(BF16)
                    P("pe", lambda e: e.transpose(pbv[:, 0:128], ctok[:, tb, :], ident_b[:]), reads=[k_ctok[tb], k_const], writes=[PK[bo]])
                    copy_on("act", cT[:, tb * 128:(tb + 1) * 128], pbv[:, 0:128], [PK[bo]], [k_cT[tb]])
                if b == 0 and "ctok" in dbg_d:
                    tmpf = sb("dbgtmp", [128, NB, 128], F32, ph)
                    k_t = Tk()
                    P("dve", lambda e: e.tensor_copy(tmpf[:], ctok[:]), reads=k_ctok, writes=[k_t])
                    dbg_out("ctok", tmpf[:], k_t)
                if b == 0 and "wabs" in dbg_d:
                    for tb in range(NB):
                        out_toks.append(fw.dma("sp", dbg_d["wabs"][:, tb, 0:8], wabs[:, tb, :], reads=[k_w8[tb]]))
                        out_toks.append(fw.dma("sp", dbg_d["wabs"][:, tb, 8:16], wsgn[:, tb, :], reads=[k_w8[tb]]))

                score = [[sb(f"score{i}{j}", [128, S], F32, ph) for j in range(2)] for i in range(2)]
                k_score = [[Tk(), Tk()] for _ in range(2)]
                rt = [sb(f"rt{i}", [128, 512], BF16, ph) for i in range(4)]
                k_rt = [Tk() for _ in range(4)]
                dsg = [sb(f"dsg{i}", [128, 8, 128], BF16, ph) for i in range(2)]
                k_dsg = [Tk(), Tk()]
                bsL = sb("bsL", [128, 2], F32, ph)
                bsM = sb("bsM", [128, 2], F32, ph)
                bsC = sb("bsC", [128, 2], F32, ph)
                bsG = sb("bsG", [128, 2], F32, ph)
                bsW = sb("bsW", [128, 2], F32, ph)
                bsH = sb("bsH", [128, NI, 2], F32, ph)
                bsT = sb("bsT", [128, NB], F32, ph)
                thrc = sb("thrc", [128, NB // 2, 2], F32, ph)
                sgnr = sb("sgnr", [128, 2], F32, ph)
                k_bs = Tk()
                k_bL, k_bM, k_bG, k_bW, k_bH = Tk(), Tk(), Tk(), Tk(), Tk()
                k_bC = [Tk(), Tk()]
                k_bsT = [Tk() for _ in range(NB)]
                k_thrc = Tk()
                k_bsj = [Tk(), Tk()]
                bsH2 = sb("bsH2", [128, NI], F32, ph)
                bsS = sb("bsS", [128, 2], F32, ph)
                bsK = sb("bsK", [128, NB // 2], F32, ph)
                negm = [[sb(f"negm{i}{j}", [128, S], BF16, ph) for j in range(2)] for i in range(2)]
                k_negm = [[Tk(), Tk()] for _ in range(2)]
                pt = [sb(f"pt{i}", [128, 512], BF16, ph) for i in range(3)]
                k_pt = [Tk() for _ in range(3)]
                rden = sb("rden", [128, 512], F32, ph)
                k_rden = Tk()
                P("dve", lambda e: e.memset(sgnr[:, 0:1], 1.0), writes=[k_thrc])
                P("dve", lambda e: e.memset(sgnr[:, 1:2], -1.0), writes=[k_thrc])
                for g in range(NB // 2):
                    P("dve", lambda e: e.memset(bsK[:, g:g + 1], 0.5 - float(2 * TOPK - 128 * (2 * g + 2))), writes=[k_thrc])
                    P("dve", lambda e: e.memset(thrc[:, g, 0:1], float(TOPK)), writes=[k_thrc])
                    P("dve", lambda e: e.memset(thrc[:, g, 1:2], float(2 * TOPK - 128 * (2 * g + 2))), writes=[k_thrc])
                rtc = [0]
                ptc = [0]

                def stage1(g):
                    for jj in range(2):
                        qb = 2 * g + jj
                        sc = score[g % 2][jj]
                        k_sc = k_score[g % 2][jj]
                        nk = 128 * (qb + 1)
                        ngrp = (nk + 511) // 512
                        di = qb % 2
                        P("pool", lambda e: e.tensor_tensor(dsg[di][:], ident_b[:].unsqueeze(1).to_broadcast([128, 8, 128]),
                                                            wsgn[:, qb, :].unsqueeze(2).to_broadcast([128, 8, 128]), ALU.mult),
                          reads=[k_const, k_w8[qb]], writes=[k_dsg[di]])
                        for gg in range(ngrp):
                            k0 = gg * 512
                            kn = min(512, nk - k0)
                            ba = 2 + gg % 2
                            ris = []

                            def dots(h):
                                j, half = h // 2, h % 2
                                bi = h % 2
                                p0 = half * 64
                                P("pe", lambda e: e.matmul(PB[bi][:, 0:kn], iqT[:, j, qb * 128:(qb + 1) * 128], ikT[:, half, k0:k0 + kn], start=True, stop=True),
                                  reads=[k_iq[j][qb // 4], k_ik[gg]], writes=[PK[bi]])
                                ri = rtc[0] % 4
                                rtc[0] += 1
                                ris.append(ri)
                                P("act", lambda e: e.activation(rt[ri][:, 0:kn], PB[bi][:, 0:kn], ACTF.Relu, scale=wabs[:, qb, h:h + 1]),
                                  reads=[PK[bi], k_w8[qb]], writes=[k_rt[ri]])

                            def accum(h):
                                ri = ris[h]
                                P("pe", lambda e: e.matmul(PB[ba][:, 0:kn], dsg[di][:, h, :], rt[ri][:, 0:kn], start=(h == 0), stop=(h == 7)),
                                  reads=[k_dsg[di], k_rt[ri]], writes=[PK[ba]])
                            dots(0)
                            for h in range(8):
                                if h + 1 < 8:
                                    dots(h + 1)
                                accum(h)
                                yield
                            d0 = qb * 128
                            last = (k0 + kn == nk)
                            nmain = kn - 128 if last else kn
                            if nmain > 0:
                                P("dve", lambda e: e.tensor_copy(sc[:, k0:k0 + nmain], PB[ba][:, 0:nmain]), reads=[PK[ba]], writes=[k_sc])
                            if last:
                                P("dve", lambda e: e.tensor_tensor(sc[:, d0:d0 + 128], PB[ba][:, kn - 128:kn], adm[:], ALU.mult), reads=[PK[ba], k_const], writes=[k_sc])
                                P("dve", lambda e: e.tensor_tensor(sc[:, d0:d0 + 128], sc[:, d0:d0 + 128], negb[:], ALU.add), reads=[k_sc, k_const], writes=[k_sc])

                def stage2(g):
                    qbs = (2 * g, 2 * g + 1)
                    scs_ = score[g % 2]
                    ks = k_score[g % 2]
                    nm = negm[g % 2]
                    knm = k_negm[g % 2]
                    if g == 0:
                        for jj in range(2):
                            P("dve", lambda e: e.memset(bsT[:, qbs[jj]:qbs[jj] + 1], -1.0e29), writes=[k_bsT[qbs[jj]]])
                    else:
                        nks = [128 * (q + 1) for q in qbs]
                        for jj in range(2):
                            P("dve", lambda e: e.tensor_reduce(bsL[:, jj:jj + 1], scs_[jj][:, 0:qbs[jj] * 128], AX.X, ALU.min), reads=[ks[jj]], writes=[k_bL])
                            P("dve", lambda e: e.tensor_reduce(bsW[:, jj:jj + 1], scs_[jj][:, 0:nks[jj]], AX.X, ALU.max), reads=[ks[jj]], writes=[k_bW])
                        P("dve", lambda e: e.tensor_tensor(bsW[:], bsW[:], bsL[:], ALU.subtract), reads=[k_bW, k_bL], writes=[k_bW])
                        P("dve", lambda e: e.tensor_tensor(bsW[:], bsW[:], sgnr[:], ALU.mult), reads=[k_bW, k_thrc], writes=[k_bW])
                        P("dve", lambda e: e.tensor_tensor(bsL[:], bsL[:], sgnr[:], ALU.mult), reads=[k_bL, k_thrc], writes=[k_bL])
                        P("dve", lambda e: e.tensor_tensor(bsH[:], pow2[:].unsqueeze(2).to_broadcast([128, NI, 2]),
                                                           bsW[:].unsqueeze(1).to_broadcast([128, NI, 2]), ALU.mult), reads=[k_bW, k_const], writes=[k_bH])
                        for it in range(NI):
                            P("dve", lambda e: e.tensor_tensor(bsM[:], bsL[:], bsH[:, it, :], ALU.add), reads=[k_bL, k_bH], writes=[k_bM])
                            P("dve", lambda e: e.tensor_scalar(nm[0][:, 0:nks[0]], scs_[0][:, 0:nks[0]], bsM[:, 0:1], float(TOPK - nks[1]), ALU.is_ge, ALU.add, accum_out=bsC[:, 0:1]),
                              reads=[ks[0], k_bM], writes=[knm[0], k_bC[0]])
                            P("act", lambda e: e.activation(nm[1][:, 0:nks[1]], scs_[1][:, 0:nks[1]], ACTF.Sign, bias=bsM[:, 1:2], scale=1.0, accum_out=bsC[:, 1:2]),
                              reads=[ks[1], k_bM], writes=[knm[1], k_bC[1]])
                            P("dve", lambda e: e.scalar_tensor_tensor(bsG[:], bsC[:], float(2 * TOPK - nks[1]), bsH[:, it, :], ALU.is_ge, ALU.mult), reads=k_bC + [k_bH], writes=[k_bG])
                            P("dve", lambda e: e.tensor_tensor(bsL[:], bsL[:], bsG[:], ALU.add), reads=[k_bL, k_bG], writes=[k_bL])
                            yield
                        P("dve", lambda e: e.tensor_tensor(bsT[:, qbs[0]:qbs[0] + 2], bsL[:], sgnr[:], ALU.mult), reads=[k_bL, k_thrc], writes=[k_bsT[qbs[0]], k_bsT[qbs[1]]])
                    for jj in range(2):
                        qb = qbs[jj]
                        nk = 128 * (qb + 1)
                        P("dve", lambda e: e.tensor_scalar(nm[jj][:, 0:nk], scs_[jj][:, 0:nk], bsT[:, qb:qb + 1], -30000.0, ALU.is_lt, ALU.mult),
                          reads=[ks[jj], k_bsT[qb]], writes=[knm[jj]])
                        if b == 0 and "score" in dbg_d:
                            out_toks.append(fw.dma("sp", dbg_d["score"][qb, :, 0:nk], scs_[jj][:, 0:nk], reads=[ks[jj]]))
                            out_toks.append(fw.dma("sp", dbg_d["thr"][qb, :, :], bsT[:, qb:qb + 1], reads=[k_bsT[qb]]))
                    yield

                def stage3(g):
                    for jj in range(2):
                        qb = 2 * g + jj
                        nm = negm[g % 2][jj]
                        knm = k_negm[g % 2][jj]
                        pis = {}

                        def logits(kb):
                            bl = 4 + kb % 2
                            dd = min(qb - kb, 3)
                            P("pe", lambda e: e.matmul(PB[bl][:], cT[:, kb * 128:(kb + 1) * 128], qlT[:, :, qb * 128:(qb + 1) * 128], start=True, stop=False),
                              reads=[k_cT[kb]] + [k_ql[h][qb // 4] for h in range(4)], writes=[PK[bl]])
                            P("pe", lambda e: e.matmul(PB[bl][:], ident_b[:], bhi[:, dd * 4:dd * 4 + 4, :], start=False, stop=False),
                              reads=[k_const, k_bias], writes=[PK[bl]])
                            P("pe", lambda e: e.matmul(PB[bl][:], ident_b[:], blo[:, dd * 4:dd * 4 + 4, :], start=False, stop=False),
                              reads=[k_const, k_bias], writes=[PK[bl]])
                            P("pe", lambda e: e.matmul(PB[bl][:], nm[:, kb * 128:(kb + 1) * 128], ident4[:].rearrange("p a b -> p (a b)"), start=False, stop=True),
                              reads=[knm, k_const], writes=[PK[bl]])
                            pi = ptc[0] % 3
                            ptc[0] += 1
                            pis[kb] = pi
                            P("act", lambda e: e.activation(pt[pi][:], PB[bl][:], ACTF.Exp, scale=128 ** -0.5), reads=[PK[bl]], writes=[k_pt[pi]])

                        def pv(kb):
                            pi = pis[kb]
                            P("pe", lambda e: e.matmul(PB[6][:], ctok[:, kb, :], pt[pi][:], start=(kb == 0), stop=(kb == qb)),
                              reads=[k_ctok[kb], k_pt[pi]], writes=[PK[6]])
                            P("pe", lambda e: e.matmul(PB[7][:], ones_b[:], pt[pi][:], start=(kb == 0), stop=(kb == qb)),
                              reads=[k_const, k_pt[pi]], writes=[PK[7]])
                        logits(0)
                        for kb in range(qb + 1):
                            if kb + 1 <= qb:
                                logits(kb + 1)
                            pv(kb)
                            yield
                        P("dve", lambda e: e.reciprocal(rden[:], PB[7][:]), reads=[PK[7]], writes=[k_rden])
                        tt = qb // 4
                        P("dve", lambda e: e.tensor_tensor(OT[:, 0:4, qb * 128:(qb + 1) * 128], PB[6][:].rearrange("r (h q) -> r h q", h=4),
                                                           rden[:].rearrange("r (h q) -> r h q", h=4), ALU.mult),
                          reads=[PK[6], k_rden], writes=[k_OT[h][tt] for h in range(4)])

                NG = NB // 2

                def n_units1(g):
                    return sum(((128 * (q + 1) + 511) // 512) * 8 for q in (2 * g, 2 * g + 1))

                def n_units3(g):
                    return sum(q + 1 for q in (2 * g, 2 * g + 1))

                def advance(gen, k):
                    for _ in range(k):
                        try:
                            next(gen)
                        except StopIteration:
                            return False
                    return True

                for step in range(NG + 2):
                    g1 = stage1(step) if step < NG else None
                    g2 = stage2(step - 1) if 1 <= step <= NG else None
                    g3 = stage3(step - 2) if 2 <= step else None
                    n2 = (NI + 1) if (g2 is not None and step - 1 >= 1) else 1
                    r1 = -(-n_units1(step) // n2) if g1 is not None else 0
                    r3 = -(-n_units3(step - 2) // n2) if g3 is not None else 0
                    alive = True
                    while alive:
                        alive = False
                        if g2 is not None and advance(g2, 1):
                            alive = True
                        if g1 is not None and advance(g1, r1):
                            alive = True
                        if g3 is not None and advance(g3, r3):
                            alive = True
                fw.barrier()

            with ExitStack() as ph:
                rmask = sb("rmask", [128, S], BF16, ph)
                k_rm = Tk()
                P("pool", lambda e: e.memset(rmask[:], 1.0), writes=[k_rm])
                P("pool", lambda e: e.memset(rmask[:].rearrange("p (n c) -> p n c", c=64)[:, :, 0:1], 0.0), writes=[k_rm])
                big1 = sb("big1", [128, 2 * S], F32, ph)
                big2 = sb("big2", [128, 2 * S], F32, ph)
                fb = big1[:, 0:S]
                kkb = big1[:, S:2 * S]
                ex = big2[:, 0:S]
                dS = big1[:].rearrange("k (v n) -> k v n", n=32)
                a0 = big2[:].rearrange("k (v n) -> k v n", n=32)
                k_fb = [Tk() for _ in range(4)]
                k_kkb = [Tk() for _ in range(4)]
                k_ex = [Tk() for _ in range(4)]
                k_b2b = Tk()
                k_big1 = k_fb + k_kkb
                k_big2 = k_ex + [k_b2b]
                qsf = sb("qsf", [128, S], F32, ph)
                k_qsf = [Tk() for _ in range(4)]
                th2_ = sb("th2", [128, 512], F32, ph)
                th2 = [th2_, th2_]
                k_th2_ = Tk()
                k_th2 = [k_th2_, k_th2_]
                the = sb("the", [128, 512], F32, ph)
                qse = sb("qse", [128, 512], F32, ph)
                ob = sb("ob", [128, 512], F32, ph)
                sq = sb("sq", [128, 512], F32, ph)
                rs = sq
                k_the, k_qse, k_ob, k_sq = Tk(), Tk(), Tk(), Tk()
                k_rs = k_sq
                NHB = 2
                qtl = [sb(f"qtl{i}", [128, S], BF16, ph) for i in range(NHB)]
                qhl = [sb(f"qhl{i}", [128, S], BF16, ph) for i in range(NHB)]
                ktl = [sb(f"ktl{i}", [128, S], BF16, ph) for i in range(NHB)]
                vtok = [sb(f"vtok{i}", [128, NB, 128], BF16, ph) for i in range(NHB)]
                Vb = [sb(f"Vb{i}", [128, 128, 32], BF16, ph) for i in range(NHB)]
                wgt = [sb(f"wgt{i}", [128, KC, 128], BF16, ph) for i in range(NHB)]
                ktA = sb("ktA", [128, NB, 128], BF16, ph)
                ktB = sb("ktB", [128, NB, 128], BF16, ph)
                cols = sb("hcols", [128, 4, 32], F32, ph)
                scs = [sb(f"scs{i}", [128, 128], BF16, ph) for i in range(2)]
                k_qtl = [[Tk() for _ in range(4)] for _ in range(NHB)]
                k_qhl = [Tk() for _ in range(NHB)]
                k_ktl = [Tk() for _ in range(NHB)]
                k_vtok = [[Tk() for _ in range(NB)] for _ in range(NHB)]
                k_Vb = [Tk() for _ in range(NHB)]
                k_wgt = [Tk() for _ in range(NHB)]
                k_kt = [Tk() for _ in range(NB)]
                k_cols = Tk()
                k_scs = [Tk(), Tk()]
                P("pool", lambda e: e.memset(ktA[64:128, :, :], 0.0), writes=k_kt)
                P("pool", lambda e: e.memset(ktB[0:64, :, :], 0.0), writes=k_kt)
                P("pool", lambda e: e.memset(cols[:, 3, 0:1], 0.0), writes=[k_cols])

                tht, sgt = the, qse
                k_tht, k_sgt = k_the, k_qse
                for h in range(4):
                    wt, k_w, _ = load_w([(C_AG + h * 128, 128)])
                    for tt in range(4):
                        bi = inproj_fm(wt, k_w, tt, (0, 1))
                        P("act", lambda e: e.activation(tht[:], PB[bi][:], ACTF.Tanh, scale=0.5), reads=[PK[bi]], writes=[k_tht])
                        P("dve", lambda e: e.scalar_tensor_tensor(sgt[:], tht[:], 1.0, PB[bi][:], ALU.add, ALU.mult),
                          reads=[k_tht, PK[bi]], writes=[k_sgt])
                        bo = 7
                        P("pe", lambda e: e.matmul(PB[bo][:], wuv[:, h, :], OT[:, h, tt * 512:(tt + 1) * 512], start=True, stop=True),
                          reads=[k_const, k_OT[h][tt]], writes=[PK[bo]])
                        P("dve", lambda e: e.scalar_tensor_tensor(OT[:, h, tt * 512:(tt + 1) * 512], PB[bo][:], 0.5, sgt[:], ALU.mult, ALU.mult),
                          reads=[PK[bo], k_sgt], writes=[k_OT[h][tt]])
                def prep(h):
                    hh = h % NHB
                    wt, k_w, _ = load_w([(C_BF + h * 128, 128)])
                    for tt in range(4):
                        bi = inproj_fm(wt, k_w, tt, (0, 1))
                        sl = slice(tt * 512, (tt + 1) * 512)
                        P("act", lambda e: e.activation(fb[:, sl], PB[bi][:], ACTF.Tanh, scale=0.5), reads=[PK[bi]], writes=[k_fb[tt]])
                        P("act", lambda e: e.activation(kkb[:, sl], fb[:, sl], ACTF.Identity, bias=lbB[:, h:h + 1], scale=lbNB[:, h:h + 1]),
                          reads=[k_fb[tt], k_const], writes=[k_kkb[tt]])
                        P("dve", lambda e: e.tensor_scalar(fb[:, sl], fb[:, sl], lbB[:, h:h + 1], lbA[:, h:h + 1], ALU.mult, ALU.add),
                          reads=[k_fb[tt], k_const], writes=[k_fb[tt]])
                        yield
                    P("act", lambda e: e.activation(fb, fb, ACTF.Ln), reads=k_fb, writes=k_fb)
                    P("dve", lambda e: e.tensor_tensor_scan(fb, rmask[:], fb, 0.0, ALU.mult, ALU.add), reads=k_fb + [k_rm], writes=k_fb)
                    yield
                    wt, k_w, _ = load_w([(C_BQ + h * 128, 128)])
                    for tt in range(4):
                        i = tt % 2
                        bi = inproj_fm(wt, k_w, tt, (0, 1))
                        sl = slice(tt * 512, (tt + 1) * 512)
                        P("act", lambda e: e.activation(th2[i][:], PB[bi][:], ACTF.Tanh, scale=0.5), reads=[PK[bi]], writes=[k_th2[i]])
                        P("dve", lambda e: e.scalar_tensor_tensor(qsf[:, sl], th2[i][:], 1.0, PB[bi][:], ALU.add, ALU.mult),
                          reads=[k_th2[i], PK[bi]], writes=[k_qsf[tt]])
                        yield
                    fb3 = fb.rearrange("p (n c) -> p n c", c=64)
                    cl = cols
                    kc_ = k_cols
                    P("dve", lambda e: e.tensor_copy(cl[:, 0, :], fb3[:, :, 31]), reads=k_fb, writes=[kc_])
                    P("dve", lambda e: e.tensor_copy(cl[:, 1, :], fb3[:, :, 63]), reads=k_fb, writes=[kc_])
                    P("dve", lambda e: e.tensor_tensor(cl[:, 2, 1:32], cl[:, 1, 0:31], cl[:, 0, 0:31], ALU.subtract), reads=[kc_], writes=[kc_])
                    P("dve", lambda e: e.tensor_tensor(cl[:, 2, 1:32], cl[:, 2, 1:32], cl[:, 0, 1:32], ALU.add), reads=[kc_], writes=[kc_])
                    P("act", lambda e: e.activation(cl[:, 3, 1:32], cl[:, 2, 1:32], ACTF.Exp), reads=[kc_], writes=[kc_])
                    P("dve", lambda e: e.tensor_tensor(fb3, fb3, cl[:, 0, :].unsqueeze(2).to_broadcast([128, 32, 64]), ALU.subtract),
                      reads=k_fb + [kc_], writes=k_fb)
                    yield
                    P("act", lambda e: e.activation(ex, fb, ACTF.Exp, scale=-1.0), reads=k_fb, writes=k_ex)
                    P("dve", lambda e: e.tensor_tensor(ktl[hh][:], kkb, ex, ALU.mult), reads=k_kkb + k_ex, writes=[k_ktl[hh]])
                    P("act", lambda e: e.activation(ex, fb, ACTF.Exp), reads=k_fb, writes=k_ex)
                    yield
                    wt, k_w, _ = load_w([(C_BI + h * 128, 128)])
                    for tb in range(NB):
                        bi = tb % 2
                        for k in range(KC):
                            P("pe", lambda e: e.matmul(PB[bi][:, 0:128], xT[:, k, tb * 128:(tb + 1) * 128], wt[:, k, 0:128], start=(k == 0), stop=(k == KC - 1)),
                              reads=[k_w, k_xT[tb][0], k_xT[tb][1]], writes=[PK[bi]])
                        copy_on(ev_eng(), vtok[hh][:, tb, :], PB[bi][:, 0:128], [PK[bi]], [k_vtok[hh][tb]])
                        if tb % 4 == 3:
                            yield
                    for tb in range(NB):
                        bo = 2 + tb % 2
                        pbv = PB[bo][:].bitcast(BF16)
                        P("pe", lambda e: e.transpose(pbv[:, 0:128], ktl[hh][:, tb * 128:(tb + 1) * 128], ident_b[:]), reads=[k_ktl[hh], k_const], writes=[PK[bo]])
                        copy_on("act", ktA[0:64, tb, :], pbv[0:64, 0:128], [PK[bo]], [k_kt[tb]])
                        copy_on("act", ktB[64:128, tb, :], pbv[64:128, 0:128], [PK[bo]], [k_kt[tb]])
                        if tb % 4 == 3:
                            yield
                    for tt in range(4):
                        sl = slice(tt * 512, (tt + 1) * 512)
                        P("dve", lambda e: e.scalar_tensor_tensor(qtl[hh][:, sl], qsf[:, sl], 0.5, ex[:, sl], ALU.mult, ALU.mult),
                          reads=[k_qsf[tt]] + k_ex, writes=[k_qtl[hh][tt]])
                    qt3 = qtl[hh][:].rearrange("p (n c) -> p n c", c=64)
                    qh3 = qhl[hh][:].rearrange("p (n c) -> p n c", c=64)
                    P("pool", lambda e: e.tensor_tensor(qh3[:, 1:32, :], qt3[:, 1:32, :], cl[:, 3, 1:32].unsqueeze(2).to_broadcast([128, 31, 64]), ALU.mult),
                      reads=k_qtl[hh] + [kc_], writes=[k_qhl[hh]])
                    yield
                    P("dve", lambda e: e.tensor_copy(a0, cl[:, 3, :].unsqueeze(1).to_broadcast([128, 128, 32])), reads=[kc_] + k_big2, writes=k_big2)
                    for grp in range(8):
                        bd = grp % 4
                        for c4 in range(4):
                            n = grp * 4 + c4
                            tb, half = n // 2, n % 2
                            kt_ = ktA if half == 0 else ktB
                            P("pe", lambda e: e.matmul(PB[bd][:, c4 * 128:(c4 + 1) * 128], kt_[:, tb, :], vtok[hh][:, tb, :], start=True, stop=True),
                              reads=[k_kt[tb], k_vtok[hh][tb]], writes=[PK[bd]])
                        copy_on("act", dS[:, :, grp * 4:grp * 4 + 4], PB[bd][:].rearrange("k (n v) -> k v n", n=4), [PK[bd]] + k_big1, k_big1)
                        if grp % 2 == 1:
                            yield
                    P("dve", lambda e: e.tensor_tensor_scan(big1[:], big2[:], big1[:], 0.0, ALU.mult, ALU.add), reads=k_big1 + k_big2, writes=k_big1)
                    P("act", lambda e: e.activation(Vb[hh][:], dS, ACTF.Copy), reads=k_big1, writes=[k_Vb[hh]])
                    wt, k_w, _ = load_w([(C_BG + h * 128, 128)])
                    P("dve", lambda e: e.tensor_copy(wgt[hh][:], wt[:, :, 0:128]), reads=[k_w], writes=[k_wgt[hh]])
                    yield

                def outp(h):
                    hh = h % NHB
                    for tt in range(4):
                        bacc = 5
                        for tbl in range(4):
                            tb = tt * 4 + tbl
                            tsl = slice(tb * 128, (tb + 1) * 128)
                            si = tb % 2
                            bs_ = 4
                            P("pe", lambda e: e.matmul(PB[bs_][:, 0:128], ktl[hh][:, tsl], qtl[hh][:, tsl], start=True, stop=True),
                              reads=[k_ktl[hh], k_qtl[hh][tt]], writes=[PK[bs_]])
                            P("dve", lambda e: e.tensor_tensor(scs[si][:], PB[bs_][:, 0:128], caus[:], ALU.mult), reads=[PK[bs_], k_const], writes=[k_scs[si]])
                            for half in range(2):
                                n = tb * 2 + half
                                csl = slice(n * 64, (n + 1) * 64)
                                oc = slice((tbl * 2 + half) * 64, (tbl * 2 + half + 1) * 64)
                                P("pe", lambda e: e.matmul(PB[bacc][:, oc], vtok[hh][:, tb, :], scs[si][:, half * 64:(half + 1) * 64], start=True, stop=(n == 0)),
                                  reads=[k_vtok[hh][tb], k_scs[si]], writes=[PK[bacc]])
                                if n > 0:
                                    P("pe", lambda e: e.matmul(PB[bacc][:, oc], Vb[hh][:, :, n - 1], qhl[hh][:, csl], start=False, stop=True),
                                      reads=[k_Vb[hh], k_qhl[hh]], writes=[PK[bacc]])
                            yield
                        sl = slice(tt * 512, (tt + 1) * 512)
                        P("act", lambda e: e.activation(ob[:], PB[bacc][:], ACTF.Copy), reads=[PK[bacc]], writes=[k_ob])
                        P("act", lambda e: e.activation(sq[:], PB[bacc][:], ACTF.Square), reads=[PK[bacc]], writes=[k_sq])
                        P("pe", lambda e: e.matmul(PB[6][:], ones_f[:], sq[:], start=True, stop=True), reads=[k_const, k_sq], writes=[PK[6]])
                        P("act", lambda e: e.activation(rs[:], PB[6][:], ACTF.Ln, bias=epsc[:, 0:1], scale=1.0 / 128), reads=[PK[6], k_const], writes=[k_rs])
                        P("act", lambda e: e.activation(rs[:], rs[:], ACTF.Exp, scale=-0.5), reads=[k_rs], writes=[k_rs])
                        yield
                        P("dve", lambda e: e.scalar_tensor_tensor(ob[:], ob[:], hgt[:, h:h + 1], rs[:], ALU.mult, ALU.mult),
                          reads=[k_ob, k_const, k_rs], writes=[k_ob])
                        bi = inproj_fm(wgt[hh], k_wgt[hh], tt, (7,))
                        P("act", lambda e: e.activation(the[:], PB[bi][:], ACTF.Tanh, scale=0.5), reads=[PK[bi]], writes=[k_the])
                        P("dve", lambda e: e.scalar_tensor_tensor(qse[:], the[:], 1.0, PB[bi][:], ALU.add, ALU.mult),
                          reads=[k_the, PK[bi]], writes=[k_qse])
                        P("pool", lambda e: e.tensor_tensor(OT[:, 4 + h, sl], ob[:], qse[:], ALU.mult),
                          reads=[k_ob, k_qse], writes=[k_OT[4 + h][tt]])
                        yield

                def run_pair(ga, gb):
                    alive = True
                    while alive:
                        alive = False
                        for g_ in (ga, gb):
                            if g_ is None:
                                continue
                            try:
                                next(g_)
                                alive = True
                            except StopIteration:
                                pass

                for h in range(5):
                    run_pair(prep(h) if h < 4 else None, outp(h - 1) if h >= 1 else None)
                fw.barrier()

            with ExitStack() as ph:
                if b == 0 and "OT" in dbg_d:
                    tmpf3 = sb("dbgtmp3", [128, 8, S], F32, ph)
                    k_t3 = Tk()
                    P("dve", lambda e: e.tensor_copy(tmpf3[:], OT[:]), reads=[k_OT[c][t] for c in range(8) for t in range(4)], writes=[k_t3])
                    dbg_out("OT", tmpf3[:], k_t3)
                wo = sb("wo", [128, KC, D], BF16, ph)
                k_wo = [Tk() for _ in range(4)]
                stg2 = [sb(f"stgw{i}", [128, KC, 256], F32, ph) for i in range(2)]
                k_s2 = [Tk(), Tk()]
                for j in range(4):
                    fw.dma("sp" if j % 2 == 0 else "pool", stg2[j % 2][:],
                           wo_d[:, j * 256:(j + 1) * 256].rearrange("(k p) c -> p k c", p=128), writes=[k_s2[j % 2]])
                    P("dve" if j % 2 == 0 else "act", (lambda e: e.tensor_copy(wo[:, :, j * 256:(j + 1) * 256], stg2[j % 2][:])) if j % 2 == 0 else
                      (lambda e: e.activation(wo[:, :, j * 256:(j + 1) * 256], stg2[j % 2][:], ACTF.Copy)),
                      reads=[k_s2[j % 2]], writes=[k_wo[j]])
                NR = 3
                xr = [sb(f"xr{i}", [128, D], F32, ph) for i in range(NR)]
                lng = sb("lng", [128, D], F32, ph)
                lnb = sb("lnb", [128, D], F32, ph)
                k_ln = Tk()
                fw.dma("sp", lng[:], lng_d.partition_broadcast(128), writes=[k_ln])
                fw.dma("sp", lnb[:], lnb_d.partition_broadcast(128), writes=[k_ln])
                zt = [sb(f"zt{i}", [128, D], F32, ph) for i in range(NR)]
                zn = [sb(f"zn{i}", [128, D], F32, ph) for i in range(NR)]
                st6 = [sb(f"st6{i}", [128, 2, 6], F32, ph) for i in range(NR)]
                mv = [sb(f"mv{i}", [128, 4], F32, ph) for i in range(NR)]
                k_xr = [Tk() for _ in range(NR)]
                k_zt = [Tk() for _ in range(NR)]
                k_zn = [Tk() for _ in range(NR)]
                k_st = [Tk() for _ in range(NR)]
                for tb0 in range(2):
                    fw.dma("sp", xr[tb0 % NR][:], x_d[b, tb0 * 128:(tb0 + 1) * 128, :], writes=[k_xr[tb0 % NR]])
                for tb in range(NB):
                    i = tb % NR
                    tt = tb // 4
                    if tb + 2 < NB:
                        fw.dma("sp", xr[(tb + 2) % NR][:], x_d[b, (tb + 2) * 128:(tb + 3) * 128, :], writes=[k_xr[(tb + 2) % NR]])
                    for hh in range(2):
                        bi = (tb % 2) * 2 + hh
                        for kc in range(8):
                            P("pe", lambda e: e.matmul(PB[bi][:], OT[:, kc, tb * 128:(tb + 1) * 128], wo[:, kc, hh * 512:(hh + 1) * 512], start=(kc == 0), stop=(kc == 7)),
                              reads=[k_OT[kc][tt], k_wo[2 * hh], k_wo[2 * hh + 1]], writes=[PK[bi]])
                        P("dve", lambda e: e.scalar_tensor_tensor(zt[i][:, hh * 512:(hh + 1) * 512], xr[i][:, hh * 512:(hh + 1) * 512], ALPHA, PB[bi][:], ALU.mult, ALU.add),
                          reads=[k_xr[i], PK[bi]], writes=[k_zt[i]])
                        P("dve", lambda e: e.bn_stats(st6[i][:, hh, :], zt[i][:, hh * 512:(hh + 1) * 512]), reads=[k_zt[i]], writes=[k_st[i]])
                    P("dve", lambda e: e.bn_aggr(mv[i][:, 0:2], st6[i][:]), reads=[k_st[i]], writes=[k_st[i]])
                    P("act", lambda e: e.activation(mv[i][:, 2:3], mv[i][:, 1:2], ACTF.Sqrt, bias=epsc[:, 1:2], scale=1.0), reads=[k_st[i], k_const], writes=[k_st[i]])
                    P("dve", lambda e: e.reciprocal(mv[i][:, 2:3], mv[i][:, 2:3]), reads=[k_st[i]], writes=[k_st[i]])
                    P("dve", lambda e: e.scalar_tensor_tensor(mv[i][:, 3:4], mv[i][:, 0:1], -1.0, mv[i][:, 2:3], ALU.mult, ALU.mult), reads=[k_st[i]], writes=[k_st[i]])
                    P("act", lambda e: e.activation(zn[i][:], zt[i][:], ACTF.Identity, bias=mv[i][:, 3:4], scale=mv[i][:, 2:3]), reads=[k_zt[i], k_st[i]], writes=[k_zn[i]])
                    P("dve", lambda e: e.tensor_tensor(zn[i][:], zn[i][:], lng[:], ALU.mult), reads=[k_zn[i], k_ln], writes=[k_zn[i]])
                    P("pool", lambda e: e.tensor_tensor(zn[i][:], zn[i][:], lnb[:], ALU.add), reads=[k_zn[i], k_ln], writes=[k_zn[i]])
                    out_toks.append(fw.dma("pool", out_d[b, tb * 128:(tb + 1) * 128, :], zn[i][:], reads=[k_zn[i]]))
                if b + 1 < n_seq:
                    phase_x(b + 1, ph)
                fw.barrier()
        fw.flush()
        _build.last_counts = (fw.nops, dict(fw.sigcnt))
    return nc


def _t5_bucket_np(rel):
    nb = 16
    max_exact = 8
    ret = np.where(rel > 0, nb, 0).astype(np.int32)
    n = np.abs(rel)
    nf = np.maximum(n, 1).astype(np.float32)
    large = max_exact + (np.log(nf / max_exact) / math.log(256 / max_exact) * (nb - max_exact)).astype(np.int32)
    large = np.minimum(large, nb - 1)
    return ret + np.where(n < max_exact, n, large)


def host_layout(inputs):
    x = np.asarray(inputs["x"], np.float32)
    rel_bias = np.asarray(inputs["rel_bias"], np.float32)
    s_idx = np.arange(128)[:, None]
    q_idx = np.arange(128)[None, :]
    tiles = []
    for d in range(4):
        rel = (s_idx - q_idx) - 128 * d
        bk = _t5_bucket_np(rel.astype(np.int32))
        tiles.append(np.transpose(rel_bias[bk], (2, 0, 1)))
    bias_t = np.ascontiguousarray(np.stack(tiles, 0))
    lb = np.asarray(inputs["lb_logits"], np.float32)
    lb_t = np.ascontiguousarray(lb.reshape(2, 4, 128).transpose(2, 0, 1))
    hg_t = np.ascontiguousarray(np.asarray(inputs["hgrn_norm_g"], np.float32).reshape(4, 128).T)
    common = {
        "w_in": np.ascontiguousarray(np.asarray(inputs["w_in"], np.float32)[0]),
        "w_uk": np.ascontiguousarray(np.asarray(inputs["w_uk"], np.float32)[0]),
        "w_uv": np.ascontiguousarray(np.asarray(inputs["w_uv"], np.float32)[0]),
        "kv_g": np.ascontiguousarray(np.asarray(inputs["kv_norm_g"], np.float32).reshape(1, 128)),
        "bias_t": bias_t,
        "lb_t": lb_t,
        "hg_t": hg_t,
        "w_o": np.ascontiguousarray(np.asarray(inputs["w_o"], np.float32)[0]),
        "ln_g": np.ascontiguousarray(np.asarray(inputs["ln_g"], np.float32).reshape(1, D)),
        "ln_b": np.ascontiguousarray(np.asarray(inputs["ln_b"], np.float32).reshape(1, D)),
    }
    return x, common


def kernel(**inputs):
    x, common = host_layout(inputs)
    nc = build_program(SEQ_PER_CORE)
    in_maps = []
    for c in range(NCORES):
        m = dict(common)
        m["x"] = np.ascontiguousarray(x[c * SEQ_PER_CORE:(c + 1) * SEQ_PER_CORE])
        in_maps.append(m)
    res = run_bass_kernel_spmd(nc, in_maps, core_ids=list(range(NCORES)))
    out = np.concatenate([np.asarray(r["out"], np.float32) for r in res.results], axis=0)
    return out
```

```python
import math
from contextlib import ExitStack

import numpy as np
import concourse.bass as bass
import concourse.mybir as mybir
from concourse.bass_utils import run_bass_kernel_spmd

F32 = mybir.dt.float32
BF16 = mybir.dt.bfloat16
I32 = mybir.dt.int32
ALU = mybir.AluOpType
ACTF = mybir.ActivationFunctionType
AX = mybir.AxisListType

EPOCH = 8192

S = 2048
D = 1024
NB = S // 128
KC = D // 128
NCORES = 8
SEQ_PER_CORE = 2
NI = 16
TOPK = 256
NEG = -1.0e30
ALPHA = 2.0 ** 0.25
LN_EPS = 1e-5
RMS_EPS = 1e-6

C_Q, C_CKV, C_IQ, C_IK, C_IW, C_AG, C_BQ, C_BF, C_BI, C_BG = 0, 512, 640, 1152, 1216, 1224, 1736, 2248, 2760, 3272


class Tk:
    __slots__ = ("w", "r", "psum")

    def __init__(self, psum=False):
        self.w = None
        self.r = []
        self.psum = psum


class _Rec:
    def __init__(self):
        self.call = None

    def __getattr__(self, name):
        def f(*a, **k):
            self.call = (name, a, k)
            return None
        return f


class Op:
    __slots__ = ("eng", "call", "preds", "idx", "dur", "lat", "dma", "succ", "npred", "ready", "start", "finish", "pos",
                 "waits", "sig", "dtok", "bl")

    def __init__(self):
        self.preds = []
        self.succ = []
        self.waits = []
        self.sig = 0
        self.dtok = None


_DVE_F = {"tensor_tensor_scan": 2.1, "reciprocal": 6.5, "bn_stats": 1.3, "tensor_reduce": 1.1}


class FW:
    HOP = 0.6

    def __init__(self, nc, stack, n_epoch=8, n_dma_sem=10):
        self.nc = nc
        self.engs = {"pe": nc.tensor, "act": nc.scalar, "dve": nc.vector, "pool": nc.gpsimd, "sp": nc.sync}
        self.sems = {}
        for e in ("pe", "act", "dve", "pool"):
            self.sems[e] = [stack.enter_context(nc.semaphore(f"s_{e}_{i}")) for i in range(n_epoch)]
        self.sigcnt = {e: 0 for e in self.engs}
        self.seen = {e: {} for e in self.engs}
        self.dsems = {}
        for q in ("sp", "pool", "act"):
            self.dsems[q] = [[stack.enter_context(nc.semaphore(f"d_{q}_{i}")), 0] for i in range(n_dma_sem)]
        self.dptr = {q: 0 for q in self.dsems}
        self.ops = []
        self.nops = 0
        self.sched = True

    def _edges(self, op, e, reads, writes):
        ps = op.preds
        for t in reads:
            if t.w is not None:
                ps.append(t.w)
            if t.psum:
                for d in t.r:
                    if d.eng != e:
                        ps.append(d)
        for t in writes:
            if t.w is not None:
                ps.append(t.w)
            ps.extend(t.r)
        for t in reads:
            t.r.append(op)
        for t in writes:
            t.w = op
            t.r = []

    def op(self, e, fn, reads=(), writes=(), dur=None):
        r = _Rec()
        fn(r)
        o = Op()
        o.eng = e
        o.call = r.call
        o.dma = False
        o.idx = self.nops
        self.nops += 1
        if dur is None:
            name, a, k = r.call
            out = k.get("out", a[0] if a else None)
            try:
                n = out.free_size()
            except Exception:
                n = 128
            if e == "pe":
                dur = max(n, 64) / 1800.0 + 0.03
            elif e == "act":
                dur = (n + 260) / 1200.0
            elif e == "dve":
                f = _DVE_F.get(name, 1.0)
                if k.get("accum_out") is not None:
                    f = 1.25
                dur = (n * f + 110) / 960.0
            else:
                dur = (n * 5.0 + 200) / 1200.0
        o.dur = dur
        o.lat = dur
        self._edges(o, e, reads, writes)
        self.ops.append(o)
        return o

    def dma(self, q, out, in_, reads=(), writes=(), **kw):
        o = Op()
        o.eng = q
        o.call = ("dma_start", (), dict(out=out, in_=in_, **kw))
        o.dma = True
        o.idx = self.nops
        self.nops += 1
        try:
            nbytes = out.free_size() * out.partition_size() * 4
        except Exception:
            nbytes = 1 << 18
        o.dur = 0.15 if q == "sp" else 1.0
        o.lat = 1.5 + nbytes / 90000.0
        self._edges(o, q, reads, writes)
        self.ops.append(o)
        return o

    def _schedule(self, ops):
        inreg = set(id(o) for o in ops)
        for o in ops:
            o.preds = [p for p in dict.fromkeys(o.preds) if id(p) in inreg and p is not o]
            o.succ = []
        for o in ops:
            o.npred = len(o.preds)
            o.ready = 0.0
            for p in o.preds:
                p.succ.append(o)
        if not self.sched:
            return list(ops)
        for o in reversed(ops):
            b = 0.0
            for s_ in o.succ:
                if s_.bl > b:
                    b = s_.bl
            o.bl = b + o.lat
        free = {e: 0.0 for e in self.engs}
        cand = {e: [] for e in self.engs}
        for o in ops:
            if o.npred == 0:
                cand[o.eng].append(o)
        order = []
        n = len(ops)
        SLACK = 0.25
        while len(order) < n:
            best = None
            bst = None
            for e, lst in cand.items():
                if not lst:
                    continue
                fe = free[e]
                stmin = None
                for o in lst:
                    st = o.ready if o.ready > fe else fe
                    if stmin is None or st < stmin:
                        stmin = st
                pick = None
                for o in lst:
                    st = o.ready if o.ready > fe else fe
                    if st <= stmin + SLACK and (pick is None or o.bl > pick.bl):
                        pick = o
                if bst is None or stmin < bst:
                    bst = stmin
                    best = pick
            o = best
            cand[o.eng].remove(o)
            fe = free[o.eng]
            o.start = o.ready if o.ready > fe else fe
            free[o.eng] = o.start + o.dur
            o.finish = o.start + o.lat
            order.append(o)
            for s_ in o.succ:
                t = o.finish + ((0.0 if o.eng == 'pe' else 0.3) if (s_.eng == o.eng and not o.dma) else self.HOP)
                if t > s_.ready:
                    s_.ready = t
                s_.npred -= 1
                if s_.npred == 0:
                    cand[s_.eng].append(s_)
        return order

    def flush(self):
        ops = self.ops
        self.ops = []
        if not ops:
            return
        order = self._schedule(ops)
        last = {}
        for pos, o in enumerate(order):
            o.pos = pos
            if not o.dma:
                last[o.eng] = o
        seenpos = {e: {} for e in self.engs}
        for o in order:
            e = o.eng
            for p in o.preds:
                if p.dma:
                    o.waits.append(p)
                    continue
                if p.eng == e and e == "pe":
                    continue
                if seenpos[e].get(p.eng, -1) >= p.pos:
                    continue
                seenpos[e][p.eng] = p.pos
                o.waits.append(p)
                p.sig = -1
        for e, o in last.items():
            if not o.dma:
                o.sig = -1
        for o in order:
            e = o.eng
            eng = self.engs[e]
            for p in o.waits:
                if p.dma:
                    q, i, v = p.dtok
                    if self.seen[e].get(("d", q, i), 0) >= v:
                        continue
                    self.seen[e][("d", q, i)] = v
                    eng.wait_ge(self.dsems[q][i][0], v)
                else:
                    sv = p.sig
                    if self.seen[e].get(("c", p.eng), 0) >= sv:
                        continue
                    self.seen[e][("c", p.eng)] = sv
                    ep, c = divmod(sv - 1, EPOCH)
                    eng.wait_ge(self.sems[p.eng][ep], c + 1)
            name, a, k = o.call
            if o.dma:
                q = e
                i = self.dptr[q]
                self.dptr[q] = (i + 1) % len(self.dsems[q])
                slot = self.dsems[q][i]
                if slot[1] > 0 and self.seen[q].get(("d", q, i), 0) < slot[1]:
                    self.seen[q][("d", q, i)] = slot[1]
                    eng.wait_ge(slot[0], slot[1])
                slot[1] += 16
                eng.dma_start(**k).then_inc(slot[0], 16)
                o.dtok = (q, i, slot[1])
            else:
                ins = getattr(eng, name)(*a, **k)
                if o.sig == -1:
                    self.sigcnt[e] += 1
                    o.sig = self.sigcnt[e]
                    ep, c = divmod(o.sig - 1, EPOCH)
                    ins.then_inc(self.sems[e][ep], 1)
        for e in ("pe", "act", "dve", "pool", "sp"):
            eng = self.engs[e]
            for x, o in last.items():
                if o.dma or x == e:
                    continue
                sv = o.sig
                if self.seen[e].get(("c", x), 0) >= sv:
                    continue
                self.seen[e][("c", x)] = sv
                ep, c = divmod(sv - 1, EPOCH)
                eng.wait_ge(self.sems[x][ep], c + 1)
            for q in self.dsems:
                for i, slot in enumerate(self.dsems[q]):
                    if slot[1] > 0 and self.seen[e].get(("d", q, i), 0) < slot[1]:
                        self.seen[e][("d", q, i)] = slot[1]
                        eng.wait_ge(slot[0], slot[1])

    def barrier(self):
        self.flush()


def build_program(n_seq=SEQ_PER_CORE, dbg=None, sched=True):
    _build.sched = sched
    return _build(n_seq, dbg, None)


def _build(n_seq, dbg, targets):
    dbg = dbg or {}
    nc = bass.Bass("TRN2", target_bir_lowering=False)
    dt = nc.dram_tensor
    x_d = dt("x", [n_seq, S, D], F32, kind="ExternalInput").ap()
    win_d = dt("w_in", [D, 3784], F32, kind="ExternalInput").ap()
    wuk_d = dt("w_uk", [4, 128, 128], F32, kind="ExternalInput").ap()
    wuv_d = dt("w_uv", [4, 128, 128], F32, kind="ExternalInput").ap()
    kvg_d = dt("kv_g", [1, 128], F32, kind="ExternalInput").ap()
    bias_d = dt("bias_t", [4, 4, 128, 128], F32, kind="ExternalInput").ap()
    lb_d = dt("lb_t", [128, 2, 4], F32, kind="ExternalInput").ap()
    hg_d = dt("hg_t", [128, 4], F32, kind="ExternalInput").ap()
    wo_d = dt("w_o", [D, D], F32, kind="ExternalInput").ap()
    lng_d = dt("ln_g", [1, D], F32, kind="ExternalInput").ap()
    lnb_d = dt("ln_b", [1, D], F32, kind="ExternalInput").ap()
    out_d = dt("out", [n_seq, S, D], F32, kind="ExternalOutput").ap()
    dbg_d = {}
    for name, shape in dbg.items():
        dbg_d[name] = dt("dbg_" + name, list(shape), F32, kind="ExternalOutput").ap()

    with ExitStack() as st:
        fw = FW(nc, st)
        fw.sched = getattr(_build, 'sched', True)
        uid = [0]

        def sb(name, shape, dtype, stack=st):
            uid[0] += 1
            return stack.enter_context(nc.sbuf_tensor(f"{name}_{uid[0]}", list(shape), dtype))
        out_toks = []

        def dbg_out(name, ap_sb, tk, dst=None):
            if name in dbg_d:
                out_toks.append(fw.dma("sp", dst if dst is not None else dbg_d[name], ap_sb, reads=[tk]))

        PB = [st.enter_context(nc.psum_tensor(f"pb{i}", [128, 512], F32)) for i in range(8)]
        PK = [Tk(psum=True) for _ in range(8)]

        ident_f = sb("ident_f", [128, 128], F32)
        ident_b = sb("ident_b", [128, 128], BF16)
        ones_f = sb("ones_f", [128, 128], F32)
        ones_b = sb("ones_b", [128, 128], BF16)
        adm = sb("adm", [128, 128], F32)
        negb = sb("negb", [128, 128], F32)
        caus = sb("caus", [128, 128], F32)
        pow2 = sb("pow2", [128, NI], F32)
        kvg = sb("kvg", [128, 128], F32)
        ident4 = sb("ident4", [128, 4, 128], BF16)
        lbt = sb("lbt", [128, 2, 4], F32)
        lbA = sb("lbA", [128, 4], F32)
        lbB = sb("lbB", [128, 4], F32)
        lbNB = sb("lbNB", [128, 4], F32)
        hgt = sb("hgt", [128, 4], F32)
        wuk = sb("wuk", [128, 4, 128], BF16)
        wuv = sb("wuv", [128, 4, 128], BF16)
        epsc = sb("epsc", [128, 2], F32)
        k_const = Tk()
        k_wo = Tk()

        P = fw.op
        P("pool", lambda e: e.memset(ones_f[:], 1.0), writes=[k_const])
        P("pool", lambda e: e.memset(epsc[:, 0:1], RMS_EPS), writes=[k_const])
        P("pool", lambda e: e.memset(epsc[:, 1:2], LN_EPS), writes=[k_const])
        P("pool", lambda e: e.memset(ones_b[:], 1.0), writes=[k_const])
        P("pool", lambda e: e.affine_select(ident_f[:], ones_f[:], [[-1, 128]], ALU.is_equal, 0.0, base=0, channel_multiplier=1),
          reads=[k_const], writes=[k_const])
        P("pool", lambda e: e.tensor_copy(ident_b[:], ident_f[:]), reads=[k_const], writes=[k_const])
        for i4 in range(4):
            P("pool", lambda e: e.tensor_copy(ident4[:, i4, :], ident_f[:]), reads=[k_const], writes=[k_const])
        P("pool", lambda e: e.memset(adm[:], 1.0), writes=[k_const])
        P("pool", lambda e: e.memset(adm[0:64, 64:128], 0.0), writes=[k_const])
        P("pool", lambda e: e.memset(negb[:], 0.0), writes=[k_const])
        P("pool", lambda e: e.memset(negb[0:64, 64:128], NEG), writes=[k_const])
        P("pool", lambda e: e.memset(caus[:], 1.0), writes=[k_const])
        P("pool", lambda e: e.affine_select(caus[:], caus[:], [[1, 128]], ALU.is_ge, 0.0, base=0, channel_multiplier=-1),
          reads=[k_const], writes=[k_const])
        P("pool", lambda e: e.memset(caus[0:64, 64:128], 0.0), writes=[k_const])
        for i in range(NI):
            P("pool", lambda e: e.memset(pow2[:, i:i + 1], 2.0 ** (-(i + 1))), writes=[k_const])

        fw.dma("sp", kvg[:], kvg_d.partition_broadcast(128), writes=[k_const])
        fw.dma("sp", lbt[:], lb_d, writes=[k_const])
        fw.dma("sp", hgt[:], hg_d, writes=[k_const])
        P("dve", lambda e: e.tensor_scalar(hgt[:], hgt[:], 0.5, None, ALU.mult), reads=[k_const], writes=[k_const])
        P("dve", lambda e: e.tensor_tensor(lbA[:], lbt[:, 0, :], lbt[:, 1, :], ALU.subtract), reads=[k_const], writes=[k_const])
        P("act", lambda e: e.activation(lbB[:], lbA[:], ACTF.Tanh, scale=0.5), reads=[k_const], writes=[k_const])
        P("dve", lambda e: e.tensor_scalar(lbA[:], lbB[:], 0.25, 0.75, ALU.mult, ALU.add), reads=[k_const], writes=[k_const])
        P("dve", lambda e: e.tensor_scalar(lbNB[:], lbB[:], 0.25, -0.25, ALU.mult, ALU.add), reads=[k_const], writes=[k_const])
        P("dve", lambda e: e.tensor_scalar(lbB[:], lbB[:], -0.25, 0.25, ALU.mult, ALU.add), reads=[k_const], writes=[k_const])


        xT = sb("xT", [128, KC, S], BF16)
        k_xT = [[Tk() for _ in range(2)] for _ in range(NB)]
        OT = sb("OT", [128, 8, S], BF16)
        k_OT = [[Tk() for _ in range(4)] for _ in range(8)]
        wstg = [sb(f"wstg{i}", [128, KC, 136], F32) for i in range(2)]
        wbf = [sb(f"wbf{i}", [128, KC, 136], BF16) for i in range(2)]
        k_wstg = [Tk(), Tk()]
        k_wbf = [Tk(), Tk()]
        wctr = [0]

        def load_w(col_slices):
            i = wctr[0] % 2
            wctr[0] += 1
            off = 0
            for (c0, n) in col_slices:
                fw.dma("sp" if wctr[0] % 2 == 0 else "pool", wstg[i][:, :, off:off + n],
                       win_d[:, c0:c0 + n].rearrange("(k p) c -> p k c", p=128), writes=[k_wstg[i]])
                off += n
            P("dve", lambda e: e.tensor_copy(wbf[i][:, :, 0:off], wstg[i][:, :, 0:off]), reads=[k_wstg[i]], writes=[k_wbf[i]])
            return wbf[i], k_wbf[i], off

        def xT_keys(tt):
            return [k_xT[tb][hh] for tb in range(tt * 4, tt * 4 + 4) for hh in range(2)]

        pctr = [0]

        def inproj_fm(wt, k_w, tt, banks):
            bi = banks[pctr[0] % len(banks)]
            pctr[0] += 1
            for k in range(KC):
                P("pe", lambda e: e.matmul(PB[bi][:], wt[:, k, 0:128], xT[:, k, tt * 512:(tt + 1) * 512], start=(k == 0), stop=(k == KC - 1)),
                  reads=[k_w] + xT_keys(tt), writes=[PK[bi]])
            return bi

        evc = [0]

        def ev_eng():
            evc[0] += 1
            return "act" if evc[0] % 2 == 0 else "dve"

        def copy_on(e, out, in_, reads, writes):
            if e == "act":
                return P("act", lambda g: g.activation(out, in_, ACTF.Copy), reads=reads, writes=writes)
            return P(e, lambda g: g.tensor_copy(out, in_), reads=reads, writes=writes)

        bhi = sb("bhi", [128, 16, 128], BF16)
        blo = sb("blo", [128, 16, 128], BF16)
        k_bias = Tk()
        SQ = math.sqrt(128.0)

        def phase_x(b, ph):
            xs = [sb(f"xs{i}", [128, D], F32, ph) for i in range(3)]
            k_xs = [Tk() for _ in range(3)]
            for tb in range(NB):
                i = tb % 3
                fw.dma("sp" if tb % 2 == 0 else "pool", xs[i][:], x_d[b, tb * 128:(tb + 1) * 128, :], writes=[k_xs[i]])
                for hh in range(2):
                    bi = 6 + (tb * 2 + hh) % 2
                    for kk in range(4):
                        kc = hh * 4 + kk
                        P("pe", lambda e: e.transpose(PB[bi][:, kk * 128:(kk + 1) * 128], xs[i][:, kc * 128:(kc + 1) * 128], ident_f[:]),
                          reads=[k_xs[i], k_const], writes=[PK[bi]])
                    copy_on(ev_eng(), xT[:, hh * 4:hh * 4 + 4, tb * 128:(tb + 1) * 128],
                            PB[bi][:].rearrange("p (k t) -> p k t", k=4), [PK[bi]], [k_xT[tb][hh]])

        for b in range(n_seq):
            if b == 0:
                with ExitStack() as ph:
                    stg = sb("stg0", [128, 4, 128], F32, ph)
                    k_stg = Tk()
                    for src, dstt in ((wuk_d, wuk), (wuv_d, wuv)):
                        fw.dma("sp", stg[:], src.rearrange("h a b -> a h b"), writes=[k_stg])
                        P("dve", lambda e: e.tensor_copy(dstt[:], stg[:]), reads=[k_stg], writes=[k_const])
                    btmp = sb("btmp", [128, 16, 128], F32, ph)
                    bt2 = sb("bt2", [128, 16, 128], F32, ph)
                    k_bt = Tk()
                    fw.dma("pool", btmp[:], bias_d.rearrange("d h s q -> s (d h) q"), writes=[k_bt])
                    P("dve", lambda e: e.tensor_scalar(btmp[:], btmp[:], SQ, None, ALU.mult), reads=[k_bt], writes=[k_bt])
                    P("dve", lambda e: e.tensor_copy(bhi[:], btmp[:]), reads=[k_bt], writes=[k_bias])
                    P("dve", lambda e: e.tensor_copy(bt2[:], bhi[:]), reads=[k_bias], writes=[k_bt])
                    P("dve", lambda e: e.tensor_tensor(bt2[:], btmp[:], bt2[:], ALU.subtract), reads=[k_bt], writes=[k_bt])
                    P("dve", lambda e: e.tensor_copy(blo[:], bt2[:]), reads=[k_bt], writes=[k_bias])
                    phase_x(0, ph)
                    fw.barrier()

            with ExitStack() as ph:
                qlT = sb("qlT", [128, 4, S], BF16, ph)
                iqT = sb("iqT", [128, 4, S], BF16, ph)
                ikT = sb("ikT", [128, 2, S], BF16, ph)
                cT = sb("cT", [128, S], BF16, ph)
                ctok = sb("ctok", [128, NB, 128], BF16, ph)
                wabs = sb("wabs", [128, NB, 8], F32, ph)
                wsgn = sb("wsgn", [128, NB, 8], F32, ph)
                k_ql = [[Tk() for _ in range(4)] for _ in range(4)]
                k_iq = [[Tk() for _ in range(4)] for _ in range(4)]
                k_ik = [Tk() for _ in range(4)]
                k_cT = [Tk() for _ in range(NB)]
                k_ctok = [Tk() for _ in range(NB)]
                k_w8 = [Tk() for _ in range(NB)]
                qtmp = [sb(f"qtmp{i}", [128, 512], BF16, ph) for i in range(2)]
                k_qtmp = [Tk(), Tk()]

                for h in range(4):
                    wt, k_w, _ = load_w([(C_Q + h * 128, 128)])
                    for tt in range(4):
                        bi = inproj_fm(wt, k_w, tt, (2, 3, 4))
                        i = (h * 4 + tt) % 2
                        copy_on(ev_eng(), qtmp[i][:], PB[bi][:], [PK[bi]], [k_qtmp[i]])
                        bo = 5 + (h * 4 + tt) % 2
                        P("pe", lambda e: e.matmul(PB[bo][:], wuk[:, h, :], qtmp[i][:], start=True, stop=True),
                          reads=[k_qtmp[i], k_const], writes=[PK[bo]])
                        copy_on(ev_eng(), qlT[:, h, tt * 512:(tt + 1) * 512], PB[bo][:], [PK[bo]], [k_ql[h][tt]])
                for j in range(4):
                    wt, k_w, _ = load_w([(C_IQ + j * 128, 128)])
                    for tt in range(4):
                        bi = inproj_fm(wt, k_w, tt, (2, 3, 4))
                        copy_on(ev_eng(), iqT[:, j, tt * 512:(tt + 1) * 512], PB[bi][:], [PK[bi]], [k_iq[j][tt]])
                wt, k_w, _ = load_w([(C_IK, 64), (C_IK, 64)])
                for tt in range(4):
                    bi = inproj_fm(wt, k_w, tt, (2, 3, 4))
                    P("pool", lambda e: e.memset(ikT[64:128, 0, tt * 512:(tt + 1) * 512], 0.0), writes=[k_ik[tt]])
                    P("pool", lambda e: e.memset(ikT[0:64, 1, tt * 512:(tt + 1) * 512], 0.0), writes=[k_ik[tt]])
                    copy_on("act", ikT[0:64, 0, tt * 512:(tt + 1) * 512], PB[bi][0:64, :], [PK[bi]], [k_ik[tt]])
                    copy_on("dve", ikT[64:128, 1, tt * 512:(tt + 1) * 512], PB[bi][64:128, :], [PK[bi]], [k_ik[tt]])

                wt, k_w, _ = load_w([(C_CKV, 128), (C_IW, 8)])
                sm = sb("a2sm", [128, NB, 4], F32, ph)
                junk = sb("a2junk", [128, 128], F32, ph)
                k_sm = [Tk() for _ in range(NB)]
                k_junk = Tk()
                for tb in range(NB):
                    bi = tb % 2
                    for k in range(KC):
                        P("pe", lambda e: e.matmul(PB[bi][:, 0:136], xT[:, k, tb * 128:(tb + 1) * 128], wt[:, k, 0:136], start=(k == 0), stop=(k == KC - 1)),
                          reads=[k_w, k_xT[tb][0], k_xT[tb][1]], writes=[PK[bi]])
                    P("act", lambda e: e.activation(junk[:], PB[bi][:, 0:128], ACTF.Square, accum_out=sm[:, tb, 0:1]),
                      reads=[PK[bi]], writes=[k_junk, k_sm[tb]])
                    P("act", lambda e: e.activation(sm[:, tb, 1:2], sm[:, tb, 0:1], ACTF.Sqrt, bias=epsc[:, 0:1], scale=1.0 / 128),
                      reads=[k_sm[tb], k_const], writes=[k_sm[tb]])
                    P("dve", lambda e: e.reciprocal(sm[:, tb, 2:3], sm[:, tb, 1:2]), reads=[k_sm[tb]], writes=[k_sm[tb]])
                    P("dve", lambda e: e.scalar_tensor_tensor(ctok[:, tb, :], PB[bi][:, 0:128], sm[:, tb, 2:3], kvg[:], ALU.mult, ALU.mult),
                      reads=[PK[bi], k_sm[tb], k_const], writes=[k_ctok[tb]])
                    P("act", lambda e: e.activation(wabs[:, tb, :], PB[bi][:, 128:136], ACTF.Abs, scale=(64 ** -0.5) * (8 ** -0.5)),
                      reads=[PK[bi]], writes=[k_w8[tb]])
                    P("act", lambda e: e.activation(wsgn[:, tb, :], PB[bi][:, 128:136], ACTF.Sign),
                      reads=[PK[bi]], writes=[k_w8[tb]])
                    bo = 5 + tb % 2
                    pbv = PB[bo][:].bitcast(BF16)
                    P("pe", lambda e: e.transpose(pbv[:, 0:128], ctok[:, tb, :], ident_b[:]), reads=[k_ctok[tb], k_const], writes=[PK[bo]])
                    copy_on("act", cT[:, tb * 128:(tb + 1) * 128], pbv[:, 0:128], [PK[bo]], [k_cT[tb]])
                if b == 0 and "ctok" in dbg_d:
                    tmpf = sb("dbgtmp", [128, NB, 128], F32, ph)
                    k_t = Tk()
                    P("dve", lambda e: e.tensor_copy(tmpf[:], ctok[:]), reads=k_ctok, writes=[k_t])
                    dbg_out("ctok", tmpf[:], k_t)
                if b == 0 and "wabs" in dbg_d:
                    for tb in range(NB):
                        out_toks.append(fw.dma("sp", dbg_d["wabs"][:, tb, 0:8], wabs[:, tb, :], reads=[k_w8[tb]]))
                        out_toks.append(fw.dma("sp", dbg_d["wabs"][:, tb, 8:16], wsgn[:, tb, :], reads=[k_w8[tb]]))

                score = [[sb(f"score{i}{j}", [128, S], F32, ph) for j in range(2)] for i in range(2)]
                k_score = [[Tk(), Tk()] for _ in range(2)]
                rt = [sb(f"rt{i}", [128, 512], BF16, ph) for i in range(4)]
                k_rt = [Tk() for _ in range(4)]
                dsg = [sb(f"dsg{i}", [128, 8, 128], BF16, ph) for i in range(2)]
                k_dsg = [Tk(), Tk()]
                bsL = sb("bsL", [128, 2], F32, ph)
                bsM = sb("bsM", [128, 2], F32, ph)
                bsC = sb("bsC", [128, 2], F32, ph)
                bsG = sb("bsG", [128, 2], F32, ph)
                bsW = sb("bsW", [128, 2], F32, ph)
                bsH = sb("bsH", [128, NI, 2], F32, ph)
                bsT = sb("bsT", [128, NB], F32, ph)
                thrc = sb("thrc", [128, NB // 2, 2], F32, ph)
                sgnr = sb("sgnr", [128, 2], F32, ph)
                k_bs = Tk()
                k_bL, k_bM, k_bG, k_bW, k_bH = Tk(), Tk(), Tk(), Tk(), Tk()
                k_bC = [Tk(), Tk()]
                k_bsT = [Tk() for _ in range(NB)]
                k_thrc = Tk()
                k_bsj = [Tk(), Tk()]
                bsH2 = sb("bsH2", [128, NI], F32, ph)
                bsS = sb("bsS", [128, 2], F32, ph)
                bsK = sb("bsK", [128, NB // 2], F32, ph)
                negm = [[sb(f"negm{i}{j}", [128, S], BF16, ph) for j in range(2)] for i in range(2)]
                k_negm = [[Tk(), Tk()] for _ in range(2)]
                pt = [sb(f"pt{i}", [128, 512], BF16, ph) for i in range(3)]
                k_pt = [Tk() for _ in range(3)]
                rden = sb("rden", [128, 512], F32, ph)
                k_rden = Tk()
                P("dve", lambda e: e.memset(sgnr[:, 0:1], 1.0), writes=[k_thrc])
                P("dve", lambda e: e.memset(sgnr[:, 1:2], -1.0), writes=[k_thrc])
                for g in range(NB // 2):
                    P("dve", lambda e: e.memset(bsK[:, g:g + 1], 0.5 - float(2 * TOPK - 128 * (2 * g + 2))), writes=[k_thrc])
                    P("dve", lambda e: e.memset(thrc[:, g, 0:1], float(TOPK)), writes=[k_thrc])
                    P("dve", lambda e: e.memset(thrc[:, g, 1:2], float(2 * TOPK - 128 * (2 * g + 2))), writes=[k_thrc])
                rtc = [0]
                ptc = [0]

                def stage1(g):
                    for jj in range(2):
                        qb = 2 * g + jj
                        sc = score[g % 2][jj]
                        k_sc = k_score[g % 2][jj]
                        nk = 128 * (qb + 1)
                        ngrp = (nk + 511) // 512
                        di = qb % 2
                        P("pool", lambda e: e.tensor_tensor(dsg[di][:], ident_b[:].unsqueeze(1).to_broadcast([128, 8, 128]),
                                                            wsgn[:, qb, :].unsqueeze(2).to_broadcast([128, 8, 128]), ALU.mult),
                          reads=[k_const, k_w8[qb]], writes=[k_dsg[di]])
                        for gg in range(ngrp):
                            k0 = gg * 512
                            kn = min(512, nk - k0)
                            ba = 2 + gg % 2
                            ris = []

                            def dots(h):
                                j, half = h // 2, h % 2
                                bi = h % 2
                                p0 = half * 64
                                P("pe", lambda e: e.matmul(PB[bi][:, 0:kn], iqT[:, j, qb * 128:(qb + 1) * 128], ikT[:, half, k0:k0 + kn], start=True, stop=True),
                                  reads=[k_iq[j][qb // 4], k_ik[gg]], writes=[PK[bi]])
                                ri = rtc[0] % 4
                                rtc[0] += 1
                                ris.append(ri)
                                P("act", lambda e: e.activation(rt[ri][:, 0:kn], PB[bi][:, 0:kn], ACTF.Relu, scale=wabs[:, qb, h:h + 1]),
                                  reads=[PK[bi], k_w8[qb]], writes=[k_rt[ri]])

                            def accum(h):
                                ri = ris[h]
                                P("pe", lambda e: e.matmul(PB[ba][:, 0:kn], dsg[di][:, h, :], rt[ri][:, 0:kn], start=(h == 0), stop=(h == 7)),
                                  reads=[k_dsg[di], k_rt[ri]], writes=[PK[ba]])
                            dots(0)
                            for h in range(8):
                                if h + 1 < 8:
                                    dots(h + 1)
                                accum(h)
                                yield
                            d0 = qb * 128
                            last = (k0 + kn == nk)
                            nmain = kn - 128 if last else kn
                            if nmain > 0:
                                P("dve", lambda e: e.tensor_copy(sc[:, k0:k0 + nmain], PB[ba][:, 0:nmain]), reads=[PK[ba]], writes=[k_sc])
                            if last:
                                P("dve", lambda e: e.tensor_tensor(sc[:, d0:d0 + 128], PB[ba][:, kn - 128:kn], adm[:], ALU.mult), reads=[PK[ba], k_const], writes=[k_sc])
                                P("dve", lambda e: e.tensor_tensor(sc[:, d0:d0 + 128], sc[:, d0:d0 + 128], negb[:], ALU.add), reads=[k_sc, k_const], writes=[k_sc])

                def stage2(g):
                    qbs = (2 * g, 2 * g + 1)
                    scs_ = score[g % 2]
                    ks = k_score[g % 2]
                    nm = negm[g % 2]
                    knm = k_negm[g % 2]
                    if g == 0:
                        for jj in range(2):
                            P("dve", lambda e: e.memset(bsT[:, qbs[jj]:qbs[jj] + 1], -1.0e29), writes=[k_bsT[qbs[jj]]])
                    else:
                        nks = [128 * (q + 1) for q in qbs]
                        for jj in range(2):
                            P("dve", lambda e: e.tensor_reduce(bsL[:, jj:jj + 1], scs_[jj][:, 0:qbs[jj] * 128], AX.X, ALU.min), reads=[ks[jj]], writes=[k_bL])
                            P("dve", lambda e: e.tensor_reduce(bsW[:, jj:jj + 1], scs_[jj][:, 0:nks[jj]], AX.X, ALU.max), reads=[ks[jj]], writes=[k_bW])
                        P("dve", lambda e: e.tensor_tensor(bsW[:], bsW[:], bsL[:], ALU.subtract), reads=[k_bW, k_bL], writes=[k_bW])
                        P("dve", lambda e: e.tensor_tensor(bsW[:], bsW[:], sgnr[:], ALU.mult), reads=[k_bW, k_thrc], writes=[k_bW])
                        P("dve", lambda e: e.tensor_tensor(bsL[:], bsL[:], sgnr[:], ALU.mult), reads=[k_bL, k_thrc], writes=[k_bL])
                        P("dve", lambda e: e.tensor_tensor(bsH[:], pow2[:].unsqueeze(2).to_broadcast([128, NI, 2]),
                                                           bsW[:].unsqueeze(1).to_broadcast([128, NI, 2]), ALU.mult), reads=[k_bW, k_const], writes=[k_bH])
                        for it in range(NI):
                            P("dve", lambda e: e.tensor_tensor(bsM[:], bsL[:], bsH[:, it, :], ALU.add), reads=[k_bL, k_bH], writes=[k_bM])
                            P("dve", lambda e: e.tensor_scalar(nm[0][:, 0:nks[0]], scs_[0][:, 0:nks[0]], bsM[:, 0:1], float(TOPK - nks[1]), ALU.is_ge, ALU.add, accum_out=bsC[:, 0:1]),
                              reads=[ks[0], k_bM], writes=[knm[0], k_bC[0]])
                            P("act", lambda e: e.activation(nm[1][:, 0:nks[1]], scs_[1][:, 0:nks[1]], ACTF.Sign, bias=bsM[:, 1:2], scale=1.0, accum_out=bsC[:, 1:2]),
                              reads=[ks[1], k_bM], writes=[knm[1], k_bC[1]])
                            P("dve", lambda e: e.scalar_tensor_tensor(bsG[:], bsC[:], float(2 * TOPK - nks[1]), bsH[:, it, :], ALU.is_ge, ALU.mult), reads=k_bC + [k_bH], writes=[k_bG])
                            P("dve", lambda e: e.tensor_tensor(bsL[:], bsL[:], bsG[:], ALU.add), reads=[k_bL, k_bG], writes=[k_bL])
                            yield
                        P("dve", lambda e: e.tensor_tensor(bsT[:, qbs[0]:qbs[0] + 2], bsL[:], sgnr[:], ALU.mult), reads=[k_bL, k_thrc], writes=[k_bsT[qbs[0]], k_bsT[qbs[1]]])
                    for jj in range(2):
                        qb = qbs[jj]
                        nk = 128 * (qb + 1)
                        P("dve", lambda e: e.tensor_scalar(nm[jj][:, 0:nk], scs_[jj][:, 0:nk], bsT[:, qb:qb + 1], -30000.0, ALU.is_lt, ALU.mult),
                          reads=[ks[jj], k_bsT[qb]], writes=[knm[jj]])
                        if b == 0 and "score" in dbg_d:
                            out_toks.append(fw.dma("sp", dbg_d["score"][qb, :, 0:nk], scs_[jj][:, 0:nk], reads=[ks[jj]]))
                            out_toks.append(fw.dma("sp", dbg_d["thr"][qb, :, :], bsT[:, qb:qb + 1], reads=[k_bsT[qb]]))
                    yield

                def stage3(g):
                    for jj in range(2):
                        qb = 2 * g + jj
                        nm = negm[g % 2][jj]
                        knm = k_negm[g % 2][jj]
                        pis = {}

                        def logits(kb):
                            bl = 4 + kb % 2
                            dd = min(qb - kb, 3)
                            P("pe", lambda e: e.matmul(PB[bl][:], cT[:, kb * 128:(kb + 1) * 128], qlT[:, :, qb * 128:(qb + 1) * 128], start=True, stop=False),
                              reads=[k_cT[kb]] + [k_ql[h][qb // 4] for h in range(4)], writes=[PK[bl]])
                            P("pe", lambda e: e.matmul(PB[bl][:], ident_b[:], bhi[:, dd * 4:dd * 4 + 4, :], start=False, stop=False),
                              reads=[k_const, k_bias], writes=[PK[bl]])
                            P("pe", lambda e: e.matmul(PB[bl][:], ident_b[:], blo[:, dd * 4:dd * 4 + 4, :], start=False, stop=False),
                              reads=[k_const, k_bias], writes=[PK[bl]])
                            P("pe", lambda e: e.matmul(PB[bl][:], nm[:, kb * 128:(kb + 1) * 128], ident4[:].rearrange("p a b -> p (a b)"), start=False, stop=True),
                              reads=[knm, k_const], writes=[PK[bl]])
                            pi = ptc[0] % 3
                            ptc[0] += 1
                            pis[kb] = pi
                            P("act", lambda e: e.activation(pt[pi][:], PB[bl][:], ACTF.Exp, scale=128 ** -0.5), reads=[PK[bl]], writes=[k_pt[pi]])

                        def pv(kb):
                            pi = pis[kb]
                            P("pe", lambda e: e.matmul(PB[6][:], ctok[:, kb, :], pt[pi][:], start=(kb == 0), stop=(kb == qb)),
                              reads=[k_ctok[kb], k_pt[pi]], writes=[PK[6]])
                            P("pe", lambda e: e.matmul(PB[7][:], ones_b[:], pt[pi][:], start=(kb == 0), stop=(kb == qb)),
                              reads=[k_const, k_pt[pi]], writes=[PK[7]])
                        logits(0)
                        for kb in range(qb + 1):
                            if kb + 1 <= qb:
                                logits(kb + 1)
                            pv(kb)
                            yield
                        P("dve", lambda e: e.reciprocal(rden[:], PB[7][:]), reads=[PK[7]], writes=[k_rden])
                        tt = qb // 4
                        P("dve", lambda e: e.tensor_tensor(OT[:, 0:4, qb * 128:(qb + 1) * 128], PB[6][:].rearrange("r (h q) -> r h q", h=4),
                                                           rden[:].rearrange("r (h q) -> r h q", h=4), ALU.mult),
                          reads=[PK[6], k_rden], writes=[k_OT[h][tt] for h in range(4)])

                NG = NB // 2

                def n_units1(g):
                    return sum(((128 * (q + 1) + 511) // 512) * 8 for q in (2 * g, 2 * g + 1))

                def n_units3(g):
                    return sum(q + 1 for q in (2 * g, 2 * g + 1))

                def advance(gen, k):
                    for _ in range(k):
                        try:
                            next(gen)
                        except StopIteration:
                            return False
                    return True

                for step in range(NG + 2):
                    g1 = stage1(step) if step < NG else None
                    g2 = stage2(step - 1) if 1 <= step <= NG else None
                    g3 = stage3(step - 2) if 2 <= step else None
                    n2 = (NI + 1) if (g2 is not None and step - 1 >= 1) else 1
                    r1 = -(-n_units1(step) // n2) if g1 is not None else 0
                    r3 = -(-n_units3(step - 2) // n2) if g3 is not None else 0
                    alive = True
                    while alive:
                        alive = False
                        if g2 is not None and advance(g2, 1):
                            alive = True
                        if g1 is not None and advance(g1, r1):
                            alive = True
                        if g3 is not None and advance(g3, r3):
                            alive = True
                fw.barrier()

            with ExitStack() as ph:
                rmask = sb("rmask", [128, S], BF16, ph)
                k_rm = Tk()
                P("pool", lambda e: e.memset(rmask[:], 1.0), writes=[k_rm])
                P("pool", lambda e: e.memset(rmask[:].rearrange("p (n c) -> p n c", c=64)[:, :, 0:1], 0.0), writes=[k_rm])
                big1 = sb("big1", [128, 2 * S], F32, ph)
                big2 = sb("big2", [128, 2 * S], F32, ph)
                fb = big1[:, 0:S]
                kkb = big1[:, S:2 * S]
                ex = big2[:, 0:S]
                dS = big1[:].rearrange("k (v n) -> k v n", n=32)
                a0 = big2[:].rearrange("k (v n) -> k v n", n=32)
                k_fb = [Tk() for _ in range(4)]
                k_kkb = [Tk() for _ in range(4)]
                k_ex = [Tk() for _ in range(4)]
                k_b2b = Tk()
                k_big1 = k_fb + k_kkb
                k_big2 = k_ex + [k_b2b]
                qsf = sb("qsf", [128, S], F32, ph)
                k_qsf = [Tk() for _ in range(4)]
                th2_ = sb("th2", [128, 512], F32, ph)
                th2 = [th2_, th2_]
                k_th2_ = Tk()
                k_th2 = [k_th2_, k_th2_]
                the = sb("the", [128, 512], F32, ph)
                qse = sb("qse", [128, 512], F32, ph)
                ob = sb("ob", [128, 512], F32, ph)
                sq = sb("sq", [128, 512], F32, ph)
                rs = sq
                k_the, k_qse, k_ob, k_sq = Tk(), Tk(), Tk(), Tk()
                k_rs = k_sq
                NHB = 2
                qtl = [sb(f"qtl{i}", [128, S], BF16, ph) for i in range(NHB)]
                qhl = [sb(f"qhl{i}", [128, S], BF16, ph) for i in range(NHB)]
                ktl = [sb(f"ktl{i}", [128, S], BF16, ph) for i in range(NHB)]
                vtok = [sb(f"vtok{i}", [128, NB, 128], BF16, ph) for i in range(NHB)]
                Vb = [sb(f"Vb{i}", [128, 128, 32], BF16, ph) for i in range(NHB)]
                wgt = [sb(f"wgt{i}", [128, KC, 128], BF16, ph) for i in range(NHB)]
                ktA = sb("ktA", [128, NB, 128], BF16, ph)
                ktB = sb("ktB", [128, NB, 128], BF16, ph)
                cols = sb("hcols", [128, 4, 32], F32, ph)
                scs = [sb(f"scs{i}", [128, 128], BF16, ph) for i in range(2)]
                k_qtl = [[Tk() for _ in range(4)] for _ in range(NHB)]
                k_qhl = [Tk() for _ in range(NHB)]
                k_ktl = [Tk() for _ in range(NHB)]
                k_vtok = [[Tk() for _ in range(NB)] for _ in range(NHB)]
                k_Vb = [Tk() for _ in range(NHB)]
                k_wgt = [Tk() for _ in range(NHB)]
                k_kt = [Tk() for _ in range(NB)]
                k_cols = Tk()
                k_scs = [Tk(), Tk()]
                P("pool", lambda e: e.memset(ktA[64:128, :, :], 0.0), writes=k_kt)
                P("pool", lambda e: e.memset(ktB[0:64, :, :], 0.0), writes=k_kt)
                P("pool", lambda e: e.memset(cols[:, 3, 0:1], 0.0), writes=[k_cols])

                tht, sgt = the, qse
                k_tht, k_sgt = k_the, k_qse
                for h in range(4):
                    wt, k_w, _ = load_w([(C_AG + h * 128, 128)])
                    for tt in range(4):
                        bi = inproj_fm(wt, k_w, tt, (0, 1))
                        P("act", lambda e: e.activation(tht[:], PB[bi][:], ACTF.Tanh, scale=0.5), reads=[PK[bi]], writes=[k_tht])
                        P("dve", lambda e: e.scalar_tensor_tensor(sgt[:], tht[:], 1.0, PB[bi][:], ALU.add, ALU.mult),
                          reads=[k_tht, PK[bi]], writes=[k_sgt])
                        bo = 6 + (h * 4 + tt) % 2
                        P("pe", lambda e: e.matmul(PB[bo][:], wuv[:, h, :], OT[:, h, tt * 512:(tt + 1) * 512], start=True, stop=True),
                          reads=[k_const, k_OT[h][tt]], writes=[PK[bo]])
                        P("dve", lambda e: e.scalar_tensor_tensor(OT[:, h, tt * 512:(tt + 1) * 512], PB[bo][:], 0.5, sgt[:], ALU.mult, ALU.mult),
                          reads=[PK[bo], k_sgt], writes=[k_OT[h][tt]])
                def prep(h):
                    hh = h % NHB
                    wt, k_w, _ = load_w([(C_BF + h * 128, 128)])
                    for tt in range(4):
                        bi = inproj_fm(wt, k_w, tt, (0, 1))
                        sl = slice(tt * 512, (tt + 1) * 512)
                        P("act", lambda e: e.activation(fb[:, sl], PB[bi][:], ACTF.Tanh, scale=0.5), reads=[PK[bi]], writes=[k_fb[tt]])
                        P("act", lambda e: e.activation(kkb[:, sl], fb[:, sl], ACTF.Identity, bias=lbB[:, h:h + 1], scale=lbNB[:, h:h + 1]),
                          reads=[k_fb[tt], k_const], writes=[k_kkb[tt]])
                        P("dve", lambda e: e.tensor_scalar(fb[:, sl], fb[:, sl], lbB[:, h:h + 1], lbA[:, h:h + 1], ALU.mult, ALU.add),
                          reads=[k_fb[tt], k_const], writes=[k_fb[tt]])
                        yield
                    P("act", lambda e: e.activation(fb, fb, ACTF.Ln), reads=k_fb, writes=k_fb)
                    P("dve", lambda e: e.tensor_tensor_scan(fb, rmask[:], fb, 0.0, ALU.mult, ALU.add), reads=k_fb + [k_rm], writes=k_fb)
                    yield
                    wt, k_w, _ = load_w([(C_BQ + h * 128, 128)])
                    for tt in range(4):
                        i = tt % 2
                        bi = inproj_fm(wt, k_w, tt, (0, 1))
                        sl = slice(tt * 512, (tt + 1) * 512)
                        P("act", lambda e: e.activation(th2[i][:], PB[bi][:], ACTF.Tanh, scale=0.5), reads=[PK[bi]], writes=[k_th2[i]])
                        P("dve", lambda e: e.scalar_tensor_tensor(qsf[:, sl], th2[i][:], 1.0, PB[bi][:], ALU.add, ALU.mult),
                          reads=[k_th2[i], PK[bi]], writes=[k_qsf[tt]])
                        yield
                    fb3 = fb.rearrange("p (n c) -> p n c", c=64)
                    cl = cols
                    kc_ = k_cols
                    P("dve", lambda e: e.tensor_copy(cl[:, 0, :], fb3[:, :, 31]), reads=k_fb, writes=[kc_])
                    P("dve", lambda e: e.tensor_copy(cl[:, 1, :], fb3[:, :, 63]), reads=k_fb, writes=[kc_])
                    P("dve", lambda e: e.tensor_tensor(cl[:, 2, 1:32], cl[:, 1, 0:31], cl[:, 0, 0:31], ALU.subtract), reads=[kc_], writes=[kc_])
                    P("dve", lambda e: e.tensor_tensor(cl[:, 2, 1:32], cl[:, 2, 1:32], cl[:, 0, 1:32], ALU.add), reads=[kc_], writes=[kc_])
                    P("act", lambda e: e.activation(cl[:, 3, 1:32], cl[:, 2, 1:32], ACTF.Exp), reads=[kc_], writes=[kc_])
                    P("dve", lambda e: e.tensor_tensor(fb3, fb3, cl[:, 0, :].unsqueeze(2).to_broadcast([128, 32, 64]), ALU.subtract),
                      reads=k_fb + [kc_], writes=k_fb)
                    yield
                    P("act", lambda e: e.activation(ex, fb, ACTF.Exp, scale=-1.0), reads=k_fb, writes=k_ex)
                    P("dve", lambda e: e.tensor_tensor(ktl[hh][:], kkb, ex, ALU.mult), reads=k_kkb + k_ex, writes=[k_ktl[hh]])
                    P("act", lambda e: e.activation(ex, fb, ACTF.Exp), reads=k_fb, writes=k_ex)
                    yield
                    wt, k_w, _ = load_w([(C_BI + h * 128, 128)])
                    for tb in range(NB):
                        bi = tb % 2
                        for k in range(KC):
                            P("pe", lambda e: e.matmul(PB[bi][:, 0:128], xT[:, k, tb * 128:(tb + 1) * 128], wt[:, k, 0:128], start=(k == 0), stop=(k == KC - 1)),
                              reads=[k_w, k_xT[tb][0], k_xT[tb][1]], writes=[PK[bi]])
                        copy_on(ev_eng(), vtok[hh][:, tb, :], PB[bi][:, 0:128], [PK[bi]], [k_vtok[hh][tb]])
                        if tb % 4 == 3:
                            yield
                    for tb in range(NB):
                        bo = 2 + tb % 2
                        pbv = PB[bo][:].bitcast(BF16)
                        P("pe", lambda e: e.transpose(pbv[:, 0:128], ktl[hh][:, tb * 128:(tb + 1) * 128], ident_b[:]), reads=[k_ktl[hh], k_const], writes=[PK[bo]])
                        copy_on("act", ktA[0:64, tb, :], pbv[0:64, 0:128], [PK[bo]], [k_kt[tb]])
                        copy_on("act", ktB[64:128, tb, :], pbv[64:128, 0:128], [PK[bo]], [k_kt[tb]])
                        if tb % 4 == 3:
                            yield
                    for tt in range(4):
                        sl = slice(tt * 512, (tt + 1) * 512)
                        P("dve", lambda e: e.scalar_tensor_tensor(qtl[hh][:, sl], qsf[:, sl], 0.5, ex[:, sl], ALU.mult, ALU.mult),
                          reads=[k_qsf[tt]] + k_ex, writes=[k_qtl[hh][tt]])
                    qt3 = qtl[hh][:].rearrange("p (n c) -> p n c", c=64)
                    qh3 = qhl[hh][:].rearrange("p (n c) -> p n c", c=64)
                    P("pool", lambda e: e.tensor_tensor(qh3[:, 1:32, :], qt3[:, 1:32, :], cl[:, 3, 1:32].unsqueeze(2).to_broadcast([128, 31, 64]), ALU.mult),
                      reads=k_qtl[hh] + [kc_], writes=[k_qhl[hh]])
                    yield
                    P("dve", lambda e: e.tensor_copy(a0, cl[:, 3, :].unsqueeze(1).to_broadcast([128, 128, 32])), reads=[kc_] + k_big2, writes=k_big2)
                    for grp in range(8):
                        bd = grp % 4
                        for c4 in range(4):
                            n = grp * 4 + c4
                            tb, half = n // 2, n % 2
                            kt_ = ktA if half == 0 else ktB
                            P("pe", lambda e: e.matmul(PB[bd][:, c4 * 128:(c4 + 1) * 128], kt_[:, tb, :], vtok[hh][:, tb, :], start=True, stop=True),
                              reads=[k_kt[tb], k_vtok[hh][tb]], writes=[PK[bd]])
                        copy_on("act", dS[:, :, grp * 4:grp * 4 + 4], PB[bd][:].rearrange("k (n v) -> k v n", n=4), [PK[bd]] + k_big1, k_big1)
                        if grp % 2 == 1:
                            yield
                    P("dve", lambda e: e.tensor_tensor_scan(big1[:], big2[:], big1[:], 0.0, ALU.mult, ALU.add), reads=k_big1 + k_big2, writes=k_big1)
                    P("act", lambda e: e.activation(Vb[hh][:], dS, ACTF.Copy), reads=k_big1, writes=[k_Vb[hh]])
                    wt, k_w, _ = load_w([(C_BG + h * 128, 128)])
                    P("dve", lambda e: e.tensor_copy(wgt[hh][:], wt[:, :, 0:128]), reads=[k_w], writes=[k_wgt[hh]])
                    yield

                def outp(h):
                    hh = h % NHB
                    for tt in range(4):
                        bacc = 5
                        for tbl in range(4):
                            tb = tt * 4 + tbl
                            tsl = slice(tb * 128, (tb + 1) * 128)
                            si = tb % 2
                            bs_ = 4
                            P("pe", lambda e: e.matmul(PB[bs_][:, 0:128], ktl[hh][:, tsl], qtl[hh][:, tsl], start=True, stop=True),
                              reads=[k_ktl[hh], k_qtl[hh][tt]], writes=[PK[bs_]])
                            P("dve", lambda e: e.tensor_tensor(scs[si][:], PB[bs_][:, 0:128], caus[:], ALU.mult), reads=[PK[bs_], k_const], writes=[k_scs[si]])
                            for half in range(2):
                                n = tb * 2 + half
                                csl = slice(n * 64, (n + 1) * 64)
                                oc = slice((tbl * 2 + half) * 64, (tbl * 2 + half + 1) * 64)
                                P("pe", lambda e: e.matmul(PB[bacc][:, oc], vtok[hh][:, tb, :], scs[si][:, half * 64:(half + 1) * 64], start=True, stop=(n == 0)),
                                  reads=[k_vtok[hh][tb], k_scs[si]], writes=[PK[bacc]])
                                if n > 0:
                                    P("pe", lambda e: e.matmul(PB[bacc][:, oc], Vb[hh][:, :, n - 1], qhl[hh][:, csl], start=False, stop=True),
                                      reads=[k_Vb[hh], k_qhl[hh]], writes=[PK[bacc]])
                            yield
                        sl = slice(tt * 512, (tt + 1) * 512)
                        P("act", lambda e: e.activation(ob[:], PB[bacc][:], ACTF.Copy), reads=[PK[bacc]], writes=[k_ob])
                        P("act", lambda e: e.activation(sq[:], PB[bacc][:], ACTF.Square), reads=[PK[bacc]], writes=[k_sq])
                        P("pe", lambda e: e.matmul(PB[6][:], ones_f[:], sq[:], start=True, stop=True), reads=[k_const, k_sq], writes=[PK[6]])
                        P("act", lambda e: e.activation(rs[:], PB[6][:], ACTF.Ln, bias=epsc[:, 0:1], scale=1.0 / 128), reads=[PK[6], k_const], writes=[k_rs])
                        P("act", lambda e: e.activation(rs[:], rs[:], ACTF.Exp, scale=-0.5), reads=[k_rs], writes=[k_rs])
                        yield
                        P("dve", lambda e: e.scalar_tensor_tensor(ob[:], ob[:], hgt[:, h:h + 1], rs[:], ALU.mult, ALU.mult),
                          reads=[k_ob, k_const, k_rs], writes=[k_ob])
                        bi = inproj_fm(wgt[hh], k_wgt[hh], tt, (7,))
                        P("act", lambda e: e.activation(the[:], PB[bi][:], ACTF.Tanh, scale=0.5), reads=[PK[bi]], writes=[k_the])
                        P("dve", lambda e: e.scalar_tensor_tensor(qse[:], the[:], 1.0, PB[bi][:], ALU.add, ALU.mult),
                          reads=[k_the, PK[bi]], writes=[k_qse])
                        P("pool", lambda e: e.tensor_tensor(OT[:, 4 + h, sl], ob[:], qse[:], ALU.mult),
                          reads=[k_ob, k_qse], writes=[k_OT[4 + h][tt]])
                        yield

                def run_pair(ga, gb):
                    alive = True
                    while alive:
                        alive = False
                        for g_ in (ga, gb):
                            if g_ is None:
                                continue
                            try:
                                next(g_)
                                alive = True
                            except StopIteration:
                                pass

                for h in range(5):
                    run_pair(prep(h) if h < 4 else None, outp(h - 1) if h >= 1 else None)
                fw.barrier()

            with ExitStack() as ph:
                if b == 0 and "OT" in dbg_d:
                    tmpf3 = sb("dbgtmp3", [128, 8, S], F32, ph)
                    k_t3 = Tk()
                    P("dve", lambda e: e.tensor_copy(tmpf3[:], OT[:]), reads=[k_OT[c][t] for c in range(8) for t in range(4)], writes=[k_t3])
                    dbg_out("OT", tmpf3[:], k_t3)
                wo = sb("wo", [128, KC, D], BF16, ph)
                k_wo = [Tk() for _ in range(4)]
                stg2 = [sb(f"stgw{i}", [128, KC, 256], F32, ph) for i in range(2)]
                k_s2 = [Tk(), Tk()]
                for j in range(4):
                    fw.dma("sp" if j % 2 == 0 else "pool", stg2[j % 2][:],
                           wo_d[:, j * 256:(j + 1) * 256].rearrange("(k p) c -> p k c", p=128), writes=[k_s2[j % 2]])
                    P("dve" if j % 2 == 0 else "act", (lambda e: e.tensor_copy(wo[:, :, j * 256:(j + 1) * 256], stg2[j % 2][:])) if j % 2 == 0 else
                      (lambda e: e.activation(wo[:, :, j * 256:(j + 1) * 256], stg2[j % 2][:], ACTF.Copy)),
                      reads=[k_s2[j % 2]], writes=[k_wo[j]])
                NR = 3
                xr = [sb(f"xr{i}", [128, D], F32, ph) for i in range(NR)]
                lng = sb("lng", [128, D], F32, ph)
                lnb = sb("lnb", [128, D], F32, ph)
                k_ln = Tk()
                fw.dma("sp", lng[:], lng_d.partition_broadcast(128), writes=[k_ln])
                fw.dma("sp", lnb[:], lnb_d.partition_broadcast(128), writes=[k_ln])
                zt = [sb(f"zt{i}", [128, D], F32, ph) for i in range(NR)]
                zn = [sb(f"zn{i}", [128, D], F32, ph) for i in range(NR)]
                st6 = [sb(f"st6{i}", [128, 2, 6], F32, ph) for i in range(NR)]
                mv = [sb(f"mv{i}", [128, 4], F32, ph) for i in range(NR)]
                k_xr = [Tk() for _ in range(NR)]
                k_zt = [Tk() for _ in range(NR)]
                k_zn = [Tk() for _ in range(NR)]
                k_st = [Tk() for _ in range(NR)]
                for tb0 in range(2):
                    fw.dma("sp", xr[tb0 % NR][:], x_d[b, tb0 * 128:(tb0 + 1) * 128, :], writes=[k_xr[tb0 % NR]])
                for tb in range(NB):
                    i = tb % NR
                    tt = tb // 4
                    if tb + 2 < NB:
                        fw.dma("sp", xr[(tb + 2) % NR][:], x_d[b, (tb + 2) * 128:(tb + 3) * 128, :], writes=[k_xr[(tb + 2) % NR]])
                    for hh in range(2):
                        bi = (tb % 2) * 2 + hh
                        for kc in range(8):
                            P("pe", lambda e: e.matmul(PB[bi][:], OT[:, kc, tb * 128:(tb + 1) * 128], wo[:, kc, hh * 512:(hh + 1) * 512], start=(kc == 0), stop=(kc == 7)),
                              reads=[k_OT[kc][tt], k_wo[2 * hh], k_wo[2 * hh + 1]], writes=[PK[bi]])
                        P("dve", lambda e: e.scalar_tensor_tensor(zt[i][:, hh * 512:(hh + 1) * 512], xr[i][:, hh * 512:(hh + 1) * 512], ALPHA, PB[bi][:], ALU.mult, ALU.add),
                          reads=[k_xr[i], PK[bi]], writes=[k_zt[i]])
                        P("dve", lambda e: e.bn_stats(st6[i][:, hh, :], zt[i][:, hh * 512:(hh + 1) * 512]), reads=[k_zt[i]], writes=[k_st[i]])
                    P("dve", lambda e: e.bn_aggr(mv[i][:, 0:2], st6[i][:]), reads=[k_st[i]], writes=[k_st[i]])
                    P("act", lambda e: e.activation(mv[i][:, 2:3], mv[i][:, 1:2], ACTF.Sqrt, bias=epsc[:, 1:2], scale=1.0), reads=[k_st[i], k_const], writes=[k_st[i]])
                    P("dve", lambda e: e.reciprocal(mv[i][:, 2:3], mv[i][:, 2:3]), reads=[k_st[i]], writes=[k_st[i]])
                    P("dve", lambda e: e.scalar_tensor_tensor(mv[i][:, 3:4], mv[i][:, 0:1], -1.0, mv[i][:, 2:3], ALU.mult, ALU.mult), reads=[k_st[i]], writes=[k_st[i]])
                    P("act", lambda e: e.activation(zn[i][:], zt[i][:], ACTF.Identity, bias=mv[i][:, 3:4], scale=mv[i][:, 2:3]), reads=[k_zt[i], k_st[i]], writes=[k_zn[i]])
                    P("dve", lambda e: e.tensor_tensor(zn[i][:], zn[i][:], lng[:], ALU.mult), reads=[k_zn[i], k_ln], writes=[k_zn[i]])
                    P("pool", lambda e: e.tensor_tensor(zn[i][:], zn[i][:], lnb[:], ALU.add), reads=[k_zn[i], k_ln], writes=[k_zn[i]])
                    out_toks.append(fw.dma("pool", out_d[b, tb * 128:(tb + 1) * 128, :], zn[i][:], reads=[k_zn[i]]))
                if b + 1 < n_seq:
                    phase_x(b + 1, ph)
                fw.barrier()
        fw.flush()
        _build.last_counts = (fw.nops, dict(fw.sigcnt))
    return nc


def _t5_bucket_np(rel):
    nb = 16
    max_exact = 8
    ret = np.where(rel > 0, nb, 0).astype(np.int32)
    n = np.abs(rel)
    nf = np.maximum(n, 1).astype(np.float32)
    large = max_exact + (np.log(nf / max_exact) / math.log(256 / max_exact) * (nb - max_exact)).astype(np.int32)
    large = np.minimum(large, nb - 1)
    return ret + np.where(n < max_exact, n, large)


def host_layout(inputs):
    x = np.asarray(inputs["x"], np.float32)
    rel_bias = np.asarray(inputs["rel_bias"], np.float32)
    s_idx = np.arange(128)[:, None]
    q_idx = np.arange(128)[None, :]
    tiles = []
    for d in range(4):
        rel = (s_idx - q_idx) - 128 * d
        bk = _t5_bucket_np(rel.astype(np.int32))
        tiles.append(np.transpose(rel_bias[bk], (2, 0, 1)))
    bias_t = np.ascontiguousarray(np.stack(tiles, 0))
    lb = np.asarray(inputs["lb_logits"], np.float32)
    lb_t = np.ascontiguousarray(lb.reshape(2, 4, 128).transpose(2, 0, 1))
    hg_t = np.ascontiguousarray(np.asarray(inputs["hgrn_norm_g"], np.float32).reshape(4, 128).T)
    common = {
        "w_in": np.ascontiguousarray(np.asarray(inputs["w_in"], np.float32)[0]),
        "w_uk": np.ascontiguousarray(np.asarray(inputs["w_uk"], np.float32)[0]),
        "w_uv": np.ascontiguousarray(np.asarray(inputs["w_uv"], np.float32)[0]),
        "kv_g": np.ascontiguousarray(np.asarray(inputs["kv_norm_g"], np.float32).reshape(1, 128)),
        "bias_t": bias_t,
        "lb_t": lb_t,
        "hg_t": hg_t,
        "w_o": np.ascontiguousarray(np.asarray(inputs["w_o"], np.float32)[0]),
        "ln_g": np.ascontiguousarray(np.asarray(inputs["ln_g"], np.float32).reshape(1, D)),
        "ln_b": np.ascontiguousarray(np.asarray(inputs["ln_b"], np.float32).reshape(1, D)),
    }
    return x, common


def kernel(**inputs):
    x, common = host_layout(inputs)
    nc = build_program(SEQ_PER_CORE)
    in_maps = []
    for c in range(NCORES):
        m = dict(common)
        m["x"] = np.ascontiguousarray(x[c * SEQ_PER_CORE:(c + 1) * SEQ_PER_CORE])
        in_maps.append(m)
    res = run_bass_kernel_spmd(nc, in_maps, core_ids=list(range(NCORES)))
    out = np.concatenate([np.asarray(r["out"], np.float32) for r in res.results], axis=0)
    return out
```
